# Optimizing a Trainium2 kernel written in Bass

```python
import jax, jax.numpy as jnp
from jax import lax
import numpy as np

D_MODEL = 2048
BATCH = 4
SEQ = 4096
DEPTH = 1

HEAD_DIM = 64
N_HEADS_SWA = 16
N_KV_SWA = 2
N_HEADS_FOX = 16
WINDOW = 128
BLOCK = 128
D_FF = 5632
CONV_WIDTH = 3
EPS = 1e-6
D_SWA = N_HEADS_SWA * HEAD_DIM
D_KV_SWA = N_KV_SWA * HEAD_DIM
D_FOX = N_HEADS_FOX * HEAD_DIM
D_MIX = D_SWA + D_FOX
D_IN = D_SWA + 2 * D_KV_SWA + 3 * D_FOX + N_HEADS_FOX

kernel_name = "hymba_swa_sink_fox_alibi_convffn"


def rmsnorm(x, g):
    xf = x.astype(jnp.float32)
    y = xf * lax.rsqrt(jnp.mean(xf * xf, axis=-1, keepdims=True) + EPS)
    return (y * g.astype(jnp.float32)).astype(x.dtype)


def alibi_slopes(n):
    return jnp.asarray(2.0 ** (-8.0 * np.arange(1, n + 1) / n), dtype=jnp.float32)


def swa_sink_attention(q, k, v, sinks):
    B, S, H, D = q.shape
    KV = k.shape[2]
    G = H // KV
    nB = S // BLOCK
    scale = 1.0 / np.sqrt(D)
    qb = q.reshape(B, nB, BLOCK, KV, G, D)

    def band(t):
        tp = jnp.pad(t, ((0, 0), (BLOCK, 0), (0, 0), (0, 0)))[:, :S]
        return jnp.concatenate([tp.reshape(B, nB, BLOCK, KV, D), t.reshape(B, nB, BLOCK, KV, D)], axis=2)

    kb, vb = band(k), band(v)
    s = jnp.einsum('bnqkgd,bnskd->bnkgqs', qb, kb).astype(jnp.float32) * scale
    q_loc = BLOCK + jnp.arange(BLOCK)
    k_loc = jnp.arange(2 * BLOCK)
    dist = (q_loc[:, None] - k_loc[None, :]).astype(jnp.float32)
    blk_start = jnp.arange(nB) * BLOCK - BLOCK
    k_abs = blk_start[:, None] + k_loc[None, :]
    mask = (dist >= 0)[None] & (dist < WINDOW)[None] & (k_abs >= 0)[:, None, :]
    slopes = alibi_slopes(H).reshape(KV, G)
    s = s - slopes[:, :, None, None] * dist
    s = jnp.where(mask[None, :, None, None], s, -jnp.inf)
    sink = sinks.astype(jnp.float32).reshape(KV, G)[None, None, :, :, None, None]
    m = jnp.maximum(jnp.max(s, axis=-1, keepdims=True), sink)
    p = jnp.exp(s - m)
    p = p / (jnp.sum(p, axis=-1, keepdims=True) + jnp.exp(sink - m))
    o = jnp.einsum('bnkgqs,bnskd->bnqkgd', p.astype(v.dtype), vb)
    return o.reshape(B, S, H, D)


def forgetting_attention(q, k, v, log_f):
    B, S, H, D = q.shape
    nB = S // BLOCK
    scale = 1.0 / np.sqrt(D)
    c = lax.cumsum(log_f, axis=1)
    cT = jnp.transpose(c, (0, 2, 1))
    qb = jnp.moveaxis(q.reshape(B, nB, BLOCK, H, D), 1, 0)
    cb = jnp.moveaxis(cT.reshape(B, H, nB, BLOCK), 2, 0)
    k_pos = jnp.arange(S)

    def one_block(args):
        q_blk, c_blk, n = args
        s = jnp.einsum('bqhd,bshd->bhqs', q_blk, k).astype(jnp.float32) * scale
        s = s + c_blk[..., None] - cT[:, :, None, :]
        q_pos = n * BLOCK + jnp.arange(BLOCK)
        s = jnp.where((q_pos[:, None] >= k_pos[None, :])[None, None], s, -jnp.inf)
        p = jax.nn.softmax(s, axis=-1)
        return jnp.einsum('bhqs,bshd->bqhd', p.astype(v.dtype), v)

    o = lax.map(one_block, (qb, cb, jnp.arange(nB)))
    return jnp.moveaxis(o, 0, 1).reshape(B, S, H, D)


def causal_dwconv(u, w, b):
    S = u.shape[1]
    up = jnp.pad(u, ((0, 0), (CONV_WIDTH - 1, 0), (0, 0)))
    y = b
    for kk in range(CONV_WIDTH):
        y = y + up[:, kk:kk + S] * w[kk]
    return y


def setup_inputs(seed: int = 0) -> dict:
    key = jax.random.key(seed)
    ks = jax.random.split(key, 17)
    f32 = jnp.float32
    nrm = lambda k, shape, s: jax.random.normal(k, shape, f32) * s
    gain = lambda k, n: 1.0 + 0.05 * jax.random.normal(k, (DEPTH, n), f32)
    return {
        "x": jax.random.normal(ks[0], (BATCH, SEQ, D_MODEL), f32),
        "pre_mix_g": gain(ks[1], D_MODEL),
        "w_in": nrm(ks[2], (DEPTH, D_MODEL, D_IN), D_MODEL ** -0.5),
        "b_forget": 2.0 + 0.5 * jax.random.normal(ks[3], (DEPTH, N_HEADS_FOX), f32),
        "sinks": nrm(ks[4], (DEPTH, N_HEADS_SWA), 1.0),
        "grp_swa_g": gain(ks[5], D_SWA),
        "grp_fox_g": gain(ks[6], D_FOX),
        "w_out": nrm(ks[7], (DEPTH, D_MIX, D_MODEL), D_MIX ** -0.5),
        "post_mix_g": gain(ks[8], D_MODEL),
        "pre_ffn_g": gain(ks[9], D_MODEL),
        "w_up": nrm(ks[10], (DEPTH, D_MODEL, 2 * D_FF), D_MODEL ** -0.5),
        "conv_w": nrm(ks[11], (DEPTH, CONV_WIDTH, 2 * D_FF), CONV_WIDTH ** -0.5),
        "conv_b": nrm(ks[12], (DEPTH, 2 * D_FF), 0.02),
        "w_down": nrm(ks[13], (DEPTH, D_FF, D_MODEL), D_FF ** -0.5),
        "post_ffn_g": gain(ks[14], D_MODEL),
    }


def reference(x, pre_mix_g, w_in, b_forget, sinks, grp_swa_g, grp_fox_g, w_out, post_mix_g,
              pre_ffn_g, w_up, conv_w, conv_b, w_down, post_ffn_g):
    B, S, _ = x.shape
    cuts = np.cumsum([D_SWA, D_KV_SWA, D_KV_SWA, D_FOX, D_FOX, D_FOX])
    for l in range(DEPTH):
        h = rmsnorm(x, pre_mix_g[l])
        proj = h @ w_in[l]
        q_a, k_a, v_a, q_b, k_b, v_b, f_b = jnp.split(proj, cuts, axis=-1)
        o_a = swa_sink_attention(q_a.reshape(B, S, N_HEADS_SWA, HEAD_DIM),
                                 k_a.reshape(B, S, N_KV_SWA, HEAD_DIM),
                                 v_a.reshape(B, S, N_KV_SWA, HEAD_DIM), sinks[l])
        log_f = jax.nn.log_sigmoid(f_b.astype(jnp.float32) + b_forget[l].astype(jnp.float32))
        o_b = forgetting_attention(q_b.reshape(B, S, N_HEADS_FOX, HEAD_DIM),
                                   k_b.reshape(B, S, N_HEADS_FOX, HEAD_DIM),
                                   v_b.reshape(B, S, N_HEADS_FOX, HEAD_DIM), log_f)
        o_a = rmsnorm(o_a.reshape(B, S, D_SWA), grp_swa_g[l])
        o_b = rmsnorm(o_b.reshape(B, S, D_FOX), grp_fox_g[l])
        mix = jnp.concatenate([o_a, o_b], axis=-1) @ w_out[l]
        x = x + rmsnorm(mix, post_mix_g[l])
        h = rmsnorm(x, pre_ffn_g[l])
        u = causal_dwconv(h @ w_up[l], conv_w[l], conv_b[l])
        gate, val = jnp.split(u, 2, axis=-1)
        y = (jax.nn.gelu(gate, approximate=True) * val) @ w_down[l]
        x = x + rmsnorm(y, post_ffn_g[l])
    return x
```

```python
import numpy as np
import ml_dtypes
from contextlib import ExitStack, contextmanager
import concourse.bass as bass
import concourse.mybir as mybir
from concourse.bass_utils import run_bass_kernel_spmd

F32 = mybir.dt.float32
BF16 = mybir.dt.bfloat16
AF = mybir.ActivationFunctionType
ALU = mybir.AluOpType

D = 2048
DIN = 4368
DFF = 5632
NH = 16
HD = 64
EPS = 1e-6
NEG = -30000.0
NFC = DFF // 128

ENGS = ("pe", "act", "dve", "pool", "sp")
CENGS = ("pe", "act", "dve", "pool")


class Res:
    __slots__ = ("last_w", "readers")

    def __init__(self):
        self.last_w = None
        self.readers = {}


class DSem:
    __slots__ = ("sem", "count", "last", "sw", "nobar")

    def __init__(self, sem, sw):
        self.sem = sem
        self.count = 0
        self.last = None
        self.sw = sw
        self.nobar = False


class Op:
    __slots__ = ("eng", "fn", "deps", "is_dma", "dsem", "dval", "sigval", "need_sig")


class Ring:
    def __init__(self, items):
        self.items = items
        self.i = 0

    def next(self):
        it = self.items[self.i % len(self.items)]
        self.i += 1
        return it


class Prog:
    def __init__(self, nc, root, same_eng_sync=True):
        self.nc = nc
        self.root = root
        self.stack = root
        self.ops = {e: [] for e in ENGS}
        self.same_eng_sync = same_eng_sync
        self.csem = {e: root.enter_context(nc.semaphore("cs_" + e)) for e in CENGS}
        self.all_dsems = []
        self.free_dsems = {False: [], True: []}
        self.scope_dsems = [[]]
        self.pending_bar = {e: None for e in ENGS}
        self.uid = 0

    def sb(self, shape, dt, name=None):
        self.uid += 1
        return self.stack.enter_context(self.nc.sbuf_tensor(name or ("t%d" % self.uid), list(shape), dt))

    def ps(self, shape, dt, name=None):
        self.uid += 1
        return self.stack.enter_context(self.nc.psum_tensor(name or ("p%d" % self.uid), list(shape), dt))

    def dsem(self, sw=False):
        if self.free_dsems[sw]:
            d = self.free_dsems[sw].pop()
        else:
            d = DSem(self.root.enter_context(self.nc.semaphore("ds%d" % len(self.all_dsems))), sw)
            self.all_dsems.append(d)
        self.scope_dsems[-1].append(d)
        d.nobar = False
        return d

    @contextmanager
    def scope(self):
        old = self.stack
        self.scope_dsems.append([])
        with ExitStack() as st:
            self.stack = st
            yield
            self.barrier()
        self.stack = old
        for d in self.scope_dsems.pop():
            self.free_dsems[d.sw].append(d)

    def barrier(self, final=False):
        bar = []
        for e in CENGS:
            for o in reversed(self.ops[e]):
                if not o.is_dma:
                    bar.append(o)
                    break
        for d in self.all_dsems:
            if d.last is not None and (final or not d.nobar):
                bar.append(d.last)
        for e in ENGS:
            self.pending_bar[e] = bar

    def _deps(self, op, r, w):
        deps = []
        pb = self.pending_bar[op.eng]
        if pb is not None:
            deps.extend(pb)
            self.pending_bar[op.eng] = None
        for res in r:
            if res.last_w is not None:
                deps.append(res.last_w)
        for res in w:
            if res.last_w is not None:
                deps.append(res.last_w)
            deps.extend(res.readers.values())
        key = id(op.dsem) if op.is_dma else op.eng
        for res in r:
            res.readers[key] = op
        for res in w:
            res.last_w = op
            res.readers = {}
        op.deps = deps

    def op(self, eng, fn, r=(), w=()):
        o = Op()
        o.eng = eng
        o.fn = fn
        o.is_dma = False
        o.dsem = None
        o.dval = 0
        o.sigval = 0
        o.need_sig = False
        self._deps(o, r, w)
        self.ops[eng].append(o)
        return o

    def dma(self, q, out, in_, dsem, r=(), w=()):
        assert dsem.sw == (q == "pool"), "semaphore / DMA queue kind mismatch"
        o = Op()
        o.eng = q
        o.fn = (out, in_)
        o.is_dma = True
        o.dsem = dsem
        dsem.count += 1
        o.dval = dsem.count * 16
        dsem.last = o
        o.sigval = 0
        o.need_sig = False
        self._deps(o, r, w)
        self.ops[q].append(o)
        return o

    def _skip(self, d, o):
        return (not d.is_dma) and (not o.is_dma) and d.eng == o.eng and \
            (d.eng == "pe" or not self.same_eng_sync)

    def emit(self):
        nc = self.nc
        self.barrier(final=True)
        final = self.pending_bar["sp"]
        for o in final:
            if not o.is_dma:
                o.need_sig = True
        for e in ENGS:
            for o in self.ops[e]:
                for d in o.deps:
                    if d.is_dma or self._skip(d, o):
                        continue
                    d.need_sig = True
        for e in CENGS:
            c = 0
            for o in self.ops[e]:
                if (not o.is_dma) and o.need_sig:
                    c += 1
                    o.sigval = c

        def run_stream(e, eh, extra):
            waited = {}

            def do_waits(deps, o):
                need = {}
                for d in deps:
                    if d.is_dma:
                        key = id(d.dsem)
                        sem = d.dsem.sem
                        val = d.dval
                    else:
                        if o is not None and self._skip(d, o):
                            continue
                        key = d.eng
                        sem = self.csem[d.eng]
                        val = d.sigval
                    if waited.get(key, 0) >= val:
                        continue
                    if key not in need or need[key][1] < val:
                        need[key] = (sem, val)
                for key, (sem, val) in need.items():
                    eh.wait_ge(sem, val)
                    waited[key] = val

            for o in self.ops[e]:
                do_waits(o.deps, o)
                if o.is_dma:
                    out, in_ = o.fn
                    eh.dma_start(out=out, in_=in_).then_inc(o.dsem.sem, 16)
                else:
                    ins = o.fn(eh)
                    if o.need_sig:
                        ins.then_inc(self.csem[e], 1)
            if extra:
                do_waits(extra, None)

        with nc.Block() as block:
            @block.sync
            def _(eh):
                run_stream("sp", eh, final)

            @block.tensor
            def _(eh):
                run_stream("pe", eh, None)

            @block.scalar
            def _(eh):
                run_stream("act", eh, None)

            @block.vector
            def _(eh):
                run_stream("dve", eh, None)

            @block.gpsimd
            def _(eh):
                run_stream("pool", eh, None)


def build_program(T_CTX, T_OWN, debug=False):
    HALO = 128 if T_CTX > 0 else 0
    T_ALL = T_CTX + T_OWN
    NQ = HALO + T_OWN
    NKB = T_ALL // 128
    nc = bass.Bass("TRN2", target_bir_lowering=False)

    def din(name, shape):
        return nc.dram_tensor(name, list(shape), F32, kind="ExternalInput").ap()

    xx = din("xx", [T_ALL, D])
    w_in = din("w_in", [D, DIN])
    w_out = din("w_out", [D, D])
    w_up = din("w_up", [D, 2 * DFF])
    w_down = din("w_down", [DFF, D])
    g_pre = din("g_pre", [1, D])
    g_post = din("g_post", [1, D])
    g_pre2 = din("g_pre2", [1, D])
    g_post2 = din("g_post2", [1, D])
    g_grpT = din("g_grpT", [128, 16])
    b_forget = din("b_forget", [16, 1])
    sinks = din("sinks", [1, 16])
    cwT_d = din("cwT", [128, 2 * NFC * 3])
    cbT_d = din("cbT", [128, 2 * NFC])
    ident_d = din("ident", [128, 128])
    masktab_d = din("masktab", [128, 512])
    alhi_d = din("alhi", [128, 16 * 256])
    allo_d = din("allo", [128, 16 * 256])
    ctxrow_d = din("ctxrow", [1, T_ALL])
    flag_d = din("flag", [128, 1])
    out = nc.dram_tensor("out", [T_OWN, D], F32, kind="ExternalOutput").ap()

    skind = "ExternalOutput" if debug else "Internal"
    qfT = nc.dram_tensor("qfT", [NH, 72, NQ], BF16, kind=skind).ap()
    kfT = nc.dram_tensor("kfT", [NH, 72, T_ALL], BF16, kind=skind).ap()
    vf = nc.dram_tensor("vf", [T_ALL, NH * 128], BF16, kind=skind).ap()
    qaT = nc.dram_tensor("qaT", [NH, 66, NQ], BF16, kind=skind).ap()
    kaT = nc.dram_tensor("kaT", [2, 66, T_ALL], BF16, kind=skind).ap()
    va = nc.dram_tensor("va", [T_ALL, 2 * 128], BF16, kind=skind).ap()
    if debug:
        dbg_oa = nc.dram_tensor("dbg_oa", [NQ // 128 + 4, 128, 16 * 512], F32, kind="ExternalOutput").ap()
        dbg_x1 = nc.dram_tensor("dbg_x1", [NQ, D], F32, kind="ExternalOutput").ap()

    w_in_v = w_in.rearrange("(co ci) n -> ci co n", ci=128)
    w_out_v = w_out.rearrange("(co ci) n -> ci co n", ci=128)
    w_up_v = w_up.rearrange("(co ci) n -> ci co n", ci=128)
    w_down_v = w_down.rearrange("(fo fi) n -> fi fo n", fi=128)

    with ExitStack() as root:
        P = Prog(nc, root)

        identf = P.sb([128, 128], F32)
        identb = P.sb([128, 128], BF16)
        onesf = P.sb([128, 128], F32)
        onesb = P.sb([128, 128], BF16)
        masktab = P.sb([128, 512], BF16)
        ggT = P.sb([128, 16], F32)
        esink = P.sb([128, 16], F32)
        negb = P.sb([16, 1], F32)
        cwT = P.sb([128, 2 * NFC, 3], F32)
        cbT = P.sb([128, 2 * NFC], F32)
        carry = P.sb([128, 2 * NFC, 2], F32)
        flag = P.sb([128, 1], F32)
        Rc = Res()
        Rcarry = Res()
        dc = P.dsem()
        dcs = P.dsem(sw=True)
        P.dma("sp", identf[:], ident_d[:, :], dc, w=[Rc])
        P.dma("pool", identb[:], ident_d[:, :], dcs, w=[Rc])
        P.dma("pool", masktab[:], masktab_d[:, :], dcs, w=[Rc])
        P.dma("sp", ggT[:], g_grpT[:, :], dc, w=[Rc])
        P.dma("sp", esink[:], sinks.broadcast_to([128, 16]), dc, w=[Rc])
        P.dma("sp", negb[:], b_forget[:, :], dc, w=[Rc])
        P.dma("sp", cwT[:], cwT_d.rearrange("p (c k) -> p c k", k=3), dc, w=[Rc])
        P.dma("sp", cbT[:], cbT_d[:, :], dc, w=[Rc])
        P.dma("sp", flag[:], flag_d[:, :], dc, w=[Rc])
        P.op("pool", lambda e: e.memset(onesf[:], 1.0), w=[Rc])
        P.op("pool", lambda e: e.memset(onesb[:], 1.0), w=[Rc])
        P.op("pool", lambda e: e.memset(carry[:], 0.0), w=[Rcarry])
        P.op("act", lambda e: e.activation(out=esink[:], in_=esink[:], func=AF.Exp), r=[Rc], w=[Rc])
        P.op("dve", lambda e: e.tensor_scalar(out=negb[:], in0=negb[:], scalar1=-1.0, scalar2=None, op0=ALU.mult),
             r=[Rc], w=[Rc])
        P.barrier()

        def norm_stats(src_ap, Rsrc, junk, Rjunk, small):
            ss, sd, rstd, Rs = small.next()
            P.op("act", lambda e: e.activation(out=junk[:], in_=src_ap, func=AF.Square, accum_out=ss[:]),
                 r=[Rsrc], w=[Rjunk, Rs])
            P.op("act", lambda e: e.activation(out=sd[:], in_=ss[:], func=AF.Sqrt, scale=1.0 / D, bias=EPS),
                 r=[Rs], w=[Rs])
            P.op("dve", lambda e: e.reciprocal(out=rstd[:], in_=sd[:]), r=[Rs], w=[Rs])
            return rstd, Rs

        def make_small_ring(n):
            items = []
            for _ in range(n):
                items.append((P.sb([128, 1], F32), P.sb([128, 1], F32), P.sb([128, 1], F32), Res()))
            return Ring(items)

        def transpose_block(hb, Rhb, dstT, RdstT, col0, tpring, evi, evac=None):
            for half in range(2):
                tp, Rtp = tpring.next()
                for k in range(8):
                    c = half * 8 + k
                    P.op("pe", lambda e, tp=tp, k=k, c=c: e.transpose(
                        out=tp[:, k, :], in_=hb[:, c * 128:(c + 1) * 128], identity=identb[:]),
                        r=[Rhb, Rc], w=[Rtp])
                eng = evac or ("dve" if (evi + half) % 2 == 0 else "act")
                if eng == "dve":
                    P.op("dve", lambda e, tp=tp, half=half: e.tensor_copy(
                        out=dstT[:, half * 8:(half + 1) * 8, col0:col0 + 128], in_=tp[:]),
                        r=[Rtp], w=[RdstT])
                else:
                    P.op("act", lambda e, tp=tp, half=half: e.activation(
                        out=dstT[:, half * 8:(half + 1) * 8, col0:col0 + 128], in_=tp[:], func=AF.Copy),
                        r=[Rtp], w=[RdstT])

        with P.scope():
            spT = P.sb([16, T_ALL], F32)
            RspT = Res()
            passes = []
            if T_CTX:
                passes.append((0, T_CTX, True))
            passes.append((T_CTX, T_OWN, False))
            TP = max(T_CTX, T_OWN)
            with P.scope():
                hT = P.sb([128, 16, TP], BF16)
                RhT = [Res() for _ in range(TP // 512)]
                gpre = P.sb([128, D], F32)
                Rg = Res()
                dg = P.dsem()
                P.dma("sp", gpre[:], g_pre.broadcast_to([128, D]), dg, w=[Rg])
                junk = P.sb([128, D], BF16)
                Rjunk = Res()
                small = make_small_ring(3)
                xring = Ring([(P.sb([128, D], F32), Res(), P.dsem()) for _ in range(4)])
                hbring = Ring([(P.sb([128, D], BF16), Res()) for _ in range(2)])
                tpring = Ring([(P.ps([128, 8, 128], BF16), Res()) for _ in range(2)])
                mmring = Ring([(P.ps([128, 512], F32), Res()) for _ in range(4)])
                wring = Ring([(P.sb([128, 16, 256], BF16), Res(), P.dsem(sw=True)) for _ in range(3)])
                stgring = Ring([(P.sb([128, 512], BF16), Res(), P.dsem()) for _ in range(4)])
                vstring = Ring([(P.sb([128, 4, 128], BF16), Res(), P.dsem()) for _ in range(3)])
                etring = Ring([(P.sb([16, 512], F32), Res()) for _ in range(2)])
                for (vst, Rv, _d) in vstring.items:
                    P.op("pool", lambda e, vst=vst: e.memset(vst[:], 1.0), w=[Rv])
                evc = [0]

                for (tok0, ntok, is_ctx) in passes:
                    ntile = ntok // 512
                    full_tiles = [(t0, 512) for t0 in range(0, ntok, 512)]
                    halo_tiles = [(ntok - 128, 128)]

                    def qcol(t0):
                        return (t0 - (ntok - 128)) if is_ctx else (HALO + t0)

                    def fm(Wt, RW, col_lo, tiles, scale, dest_fn):
                        for (t0, tw) in tiles:
                            ps, Rps = mmring.next()
                            Rh = RhT[t0 // 512]
                            for c in range(16):
                                P.op("pe", lambda e, ps=ps, c=c, t0=t0, tw=tw: e.matmul(
                                    ps[:, 0:tw], lhsT=Wt[:, c, col_lo:col_lo + 128], rhs=hT[:, c, t0:t0 + tw],
                                    start=(c == 0), stop=(c == 15)), r=[RW, Rh], w=[Rps])
                            stg, Rstg, dstg = stgring.next()
                            evc[0] += 1
                            if evc[0] % 2 == 0:
                                P.op("act", lambda e, ps=ps, stg=stg, tw=tw: e.activation(
                                    out=stg[:, 0:tw], in_=ps[:, 0:tw], func=AF.Copy, scale=scale),
                                    r=[Rps], w=[Rstg])
                            else:
                                P.op("dve", lambda e, ps=ps, stg=stg, tw=tw: e.tensor_scalar(
                                    out=stg[:, 0:tw], in0=ps[:, 0:tw], scalar1=scale, scalar2=None, op0=ALU.mult),
                                    r=[Rps], w=[Rstg])
                            for hh in range(2):
                                P.dma("sp", dest_fn(hh, t0, tw), stg[hh * 64:(hh + 1) * 64, 0:tw], dstg, r=[Rstg])

                    def tm(Wt, RW, col_lo, nheads, dst, dcol0, tbs):
                        ncols = nheads * 64
                        for tb in tbs:
                            ps, Rps = mmring.next()
                            Rh = RhT[tb // 4]
                            for c in range(16):
                                P.op("pe", lambda e, ps=ps, c=c, tb=tb: e.matmul(
                                    ps[:, 0:ncols], lhsT=hT[:, c, tb * 128:(tb + 1) * 128],
                                    rhs=Wt[:, c, col_lo:col_lo + ncols], start=(c == 0), stop=(c == 15)),
                                    r=[RW, Rh], w=[Rps])
                            vst, Rvst, dvst = vstring.next()
                            evc[0] += 1
                            src = ps[:, 0:ncols].rearrange("p (h d) -> p h d", d=64)
                            if evc[0] % 2 == 0:
                                P.op("act", lambda e, vst=vst, src=src: e.activation(
                                    out=vst[:, 0:nheads, 0:64], in_=src, func=AF.Copy), r=[Rps], w=[Rvst])
                            else:
                                P.op("dve", lambda e, vst=vst, src=src: e.tensor_copy(
                                    out=vst[:, 0:nheads, 0:64], in_=src), r=[Rps], w=[Rvst])
                            r0 = tok0 + tb * 128
                            P.dma("sp", dst[r0:r0 + 128, dcol0:dcol0 + nheads * 128].rearrange("p (h d) -> p h d", d=128),
                                  vst[:, 0:nheads, :], dvst, r=[Rvst])

                    def do_chunk(ch, Wt, RW, tsel):
                        ft = full_tiles if tsel is None else [full_tiles[tsel]]
                        tbs = range(ntok // 128) if tsel is None else range(tsel * 4, tsel * 4 + 4)
                        if ch < 4 or 5 <= ch <= 8:
                            dstT = qaT if ch < 4 else qfT
                            h0 = (ch if ch < 4 else ch - 5) * 4
                            if is_ctx:
                                tiles = halo_tiles if (tsel is None or tsel == ntile - 1) else []
                            else:
                                tiles = ft
                            for sub in range(2):
                                fm(Wt, RW, sub * 128, tiles, 0.125,
                                   lambda hh, t0, tw, h0=h0, sub=sub, dstT=dstT:
                                   dstT[h0 + 2 * sub + hh, 0:64, qcol(t0):qcol(t0) + tw])
                        elif ch == 4:
                            fm(Wt, RW, 0, ft, 1.0,
                               lambda hh, t0, tw: kaT[hh, 0:64, tok0 + t0: tok0 + t0 + tw])
                            tm(Wt, RW, 128, 2, va, 0, tbs)
                        elif 9 <= ch <= 12:
                            h0 = (ch - 9) * 4
                            for sub in range(2):
                                fm(Wt, RW, sub * 128, ft, 1.0,
                                   lambda hh, t0, tw, h0=h0, sub=sub:
                                   kfT[h0 + 2 * sub + hh, 0:64, tok0 + t0: tok0 + t0 + tw])
                        elif 13 <= ch <= 16:
                            tm(Wt, RW, 0, 4, vf, (ch - 13) * 4 * 128, tbs)
                        else:
                            for (t0, tw) in ft:
                                ps, Rps = mmring.next()
                                Rh = RhT[t0 // 512]
                                for c in range(16):
                                    P.op("pe", lambda e, ps=ps, c=c, t0=t0, tw=tw, Wt=Wt: e.matmul(
                                        ps[0:16, 0:tw], lhsT=Wt[:, c, 0:16], rhs=hT[:, c, t0:t0 + tw],
                                        start=(c == 0), stop=(c == 15)), r=[RW, Rh], w=[Rps])
                                et, Ret = etring.next()
                                P.op("act", lambda e, ps=ps, et=et, tw=tw: e.activation(
                                    out=et[:, 0:tw], in_=ps[0:16, 0:tw], func=AF.Exp, scale=-1.0, bias=negb[:, 0:1]),
                                    r=[Rps, Rc], w=[Ret])
                                P.op("act", lambda e, et=et, t0=t0, tw=tw, tok0=tok0: e.activation(
                                    out=spT[:, tok0 + t0: tok0 + t0 + tw], in_=et[:, 0:tw], func=AF.Ln, bias=1.0),
                                    r=[Ret], w=[RspT])

                    def load_chunk(ch):
                        Wt, RW, dW = wring.next()
                        ncol = 256 if ch < 17 else 16
                        P.dma("pool", Wt[:, :, 0:ncol], w_in_v[:, :, ch * 256: ch * 256 + ncol], dW, w=[RW])
                        return Wt, RW

                    order = ([4, 9, 10, 11, 12, 13, 14, 15, 16, 17, 0, 1, 2, 3, 5, 6, 7, 8] if is_ctx
                             else list(range(18)))
                    NE = 3
                    early = [(ch,) + load_chunk(ch) for ch in order[:NE]]

                    def after_block(b):
                        if (b + 1) % 4 == 0:
                            for (ch, Wt, RW) in early:
                                do_chunk(ch, Wt, RW, b // 4)

                    pend = None
                    for tb in range(ntok // 128):
                        xs, Rxs, dxs = xring.next()
                        P.dma("sp", xs[:], xx[tok0 + tb * 128: tok0 + (tb + 1) * 128, :], dxs, w=[Rxs])
                        rstd, Rs = norm_stats(xs[:], Rxs, junk, Rjunk, small)
                        hb, Rhb = hbring.next()
                        P.op("dve", lambda e, hb=hb, xs=xs, rstd=rstd: e.scalar_tensor_tensor(
                            out=hb[:], in0=xs[:], scalar=rstd[:, 0:1], in1=gpre[:], op0=ALU.mult, op1=ALU.mult),
                            r=[Rxs, Rs, Rg], w=[Rhb])
                        if pend is not None:
                            transpose_block(*pend)
                            after_block(tb - 1)
                        pend = (hb, Rhb, hT, RhT[tb // 4], tb * 128, tpring, tb)
                    transpose_block(*pend)
                    after_block(ntok // 128 - 1)

                    for ch in order[NE:]:
                        Wt, RW = load_chunk(ch)
                        do_chunk(ch, Wt, RW, None)

            with P.scope():
                zeros = P.sb([16, T_ALL], F32)
                cs = P.sb([16, T_ALL], F32)
                r1 = P.sb([16, T_ALL], F32)
                hi = P.sb([16, T_ALL], BF16)
                mid = P.sb([16, T_ALL], BF16)
                lo = P.sb([16, T_ALL], BF16)
                nq3 = P.sb([16, 3, NQ], BF16)
                ones = P.sb([16, T_ALL], BF16)
                row70 = P.sb([16, NQ], BF16)
                ctxb = P.sb([16, T_ALL], BF16)
                Rz = Res()
                dz = P.dsem(sw=True)
                dz2 = P.dsem()
                P.op("pool", lambda e: e.memset(zeros[:], 0.0), w=[Rz])
                P.op("pool", lambda e: e.memset(ones[:], 1.0), w=[Rz])
                P.op("pool", lambda e: e.memset(row70[:], 1.0), w=[Rz])
                if HALO:
                    P.op("pool", lambda e: e.memset(row70[:, 0:HALO], 0.0), w=[Rz])
                P.dma("pool", ctxb[:], ctxrow_d.broadcast_to([16, T_ALL]), dz, w=[Rz])
                V_ = "dve"
                P.op(V_, lambda e: e.tensor_tensor_scan(out=cs[:], data0=spT[:], data1=zeros[:], initial=0.0,
                                                        op0=ALU.add, op1=ALU.add), r=[RspT, Rz], w=[Rz])
                P.op(V_, lambda e: e.tensor_copy(out=hi[:], in_=cs[:]), r=[Rz], w=[Rz])
                P.op(V_, lambda e: e.tensor_tensor(out=r1[:], in0=cs[:], in1=hi[:], op=ALU.subtract), r=[Rz], w=[Rz])
                P.op(V_, lambda e: e.tensor_copy(out=mid[:], in_=r1[:]), r=[Rz], w=[Rz])
                P.op(V_, lambda e: e.tensor_tensor(out=cs[:], in0=r1[:], in1=mid[:], op=ALU.subtract), r=[Rz], w=[Rz])
                P.op(V_, lambda e: e.tensor_copy(out=lo[:], in_=cs[:]), r=[Rz], w=[Rz])
                qs = T_CTX - HALO
                for j, src in enumerate((hi, mid, lo)):
                    P.op(V_, lambda e, j=j, src=src: e.tensor_scalar(
                        out=nq3[:, j, :], in0=src[:, qs:T_ALL], scalar1=-1.0, scalar2=None, op0=ALU.mult),
                        r=[Rz], w=[Rz])
                P.dma("sp", qfT[:, 64:67, :], nq3[:], dz2, r=[Rz])
                for j, src in enumerate((hi, mid, lo)):
                    P.dma("sp", kfT[:, 67 + j, :], src[:], dz2, r=[Rz])
                for j in range(3):
                    P.dma("sp", qfT[:, 67 + j, :], ones[:, 0:NQ], dz2, r=[Rz])
                    P.dma("sp", kfT[:, 64 + j, :], ones[:], dz2, r=[Rz])
                P.dma("sp", qfT[:, 70, :], row70[:], dz2, r=[Rz])
                P.dma("sp", qaT[:, 64, :], row70[:], dz2, r=[Rz])
                P.dma("sp", kfT[:, 70, :], ctxb[:], dz2, r=[Rz])
                P.dma("sp", kaT[:, 64, :], ctxb[0:2, :], dz2, r=[Rz])
                zb = P.sb([16, T_ALL], BF16)
                P.op("pool", lambda e: e.memset(zb[:], 0.0), w=[Rz])
                P.dma("sp", qfT[:, 71, :], zb[:, 0:NQ], dz2, r=[Rz])
                P.dma("sp", kfT[:, 71, :], zb[:], dz2, r=[Rz])
                P.dma("sp", qaT[:, 65, :], zb[:, 0:NQ], dz2, r=[Rz])
                P.dma("sp", kaT[:, 65, :], zb[0:2, :], dz2, r=[Rz])

        tiles = []
        if HALO:
            tiles.append((0, 128, True))
        for i in range(T_OWN // 512):
            tiles.append((HALO + i * 512, 512, False))

        with P.scope():
            h2T_halo = P.sb([128, 16, 128], BF16)
            Rh2h = Res()
            OA = P.sb([128, 16, 512], F32)
            ROA = Res()
            have_halo = [False]

            def do_tile(ti, q0, W, is_halo):
                ts = T_CTX - HALO + q0
                kb0 = ts // 128
                nqb = W // 128
                nkb = kb0 + nqb
                with P.scope():

                    with P.scope():
                        sring = Ring([(P.ps([128, 512], F32), Res()) for _ in range(3)])
                        oring = Ring([(P.ps([128, 512], F32), Res()) for _ in range(2)])
                        oring_swa = Ring([(P.ps([128, 512], F32), Res()) for _ in range(1)])
                        sring_swa = Ring([(P.ps([128, 512], F32), Res()) for _ in range(2)])
                        ptring_swa = Ring([(P.sb([128, 512], BF16), Res()) for _ in range(2)])
                        ptring = Ring([(P.sb([128, 512], BF16), Res()) for _ in range(4)])
                        ktring = Ring([(P.sb([72, T_ALL], BF16), Res(), P.dsem()) for _ in range(2)])
                        qtring = Ring([(P.sb([72, 512], BF16), Res(), P.dsem()) for _ in range(3)])
                        vgring = Ring([(P.sb([128, NKB, 512], BF16), (Res(), Res()), (P.dsem(), P.dsem()))
                                       for _ in range(2)])
                        rdring = Ring([(P.sb([64, 512], F32), Res()) for _ in range(2)])
                        alhi = P.sb([128, 16, 256], BF16)
                        allo = P.sb([128, 16, 256], BF16)
                        kbase = max(kb0 - 1, 0)
                        nka = nkb - kbase
                        KaT = P.sb([66, 2, nka * 128], BF16)
                        VA = P.sb([128, nka, 256], BF16)
                        Rtab = Res()
                        dtab = P.dsem()
                        dtabs = P.dsem(sw=True)
                        P.dma("sp", KaT[:], kaT.rearrange("k r t -> r k t")[:, :, kbase * 128: nkb * 128], dtab, w=[Rtab])
                        P.dma("sp", VA[:], va[kbase * 128: nkb * 128, :].rearrange("(kb p) c -> p kb c", p=128),
                              dtab, w=[Rtab])

                        def finalize(h, O, RO, ch, pb, is_swa):
                            rd, Rrd = rdring.next()
                            if is_swa:
                                P.op("dve", lambda e: e.tensor_scalar(
                                    out=rd[:, 0:W], in0=O[64:128, 0:W], scalar1=esink[64:128, h:h + 1],
                                    scalar2=None, op0=ALU.add), r=[RO, Rc], w=[Rrd])
                                P.op("dve", lambda e: e.reciprocal(out=rd[:, 0:W], in_=rd[:, 0:W]),
                                     r=[Rrd], w=[Rrd])
                            else:
                                P.op("dve", lambda e: e.reciprocal(out=rd[:, 0:W], in_=O[64:128, 0:W]),
                                     r=[RO], w=[Rrd])
                            P.op("dve", lambda e: e.tensor_tensor(
                                out=OA[pb:pb + 64, ch, 0:W], in0=O[0:64, 0:W], in1=rd[:, 0:W], op=ALU.mult),
                                r=[RO, Rrd], w=[ROA])

                        def swa_head(h):
                            kv = h // 8
                            qt, Rqt, dqt = qtring.next()
                            P.dma("sp", qt[0:66, 0:W], qaT[h, :, q0:q0 + W], dqt, w=[Rqt])
                            O, RO = oring_swa.next()
                            pvs = []
                            for n0 in range(0, nqb, 2):
                                S, RS = sring_swa.next()
                                PT, RPT = ptring_swa.next()
                                ns = [n for n in (n0, n0 + 1) if n < nqb]
                                lo_c = None
                                for n in ns:
                                    cur = kb0 + n
                                    prev = cur - 1
                                    base = (n % 2) * 256
                                    a0 = 0 if prev >= 0 else 128
                                    if lo_c is None:
                                        lo_c = base + a0
                                    P.op("pe", lambda e, base=base, a0=a0, S=S: e.matmul(
                                        S[:, base + a0:base + 256], lhsT=identb[:], rhs=alhi[:, h, a0:256],
                                        start=True, stop=False), r=[Rc, Rtab], w=[RS])
                                    P.op("pe", lambda e, base=base, a0=a0, S=S: e.matmul(
                                        S[:, base + a0:base + 256], lhsT=identb[:], rhs=allo[:, h, a0:256],
                                        start=False, stop=False), r=[Rc, Rtab], w=[RS])
                                    if prev >= 0:
                                        P.op("pe", lambda e, base=base, prev=prev, n=n, S=S: e.matmul(
                                            S[:, base:base + 128],
                                            lhsT=KaT[0:65, kv, (prev - kbase) * 128:(prev - kbase + 1) * 128],
                                            rhs=qt[0:65, n * 128:(n + 1) * 128], start=False, stop=False),
                                            r=[Rtab, Rqt], w=[RS])
                                    P.op("pe", lambda e, base=base, cur=cur, n=n, S=S: e.matmul(
                                        S[:, base + 128:base + 256],
                                        lhsT=KaT[0:65, kv, (cur - kbase) * 128:(cur - kbase + 1) * 128],
                                        rhs=qt[0:65, n * 128:(n + 1) * 128], start=False, stop=True),
                                        r=[Rtab, Rqt], w=[RS])
                                hi_c = (ns[-1] % 2) * 256 + 256
                                P.op("act", lambda e, lo_c=lo_c, hi_c=hi_c, S=S, PT=PT: e.activation(
                                    out=PT[:, lo_c:hi_c], in_=S[:, lo_c:hi_c], func=AF.Exp), r=[RS], w=[RPT])
                                pvs.append((ns, PT, RPT))

                            def part_b():
                                for (ns, PT, RPT) in pvs:
                                    for n in ns:
                                        cur = kb0 + n
                                        prev = cur - 1
                                        base = (n % 2) * 256
                                        if prev >= 0:
                                            P.op("pe", lambda e, base=base, prev=prev, n=n, PT=PT: e.matmul(
                                                O[:, n * 128:(n + 1) * 128],
                                                lhsT=VA[:, prev - kbase, kv * 128:(kv + 1) * 128],
                                                rhs=PT[:, base:base + 128], start=True, stop=False),
                                                r=[Rtab, RPT], w=[RO])
                                        P.op("pe", lambda e, base=base, cur=cur, n=n, PT=PT, prev=prev: e.matmul(
                                            O[:, n * 128:(n + 1) * 128],
                                            lhsT=VA[:, cur - kbase, kv * 128:(kv + 1) * 128],
                                            rhs=PT[:, base + 128:base + 256], start=(prev < 0), stop=True),
                                            r=[Rtab, RPT], w=[RO])
                                finalize(h, O, RO, h // 2, (h % 2) * 64, True)
                            return part_b
                        units = [(h, j) for h in range(NH) for j in range(nkb)]
                        state = {}
                        headres = {}
                        deferred = []
                        swa_b = {}
                        LOOK = 2

                        vhalf = (nkb + 1) // 2

                        def head_setup(h):
                            kt, Rkt, dkt = ktring.next()
                            P.dma("sp", kt[:, 0:nkb * 128], kfT[h, :, 0:nkb * 128], dkt, w=[Rkt])
                            qt, Rqt, dqt = qtring.next()
                            P.dma("sp", qt[:, 0:W], qfT[h, :, q0:q0 + W], dqt, w=[Rqt])
                            if h % 4 == 0:
                                vg, Rvg, dvg = vgring.next()
                                g = h // 4
                                for k, (a, b) in enumerate(((0, vhalf), (vhalf, nkb))):
                                    if b > a:
                                        P.dma("sp", vg[:, a:b, :],
                                              vf[a * 128:b * 128, g * 512:(g + 1) * 512].rearrange(
                                                  "(kb p) c -> p kb c", p=128), dvg[k], w=[Rvg[k]])
                                headres["vg"] = (vg, Rvg)
                                if h == 0:
                                    P.dma("pool", alhi[:], alhi_d.rearrange("p (h t) -> p h t", t=256), dtabs,
                                          r=[Rvg[0]], w=[Rtab])
                                    P.dma("pool", allo[:], allo_d.rearrange("p (h t) -> p h t", t=256), dtabs,
                                          r=[Rvg[0]], w=[Rtab])
                            O, RO = oring.next()
                            headres[h] = (kt, Rkt, qt, Rqt, O, RO) + headres["vg"]

                        def emit_S(h, j):
                            if j == 0:
                                head_setup(h)
                            kt, Rkt, qt, Rqt, O, RO, vg, Rvg = headres[h]
                            o = j - kb0
                            c0 = max(o, 0) * 128
                            N = W - c0
                            S, RS = sring.next()
                            if o >= 0:
                                P.op("pe", lambda e: e.matmul(S[:, c0:W], lhsT=identb[:], rhs=masktab[:, 0:N],
                                                              start=True, stop=False), r=[Rc], w=[RS])
                                P.op("pe", lambda e: e.matmul(S[:, c0:W], lhsT=kt[0:71, j * 128:(j + 1) * 128],
                                                              rhs=qt[0:71, c0:W], start=False, stop=True),
                                     r=[Rkt, Rqt], w=[RS])
                            else:
                                P.op("pe", lambda e: e.matmul(S[:, 0:W], lhsT=kt[0:71, j * 128:(j + 1) * 128],
                                                              rhs=qt[0:71, 0:W], start=True, stop=True),
                                     r=[Rkt, Rqt], w=[RS])
                            PT, RPT = ptring.next()
                            P.op("act", lambda e: e.activation(out=PT[:, c0:W], in_=S[:, c0:W], func=AF.Exp),
                                 r=[RS], w=[RPT])
                            state[(h, j)] = (PT, RPT, c0)

                        def emit_PV(h, j):
                            kt, Rkt, qt, Rqt, O, RO, vg, Rvg = headres[h]
                            PT, RPT, c0 = state.pop((h, j))
                            hl = h % 4
                            P.op("pe", lambda e: e.matmul(O[:, c0:W], lhsT=vg[:, j, hl * 128:(hl + 1) * 128],
                                                          rhs=PT[:, c0:W], start=(j == 0), stop=(j == nkb - 1)),
                                 r=[Rvg[0 if j < vhalf else 1], RPT], w=[RO])
                            if j == nkb - 1:
                                finalize(h, O, RO, 8 + h // 2, (h % 2) * 64, False)

                        def run_deferred(force=False):
                            for d in list(deferred):
                                d[0] -= 1
                                if d[0] <= 0 or force:
                                    d[1]()
                                    deferred.remove(d)

                        for i in range(len(units) + LOOK):
                            if i < len(units):
                                emit_S(*units[i])
                                if units[i][1] == nkb // 2:
                                    swa_b[units[i][0]] = swa_head(units[i][0])
                                if units[i][1] == min(nkb // 2 + 3, nkb - 1):
                                    swa_b.pop(units[i][0])()
                            if i - LOOK >= 0:
                                emit_PV(*units[i - LOOK])
                            run_deferred()
                        run_deferred(force=True)


                    if debug:
                        dd = P.dsem()
                        P.dma("sp", dbg_oa[ti].rearrange("p (c t) -> p c t", t=512), OA[:], dd, r=[ROA])

                    x1 = P.sb([128, 4, D], F32)
                    Rx1 = [Res() for _ in range(4)]
                    h2T = P.sb([128, 16, 512], BF16)
                    Rh2T = Res()
                    wu0 = None
                    if not is_halo:
                        wu0 = (P.sb([128, 16, 512], BF16), Res(), P.dsem(sw=True))
                    with P.scope():
                      gpost = P.sb([128, D], F32)
                      gpre2 = P.sb([128, D], F32)
                      Rg = Res()
                      dg = P.dsem()
                      P.dma("sp", gpost[:], g_post.broadcast_to([128, D]), dg, w=[Rg])
                      P.dma("sp", gpre2[:], g_pre2.broadcast_to([128, D]), dg, w=[Rg])
                      xring = Ring([(P.sb([128, D], F32), Res(), P.dsem()) for _ in range(2)])
                      with P.scope():
                        sqs = [(P.sb([128, 8, 512], BF16), Res()) for _ in range(2)]
                        onT = P.sb([128, 16, 512], BF16)
                        RonT = Res()
                        rsbs = [(P.sb([128, 512], F32), Res()) for _ in range(2)]
                        ssbs = [(P.ps([128, 512], F32), Res()) for _ in range(2)]
                        for g in range(2):
                            sq, Rsq = sqs[g]
                            P.op("act", lambda e, g=g, sq=sq: e.activation(
                                out=sq[:, :, 0:W], in_=OA[:, g * 8:(g + 1) * 8, 0:W], func=AF.Square),
                                r=[ROA], w=[Rsq])
                        for g in range(2):
                            sq, Rsq = sqs[g]
                            ssb, Rssb = ssbs[g]
                            for k in range(8):
                                P.op("pe", lambda e, k=k, sq=sq, ssb=ssb: e.matmul(
                                    ssb[:, 0:W], lhsT=onesb[:], rhs=sq[:, k, 0:W], start=(k == 0), stop=(k == 7)),
                                    r=[Rsq, Rc], w=[Rssb])
                        for g in range(2):
                            ssb, Rssb = ssbs[g]
                            rsb, Rrsb = rsbs[g]
                            P.op("act", lambda e, ssb=ssb, rsb=rsb: e.activation(
                                out=rsb[:, 0:W], in_=ssb[:, 0:W], func=AF.Sqrt, scale=1.0 / 1024, bias=EPS),
                                r=[Rssb], w=[Rrsb])
                            P.op("dve", lambda e, rsb=rsb: e.reciprocal(out=rsb[:, 0:W], in_=rsb[:, 0:W]),
                                 r=[Rrsb], w=[Rrsb])
                            for k in range(8):
                                c = g * 8 + k
                                P.op("dve", lambda e, c=c, rsb=rsb: e.scalar_tensor_tensor(
                                    out=onT[:, c, 0:W], in0=OA[:, c, 0:W], scalar=ggT[:, c:c + 1], in1=rsb[:, 0:W],
                                    op0=ALU.mult, op1=ALU.mult), r=[ROA, Rrsb, Rc], w=[RonT])
                        woring = Ring([(P.sb([128, 16, 512], BF16), Res(), P.dsem(sw=True)) for _ in range(2)])
                        mmring = Ring([(P.ps([128, 512], F32), Res()) for _ in range(3)])
                        evc = 0
                        for nt in range(4):
                            Wo, RWo, dWo = woring.next()
                            P.dma("pool", Wo[:], w_out_v[:, :, nt * 512:(nt + 1) * 512], dWo, w=[RWo])
                            for tb in range(nqb):
                                ps, Rps = mmring.next()
                                for c in range(16):
                                    P.op("pe", lambda e, ps=ps, c=c, tb=tb, Wo=Wo: e.matmul(
                                        ps[:, :], lhsT=onT[:, c, tb * 128:(tb + 1) * 128], rhs=Wo[:, c, :],
                                        start=(c == 0), stop=(c == 15)), r=[RonT, RWo], w=[Rps])
                                evc += 1
                                if evc % 2 == 0:
                                    P.op("act", lambda e, ps=ps, tb=tb, nt=nt: e.activation(
                                        out=x1[:, tb, nt * 512:(nt + 1) * 512], in_=ps[:, :], func=AF.Copy),
                                        r=[Rps], w=[Rx1[tb]])
                                else:
                                    P.op("dve", lambda e, ps=ps, tb=tb, nt=nt: e.tensor_copy(
                                        out=x1[:, tb, nt * 512:(nt + 1) * 512], in_=ps[:, :]), r=[Rps], w=[Rx1[tb]])
                      if True:
                        if wu0 is not None:
                            P.dma("pool", wu0[0][:, :, 0:256], w_up_v[:, :, 0:256], wu0[2], w=[wu0[1]])
                            P.dma("pool", wu0[0][:, :, 256:512], w_up_v[:, :, DFF:DFF + 256], wu0[2], w=[wu0[1]])
                        junk = P.sb([128, D], BF16)
                        Rjunk = Res()
                        junk2 = P.sb([128, D], BF16)
                        Rjunk2 = Res()
                        small = make_small_ring(8)
                        hbring = Ring([(P.sb([128, D], BF16), Res()) for _ in range(2)])
                        tpring = Ring([(P.ps([128, 8, 128], BF16), Res()) for _ in range(2)])
                        dst_h2T, Rdst = (h2T_halo, Rh2h) if is_halo else (h2T, Rh2T)
                        hbs = {}

                        def stA(tb):
                            xs, Rxs, dxs = xring.next()
                            P.dma("sp", xs[:], xx[ts + tb * 128: ts + (tb + 1) * 128, :], dxs, w=[Rxs])
                            rstd, Rs = norm_stats(x1[:, tb, :], Rx1[tb], junk, Rjunk, small)
                            P.op("dve", lambda e: e.scalar_tensor_tensor(
                                out=x1[:, tb, :], in0=x1[:, tb, :], scalar=rstd[:, 0:1], in1=gpost[:],
                                op0=ALU.mult, op1=ALU.mult), r=[Rx1[tb], Rs, Rg], w=[Rx1[tb]])
                            P.op("pool", lambda e: e.tensor_tensor(
                                out=x1[:, tb, :], in0=x1[:, tb, :], in1=xs[:], op=ALU.add),
                                r=[Rx1[tb], Rxs], w=[Rx1[tb]])
                            if debug:
                                dd = P.dsem()
                                P.dma("sp", dbg_x1[q0 + tb * 128: q0 + (tb + 1) * 128, :], x1[:, tb, :], dd, r=[Rx1[tb]])

                        def stB(tb):
                            rstd2, Rs2 = norm_stats(x1[:, tb, :], Rx1[tb], junk2, Rjunk2, small)
                            hb, Rhb = hbring.next()
                            hbs[tb] = (hb, Rhb)
                            P.op("dve", lambda e: e.scalar_tensor_tensor(
                                out=hb[:], in0=x1[:, tb, :], scalar=rstd2[:, 0:1], in1=gpre2[:],
                                op0=ALU.mult, op1=ALU.mult), r=[Rx1[tb], Rs2, Rg], w=[Rhb])

                        def stC(tb):
                            hb, Rhb = hbs.pop(tb)
                            transpose_block(hb, Rhb, dst_h2T, Rdst, tb * 128, tpring, 0, evac="dve")

                        for step in range(nqb + 3):
                            for fn, lag in ((stA, 0), (stB, 2), (stC, 3)):
                                if 0 <= step - lag < nqb:
                                    fn(step - lag)
                    if is_halo:
                        have_halo[0] = True
                        return

                    with P.scope():
                        aT = P.sb([128, NFC, 512], BF16)
                        RaT = Res()
                        use_halo = have_halo[0]
                        have_halo[0] = False
                        with P.scope():
                            wuring = Ring([wu0] + [(P.sb([128, 16, 512], BF16), Res(), P.dsem(sw=True))
                                                   for _ in range(2)])
                            wuring.next()
                            wu_slots = {0: (wu0[0], wu0[1])}
                            upring = Ring([(P.ps([128, 512], F32), Res()) for _ in range(4)])
                            phring = Ring([(P.ps([128, 2], F32), Res()) for _ in range(2)])
                            uering = Ring([(P.sb([128, 514], F32), Res()) for _ in range(3)])
                            ybring = Ring([(P.sb([128, 512], F32), Res()) for _ in range(4)])
                            glring = Ring([(P.sb([128, 512], F32), Res()) for _ in range(2)])
                            for fc in range(NFC):
                                if fc % 2 == 0:
                                    for g in ([1, 2] if fc == 0 else [fc // 2 + 2]):
                                        if g < NFC // 2:
                                            Wn, RWn, dWn = wuring.next()
                                            wu_slots[g] = (Wn, RWn)
                                            P.dma("pool", Wn[:, :, 0:256], w_up_v[:, :, g * 256:(g + 1) * 256],
                                                  dWn, w=[RWn])
                                            P.dma("pool", Wn[:, :, 256:512],
                                                  w_up_v[:, :, DFF + g * 256: DFF + (g + 1) * 256], dWn, w=[RWn])
                                    Wu, RWu = wu_slots.pop(fc // 2)
                                wo = (fc % 2) * 128
                                ys = []
                                for part in range(2):
                                    idx = part * NFC + fc
                                    ue, Rue = uering.next()
                                    if use_halo:
                                        ph, Rph = phring.next()
                                        for c in range(16):
                                            P.op("pe", lambda e, ph=ph, c=c, Wu=Wu, part=part, wo=wo: e.matmul(
                                                ph[:, :], lhsT=Wu[:, c, part * 256 + wo:part * 256 + wo + 128],
                                                rhs=h2T_halo[:, c, 126:128], start=(c == 0), stop=(c == 15)),
                                                r=[RWu, Rh2h], w=[Rph])
                                        P.op("dve", lambda e, ph=ph, ue=ue: e.tensor_scalar(
                                            out=ue[:, 0:2], in0=ph[:, 0:2], scalar1=flag[:, 0:1], scalar2=None,
                                            op0=ALU.mult), r=[Rph, Rc], w=[Rue])
                                    else:
                                        P.op("act", lambda e, ue=ue, idx=idx: e.activation(
                                            out=ue[:, 0:2], in_=carry[:, idx, :], func=AF.Copy), r=[Rcarry], w=[Rue])
                                    ps, Rps = upring.next()
                                    for c in range(16):
                                        P.op("pe", lambda e, ps=ps, c=c, Wu=Wu, part=part, wo=wo: e.matmul(
                                            ps[:, :], lhsT=Wu[:, c, part * 256 + wo:part * 256 + wo + 128], rhs=h2T[:, c, :],
                                            start=(c == 0), stop=(c == 15)), r=[RWu, Rh2T], w=[Rps])
                                    P.op("act", lambda e, ps=ps, ue=ue: e.activation(
                                        out=ue[:, 2:514], in_=ps[:, :], func=AF.Copy), r=[Rps], w=[Rue])
                                    P.op("act", lambda e, ue=ue, idx=idx: e.activation(
                                        out=carry[:, idx, :], in_=ue[:, 512:514], func=AF.Copy), r=[Rue], w=[Rcarry])
                                    yb, Ryb = ybring.next()
                                    P.op("act", lambda e, ps=ps, yb=yb, idx=idx: e.activation(
                                        out=yb[:], in_=ps[:, :], func=AF.Identity, scale=cwT[:, idx, 2:3],
                                        bias=cbT[:, idx:idx + 1]), r=[Rps, Rc], w=[Ryb])
                                    P.op("dve", lambda e, ue=ue, yb=yb, idx=idx: e.scalar_tensor_tensor(
                                        out=yb[:], in0=ue[:, 1:513], scalar=cwT[:, idx, 1:2], in1=yb[:],
                                        op0=ALU.mult, op1=ALU.add), r=[Rue, Rc, Ryb], w=[Ryb])
                                    P.op("dve", lambda e, ue=ue, yb=yb, idx=idx: e.scalar_tensor_tensor(
                                        out=yb[:], in0=ue[:, 0:512], scalar=cwT[:, idx, 0:1], in1=yb[:],
                                        op0=ALU.mult, op1=ALU.add), r=[Rue, Rc, Ryb], w=[Ryb])
                                    ys.append((yb, Ryb))
                                gl, Rgl = glring.next()
                                (yg, Ryg), (yv, Ryv) = ys
                                P.op("act", lambda e, gl=gl, yg=yg: e.activation(
                                    out=gl[:], in_=yg[:], func=AF.Gelu_apprx_tanh), r=[Ryg], w=[Rgl])
                                P.op("dve", lambda e, gl=gl, yv=yv, fc=fc: e.tensor_tensor(
                                    out=aT[:, fc, :], in0=gl[:], in1=yv[:], op=ALU.mult), r=[Rgl, Ryv], w=[RaT])

                        with P.scope():
                            yt = OA[:].rearrange("p (a b) c -> p a (b c)", a=4)
                            Ryt = [Res() for _ in range(4)]
                            wdring = Ring([(P.sb([128, 4, 512], BF16), Res(), P.dsem(sw=True)) for _ in range(4)])
                            acc = [(P.ps([128, 512], F32), Res()) for _ in range(4)]
                            gpost2 = P.sb([128, D], F32)
                            Rg = Res()
                            dg = P.dsem()
                            P.dma("sp", gpost2[:], g_post2.broadcast_to([128, D]), dg, w=[Rg])
                            small = make_small_ring(4)
                            dout = [P.dsem() for _ in range(4)]
                            for d in dout:
                                d.nobar = True
                            evc = 0
                            for nt in range(4):
                                for piece in range(11):
                                    Wd, RWd, dWd = wdring.next()
                                    P.dma("pool", Wd[:], w_down_v[:, piece * 4:(piece + 1) * 4,
                                                                   nt * 512:(nt + 1) * 512], dWd, w=[RWd])
                                    for tb in range(4):
                                        a_ps, Ra = acc[tb]
                                        for k in range(4):
                                            fc = piece * 4 + k
                                            P.op("pe", lambda e, a_ps=a_ps, fc=fc, tb=tb, Wd=Wd, k=k: e.matmul(
                                                a_ps[:, :], lhsT=aT[:, fc, tb * 128:(tb + 1) * 128], rhs=Wd[:, k, :],
                                                start=(fc == 0), stop=(fc == NFC - 1)), r=[RaT, RWd], w=[Ra])
                                for tb in range(4):
                                    a_ps, Ra = acc[tb]
                                    evc += 1
                                    if evc % 2 == 0:
                                        P.op("act", lambda e, a_ps=a_ps, tb=tb, nt=nt: e.activation(
                                            out=yt[:, tb, nt * 512:(nt + 1) * 512], in_=a_ps[:, :], func=AF.Copy),
                                            r=[Ra], w=[Ryt[tb]])
                                    else:
                                        P.op("dve", lambda e, a_ps=a_ps, tb=tb, nt=nt: e.tensor_copy(
                                            out=yt[:, tb, nt * 512:(nt + 1) * 512], in_=a_ps[:, :]), r=[Ra], w=[Ryt[tb]])
                            for tb in range(4):
                                rstd, Rs = norm_stats(yt[:, tb, :].rearrange("p (a b) -> p a b", b=512), Ryt[tb],
                                                      aT[:, 0:4, :], RaT, small)
                                P.op("dve", lambda e, tb=tb, rstd=rstd: e.scalar_tensor_tensor(
                                    out=yt[:, tb, :], in0=yt[:, tb, :], scalar=rstd[:, 0:1], in1=gpost2[:],
                                    op0=ALU.mult, op1=ALU.mult), r=[Ryt[tb], Rs, Rg], w=[Ryt[tb]])
                                P.op("pool" if tb % 2 == 0 else "dve", lambda e, tb=tb: e.tensor_tensor(
                                    out=yt[:, tb, :], in0=yt[:, tb, :], in1=x1[:, tb, :], op=ALU.add),
                                    r=[Ryt[tb], Rx1[tb]], w=[Ryt[tb]])
                                orow = q0 - HALO + tb * 128
                                P.dma("sp", out[orow:orow + 128, :], yt[:, tb, :], dout[tb], r=[Ryt[tb], ROA])
            for ti, (q0, W, is_halo) in enumerate(tiles):
                do_tile(ti, q0, W, is_halo)
        P.emit()
    return nc


_CACHE = {}


def _consts():
    ident = np.eye(128, dtype=np.float32)
    s = np.arange(128)[:, None]
    t = np.arange(128)[None, :]
    masktab = np.zeros((128, 512), np.float32)
    masktab[:, 0:128] = np.where(t >= s, 0.0, NEG)
    slopes = (2.0 ** (-8.0 * np.arange(1, NH + 1) / NH)).astype(np.float32)
    al = np.zeros((128, NH, 256), np.float32)
    for h in range(NH):
        dist_prev = (t + 128 - s).astype(np.float32)
        al[:, h, 0:128] = np.where(s > t, -slopes[h] * dist_prev, NEG)
        dist_cur = (t - s).astype(np.float32)
        al[:, h, 128:256] = np.where(t >= s, -slopes[h] * dist_cur, NEG)
    hi = al.astype(ml_dtypes.bfloat16).astype(np.float32)
    lo = (al - hi).astype(ml_dtypes.bfloat16).astype(np.float32)
    return ident, masktab, hi.reshape(128, NH * 256), lo.reshape(128, NH * 256)


def make_in_maps(inputs, T_CTX, T_OWN, n_cores):
    x = np.asarray(inputs["x"], np.float32)
    B, S, _ = x.shape
    ident, masktab, alhi, allo = _consts()
    f32 = lambda a: np.ascontiguousarray(np.asarray(a, np.float32))
    g_grp = np.concatenate([np.asarray(inputs["grp_swa_g"])[0], np.asarray(inputs["grp_fox_g"])[0]])
    conv_w = np.asarray(inputs["conv_w"], np.float32)[0]
    conv_b = np.asarray(inputs["conv_b"], np.float32)[0]
    common = {
        "w_in": f32(inputs["w_in"][0]), "w_out": f32(inputs["w_out"][0]),
        "w_up": f32(inputs["w_up"][0]), "w_down": f32(inputs["w_down"][0]),
        "g_pre": f32(inputs["pre_mix_g"][0]).reshape(1, D), "g_post": f32(inputs["post_mix_g"][0]).reshape(1, D),
        "g_pre2": f32(inputs["pre_ffn_g"][0]).reshape(1, D), "g_post2": f32(inputs["post_ffn_g"][0]).reshape(1, D),
        "g_grpT": f32(g_grp.reshape(16, 128).T),
        "b_forget": f32(inputs["b_forget"][0]).reshape(16, 1),
        "sinks": f32(inputs["sinks"][0]).reshape(1, 16),
        "cwT": f32(conv_w.reshape(3, 2 * NFC, 128).transpose(2, 1, 0).reshape(128, 2 * NFC * 3)),
        "cbT": f32(conv_b.reshape(2 * NFC, 128).T),
        "ident": ident, "masktab": masktab, "alhi": alhi, "allo": allo,
    }
    nhalf = S // T_OWN
    maps = []
    for c in range(n_cores):
        b, half = c // nhalf, c % nhalf
        m = dict(common)
        own = x[b, half * T_OWN:(half + 1) * T_OWN]
        T_ALL = T_CTX + T_OWN
        ctxrow = np.zeros((1, T_ALL), np.float32)
        if T_CTX:
            if half == 0:
                ctx = x[b, 0:T_CTX]
                ctxrow[0, 0:T_CTX] = NEG
            else:
                ctx = x[b, half * T_OWN - T_CTX: half * T_OWN]
            m["xx"] = np.ascontiguousarray(np.concatenate([ctx, own], axis=0))
        else:
            m["xx"] = np.ascontiguousarray(own)
        m["ctxrow"] = ctxrow
        m["flag"] = np.full((128, 1), 0.0 if half == 0 else 1.0, np.float32)
        maps.append(m)
    return maps


T_CTX_CFG = 2048
T_OWN_CFG = 2048


def kernel(**inputs):
    x = np.asarray(inputs["x"])
    B, S, _ = x.shape
    n_cores = B * (S // T_OWN_CFG)
    key = (T_CTX_CFG, T_OWN_CFG)
    if key not in _CACHE:
        _CACHE[key] = build_program(T_CTX_CFG, T_OWN_CFG)
    nc = _CACHE[key]
    maps = make_in_maps(inputs, T_CTX_CFG, T_OWN_CFG, n_cores)
    res = run_bass_kernel_spmd(nc, maps, core_ids=list(range(n_cores)))
    outs = [np.asarray(r["out"], np.float32) for r in res.results]
    nhalf = S // T_OWN_CFG
    full = np.stack([np.concatenate(outs[b * nhalf:(b + 1) * nhalf], axis=0) for b in range(B)], axis=0)
    return full.astype(np.float32)
```

```python
import numpy as np
import ml_dtypes
from contextlib import ExitStack, contextmanager
import concourse.bass as bass
import concourse.mybir as mybir
from concourse.bass_utils import run_bass_kernel_spmd

F32 = mybir.dt.float32
BF16 = mybir.dt.bfloat16
AF = mybir.ActivationFunctionType
ALU = mybir.AluOpType

D = 2048
DIN = 4368
DFF = 5632
NH = 16
HD = 64
EPS = 1e-6
NEG = -30000.0
NFC = DFF // 128

ENGS = ("pe", "act", "dve", "pool", "sp")
CENGS = ("pe", "act", "dve", "pool")


class Res:
    __slots__ = ("last_w", "readers")

    def __init__(self):
        self.last_w = None
        self.readers = {}


class DSem:
    __slots__ = ("sem", "count", "last", "sw")

    def __init__(self, sem, sw):
        self.sem = sem
        self.count = 0
        self.last = None
        self.sw = sw


class Op:
    __slots__ = ("eng", "fn", "deps", "is_dma", "dsem", "dval", "sigval", "need_sig")


class Ring:
    def __init__(self, items):
        self.items = items
        self.i = 0

    def next(self):
        it = self.items[self.i % len(self.items)]
        self.i += 1
        return it


class Prog:
    def __init__(self, nc, root, same_eng_sync=True):
        self.nc = nc
        self.root = root
        self.stack = root
        self.ops = {e: [] for e in ENGS}
        self.same_eng_sync = same_eng_sync
        self.csem = {e: root.enter_context(nc.semaphore("cs_" + e)) for e in CENGS}
        self.all_dsems = []
        self.free_dsems = {False: [], True: []}
        self.scope_dsems = [[]]
        self.pending_bar = {e: None for e in ENGS}
        self.uid = 0

    def sb(self, shape, dt, name=None):
        self.uid += 1
        return self.stack.enter_context(self.nc.sbuf_tensor(name or ("t%d" % self.uid), list(shape), dt))

    def ps(self, shape, dt, name=None):
        self.uid += 1
        return self.stack.enter_context(self.nc.psum_tensor(name or ("p%d" % self.uid), list(shape), dt))

    def dsem(self, sw=False):
        if self.free_dsems[sw]:
            d = self.free_dsems[sw].pop()
        else:
            d = DSem(self.root.enter_context(self.nc.semaphore("ds%d" % len(self.all_dsems))), sw)
            self.all_dsems.append(d)
        self.scope_dsems[-1].append(d)
        return d

    @contextmanager
    def scope(self):
        old = self.stack
        self.scope_dsems.append([])
        with ExitStack() as st:
            self.stack = st
            yield
            self.barrier()
        self.stack = old
        for d in self.scope_dsems.pop():
            self.free_dsems[d.sw].append(d)

    def barrier(self):
        bar = []
        for e in CENGS:
            for o in reversed(self.ops[e]):
                if not o.is_dma:
                    bar.append(o)
                    break
        for d in self.all_dsems:
            if d.last is not None:
                bar.append(d.last)
        for e in ENGS:
            self.pending_bar[e] = bar

    def _deps(self, op, r, w):
        deps = []
        pb = self.pending_bar[op.eng]
        if pb is not None:
            deps.extend(pb)
            self.pending_bar[op.eng] = None
        for res in r:
            if res.last_w is not None:
                deps.append(res.last_w)
        for res in w:
            if res.last_w is not None:
                deps.append(res.last_w)
            deps.extend(res.readers.values())
        key = id(op.dsem) if op.is_dma else op.eng
        for res in r:
            res.readers[key] = op
        for res in w:
            res.last_w = op
            res.readers = {}
        op.deps = deps

    def op(self, eng, fn, r=(), w=()):
        o = Op()
        o.eng = eng
        o.fn = fn
        o.is_dma = False
        o.dsem = None
        o.dval = 0
        o.sigval = 0
        o.need_sig = False
        self._deps(o, r, w)
        self.ops[eng].append(o)
        return o

    def dma(self, q, out, in_, dsem, r=(), w=()):
        assert dsem.sw == (q == "pool"), "semaphore / DMA queue kind mismatch"
        o = Op()
        o.eng = q
        o.fn = (out, in_)
        o.is_dma = True
        o.dsem = dsem
        dsem.count += 1
        o.dval = dsem.count * 16
        dsem.last = o
        o.sigval = 0
        o.need_sig = False
        self._deps(o, r, w)
        self.ops[q].append(o)
        return o

    def _skip(self, d, o):
        return (not d.is_dma) and (not o.is_dma) and d.eng == o.eng and \
            (d.eng == "pe" or not self.same_eng_sync)

    def emit(self):
        nc = self.nc
        self.barrier()
        final = self.pending_bar["sp"]
        for o in final:
            if not o.is_dma:
                o.need_sig = True
        for e in ENGS:
            for o in self.ops[e]:
                for d in o.deps:
                    if d.is_dma or self._skip(d, o):
                        continue
                    d.need_sig = True
        for e in CENGS:
            c = 0
            for o in self.ops[e]:
                if (not o.is_dma) and o.need_sig:
                    c += 1
                    o.sigval = c

        def run_stream(e, eh, extra):
            waited = {}

            def do_waits(deps, o):
                need = {}
                for d in deps:
                    if d.is_dma:
                        key = id(d.dsem)
                        sem = d.dsem.sem
                        val = d.dval
                    else:
                        if o is not None and self._skip(d, o):
                            continue
                        key = d.eng
                        sem = self.csem[d.eng]
                        val = d.sigval
                    if waited.get(key, 0) >= val:
                        continue
                    if key not in need or need[key][1] < val:
                        need[key] = (sem, val)
                for key, (sem, val) in need.items():
                    eh.wait_ge(sem, val)
                    waited[key] = val

            for o in self.ops[e]:
                do_waits(o.deps, o)
                if o.is_dma:
                    out, in_ = o.fn
                    eh.dma_start(out=out, in_=in_).then_inc(o.dsem.sem, 16)
                else:
                    ins = o.fn(eh)
                    if o.need_sig:
                        ins.then_inc(self.csem[e], 1)
            if extra:
                do_waits(extra, None)

        with nc.Block() as block:
            @block.sync
            def _(eh):
                run_stream("sp", eh, final)

            @block.tensor
            def _(eh):
                run_stream("pe", eh, None)

            @block.scalar
            def _(eh):
                run_stream("act", eh, None)

            @block.vector
            def _(eh):
                run_stream("dve", eh, None)

            @block.gpsimd
            def _(eh):
                run_stream("pool", eh, None)


def build_program(T_CTX, T_OWN, debug=False):
    HALO = 128 if T_CTX > 0 else 0
    T_ALL = T_CTX + T_OWN
    NQ = HALO + T_OWN
    NKB = T_ALL // 128
    nc = bass.Bass("TRN2", target_bir_lowering=False)

    def din(name, shape):
        return nc.dram_tensor(name, list(shape), F32, kind="ExternalInput").ap()

    xx = din("xx", [T_ALL, D])
    w_in = din("w_in", [D, DIN])
    w_out = din("w_out", [D, D])
    w_up = din("w_up", [D, 2 * DFF])
    w_down = din("w_down", [DFF, D])
    g_pre = din("g_pre", [1, D])
    g_post = din("g_post", [1, D])
    g_pre2 = din("g_pre2", [1, D])
    g_post2 = din("g_post2", [1, D])
    g_grpT = din("g_grpT", [128, 16])
    b_forget = din("b_forget", [16, 1])
    sinks = din("sinks", [1, 16])
    cwT_d = din("cwT", [128, 2 * NFC * 3])
    cbT_d = din("cbT", [128, 2 * NFC])
    ident_d = din("ident", [128, 128])
    masktab_d = din("masktab", [128, 512])
    alhi_d = din("alhi", [128, 16 * 256])
    allo_d = din("allo", [128, 16 * 256])
    ctxrow_d = din("ctxrow", [1, T_ALL])
    flag_d = din("flag", [128, 1])
    out = nc.dram_tensor("out", [T_OWN, D], F32, kind="ExternalOutput").ap()

    skind = "ExternalOutput" if debug else "Internal"
    qfT = nc.dram_tensor("qfT", [NH, 72, NQ], BF16, kind=skind).ap()
    kfT = nc.dram_tensor("kfT", [NH, 72, T_ALL], BF16, kind=skind).ap()
    vf = nc.dram_tensor("vf", [T_ALL, NH * 128], BF16, kind=skind).ap()
    qaT = nc.dram_tensor("qaT", [NH, 66, NQ], BF16, kind=skind).ap()
    kaT = nc.dram_tensor("kaT", [2, 66, T_ALL], BF16, kind=skind).ap()
    va = nc.dram_tensor("va", [T_ALL, 2 * 128], BF16, kind=skind).ap()
    if debug:
        dbg_oa = nc.dram_tensor("dbg_oa", [NQ // 128 + 4, 128, 16 * 512], F32, kind="ExternalOutput").ap()
        dbg_x1 = nc.dram_tensor("dbg_x1", [NQ, D], F32, kind="ExternalOutput").ap()

    w_in_v = w_in.rearrange("(co ci) n -> ci co n", ci=128)
    w_out_v = w_out.rearrange("(co ci) n -> ci co n", ci=128)
    w_up_v = w_up.rearrange("(co ci) n -> ci co n", ci=128)
    w_down_v = w_down.rearrange("(fo fi) n -> fi fo n", fi=128)

    with ExitStack() as root:
        P = Prog(nc, root)

        identf = P.sb([128, 128], F32)
        identb = P.sb([128, 128], BF16)
        onesf = P.sb([128, 128], F32)
        onesb = P.sb([128, 128], BF16)
        masktab = P.sb([128, 512], BF16)
        ggT = P.sb([128, 16], F32)
        esink = P.sb([128, 16], F32)
        negb = P.sb([16, 1], F32)
        cwT = P.sb([128, 2 * NFC, 3], F32)
        cbT = P.sb([128, 2 * NFC], F32)
        carry = P.sb([128, 2 * NFC, 2], F32)
        flag = P.sb([128, 1], F32)
        Rc = Res()
        Rcarry = Res()
        dc = P.dsem()
        dcs = P.dsem(sw=True)
        P.dma("sp", identf[:], ident_d[:, :], dc, w=[Rc])
        P.dma("pool", identb[:], ident_d[:, :], dcs, w=[Rc])
        P.dma("pool", masktab[:], masktab_d[:, :], dcs, w=[Rc])
        P.dma("sp", ggT[:], g_grpT[:, :], dc, w=[Rc])
        P.dma("sp", esink[:], sinks.broadcast_to([128, 16]), dc, w=[Rc])
        P.dma("sp", negb[:], b_forget[:, :], dc, w=[Rc])
        P.dma("sp", cwT[:], cwT_d.rearrange("p (c k) -> p c k", k=3), dc, w=[Rc])
        P.dma("sp", cbT[:], cbT_d[:, :], dc, w=[Rc])
        P.dma("sp", flag[:], flag_d[:, :], dc, w=[Rc])
        P.op("pool", lambda e: e.memset(onesf[:], 1.0), w=[Rc])
        P.op("pool", lambda e: e.memset(onesb[:], 1.0), w=[Rc])
        P.op("pool", lambda e: e.memset(carry[:], 0.0), w=[Rcarry])
        P.op("act", lambda e: e.activation(out=esink[:], in_=esink[:], func=AF.Exp), r=[Rc], w=[Rc])
        P.op("dve", lambda e: e.tensor_scalar(out=negb[:], in0=negb[:], scalar1=-1.0, scalar2=None, op0=ALU.mult),
             r=[Rc], w=[Rc])
        P.barrier()

        def norm_stats(src_ap, Rsrc, junk, Rjunk, small):
            ss, sd, rstd, Rs = small.next()
            P.op("act", lambda e: e.activation(out=junk[:], in_=src_ap, func=AF.Square, accum_out=ss[:]),
                 r=[Rsrc], w=[Rjunk, Rs])
            P.op("act", lambda e: e.activation(out=sd[:], in_=ss[:], func=AF.Sqrt, scale=1.0 / D, bias=EPS),
                 r=[Rs], w=[Rs])
            P.op("dve", lambda e: e.reciprocal(out=rstd[:], in_=sd[:]), r=[Rs], w=[Rs])
            return rstd, Rs

        def make_small_ring(n):
            items = []
            for _ in range(n):
                items.append((P.sb([128, 1], F32), P.sb([128, 1], F32), P.sb([128, 1], F32), Res()))
            return Ring(items)

        def transpose_block(hb, Rhb, dstT, RdstT, col0, tpring, evi, evac=None):
            for half in range(2):
                tp, Rtp = tpring.next()
                for k in range(8):
                    c = half * 8 + k
                    P.op("pe", lambda e, tp=tp, k=k, c=c: e.transpose(
                        out=tp[:, k, :], in_=hb[:, c * 128:(c + 1) * 128], identity=identb[:]),
                        r=[Rhb, Rc], w=[Rtp])
                eng = evac or ("dve" if (evi + half) % 2 == 0 else "act")
                if eng == "dve":
                    P.op("dve", lambda e, tp=tp, half=half: e.tensor_copy(
                        out=dstT[:, half * 8:(half + 1) * 8, col0:col0 + 128], in_=tp[:]),
                        r=[Rtp], w=[RdstT])
                else:
                    P.op("act", lambda e, tp=tp, half=half: e.activation(
                        out=dstT[:, half * 8:(half + 1) * 8, col0:col0 + 128], in_=tp[:], func=AF.Copy),
                        r=[Rtp], w=[RdstT])

        with P.scope():
            spT = P.sb([16, T_ALL], F32)
            RspT = Res()
            passes = []
            if T_CTX:
                passes.append((0, T_CTX, True))
            passes.append((T_CTX, T_OWN, False))
            TP = max(T_CTX, T_OWN)
            with P.scope():
                hT = P.sb([128, 16, TP], BF16)
                RhT = Res()
                gpre = P.sb([128, D], F32)
                Rg = Res()
                dg = P.dsem()
                P.dma("sp", gpre[:], g_pre.broadcast_to([128, D]), dg, w=[Rg])
                junk = P.sb([128, D], BF16)
                Rjunk = Res()
                small = make_small_ring(3)
                xring = Ring([(P.sb([128, D], F32), Res(), P.dsem()) for _ in range(4)])
                hbring = Ring([(P.sb([128, D], BF16), Res()) for _ in range(2)])
                tpring = Ring([(P.ps([128, 8, 128], BF16), Res()) for _ in range(2)])
                mmring = Ring([(P.ps([128, 512], F32), Res()) for _ in range(4)])
                wring = Ring([(P.sb([128, 16, 256], BF16), Res(), P.dsem(sw=True)) for _ in range(3)])
                stgring = Ring([(P.sb([128, 512], BF16), Res(), P.dsem()) for _ in range(4)])
                vstring = Ring([(P.sb([128, 4, 128], BF16), Res(), P.dsem()) for _ in range(3)])
                etring = Ring([(P.sb([16, 512], F32), Res()) for _ in range(2)])
                for (vst, Rv, _d) in vstring.items:
                    P.op("pool", lambda e, vst=vst: e.memset(vst[:], 1.0), w=[Rv])
                evc = [0]

                for (tok0, ntok, is_ctx) in passes:
                    pend = None
                    for tb in range(ntok // 128):
                        xs, Rxs, dxs = xring.next()
                        P.dma("sp", xs[:], xx[tok0 + tb * 128: tok0 + (tb + 1) * 128, :], dxs, w=[Rxs])
                        rstd, Rs = norm_stats(xs[:], Rxs, junk, Rjunk, small)
                        hb, Rhb = hbring.next()
                        P.op("dve", lambda e, hb=hb, xs=xs, rstd=rstd: e.scalar_tensor_tensor(
                            out=hb[:], in0=xs[:], scalar=rstd[:, 0:1], in1=gpre[:], op0=ALU.mult, op1=ALU.mult),
                            r=[Rxs, Rs, Rg], w=[Rhb])
                        if pend is not None:
                            transpose_block(*pend)
                        pend = (hb, Rhb, hT, RhT, tb * 128, tpring, tb)
                    transpose_block(*pend)

                    full_tiles = [(t0, 512) for t0 in range(0, ntok, 512)]
                    halo_tiles = [(ntok - 128, 128)]

                    def qcol(t0):
                        return (t0 - (ntok - 128)) if is_ctx else (HALO + t0)

                    def fm(Wt, RW, col_lo, tiles, scale, dest_fn):
                        for (t0, tw) in tiles:
                            ps, Rps = mmring.next()
                            for c in range(16):
                                P.op("pe", lambda e, ps=ps, c=c, t0=t0, tw=tw: e.matmul(
                                    ps[:, 0:tw], lhsT=Wt[:, c, col_lo:col_lo + 128], rhs=hT[:, c, t0:t0 + tw],
                                    start=(c == 0), stop=(c == 15)), r=[RW, RhT], w=[Rps])
                            stg, Rstg, dstg = stgring.next()
                            evc[0] += 1
                            if evc[0] % 2 == 0:
                                P.op("act", lambda e, ps=ps, stg=stg, tw=tw: e.activation(
                                    out=stg[:, 0:tw], in_=ps[:, 0:tw], func=AF.Copy, scale=scale),
                                    r=[Rps], w=[Rstg])
                            else:
                                P.op("dve", lambda e, ps=ps, stg=stg, tw=tw: e.tensor_scalar(
                                    out=stg[:, 0:tw], in0=ps[:, 0:tw], scalar1=scale, scalar2=None, op0=ALU.mult),
                                    r=[Rps], w=[Rstg])
                            for hh in range(2):
                                P.dma("sp", dest_fn(hh, t0, tw), stg[hh * 64:(hh + 1) * 64, 0:tw], dstg, r=[Rstg])

                    def tm(Wt, RW, col_lo, nheads, dst, dcol0):
                        ncols = nheads * 64
                        for tb in range(ntok // 128):
                            ps, Rps = mmring.next()
                            for c in range(16):
                                P.op("pe", lambda e, ps=ps, c=c, tb=tb: e.matmul(
                                    ps[:, 0:ncols], lhsT=hT[:, c, tb * 128:(tb + 1) * 128],
                                    rhs=Wt[:, c, col_lo:col_lo + ncols], start=(c == 0), stop=(c == 15)),
                                    r=[RW, RhT], w=[Rps])
                            vst, Rvst, dvst = vstring.next()
                            evc[0] += 1
                            src = ps[:, 0:ncols].rearrange("p (h d) -> p h d", d=64)
                            if evc[0] % 2 == 0:
                                P.op("act", lambda e, vst=vst, src=src: e.activation(
                                    out=vst[:, 0:nheads, 0:64], in_=src, func=AF.Copy), r=[Rps], w=[Rvst])
                            else:
                                P.op("dve", lambda e, vst=vst, src=src: e.tensor_copy(
                                    out=vst[:, 0:nheads, 0:64], in_=src), r=[Rps], w=[Rvst])
                            r0 = tok0 + tb * 128
                            P.dma("sp", dst[r0:r0 + 128, dcol0:dcol0 + nheads * 128].rearrange("p (h d) -> p h d", d=128),
                                  vst[:, 0:nheads, :], dvst, r=[Rvst])

                    for ch in range(18):
                        if is_ctx and False:
                            continue
                        Wt, RW, dW = wring.next()
                        ncol = 256 if ch < 17 else 16
                        P.dma("pool", Wt[:, :, 0:ncol], w_in_v[:, :, ch * 256: ch * 256 + ncol], dW, w=[RW])
                        if ch < 4 or 5 <= ch <= 8:
                            dstT = qaT if ch < 4 else qfT
                            h0 = (ch if ch < 4 else ch - 5) * 4
                            tiles = halo_tiles if is_ctx else full_tiles
                            for sub in range(2):
                                fm(Wt, RW, sub * 128, tiles, 0.125,
                                   lambda hh, t0, tw, h0=h0, sub=sub, dstT=dstT:
                                   dstT[h0 + 2 * sub + hh, 0:64, qcol(t0):qcol(t0) + tw])
                        elif ch == 4:
                            fm(Wt, RW, 0, full_tiles, 1.0,
                               lambda hh, t0, tw: kaT[hh, 0:64, tok0 + t0: tok0 + t0 + tw])
                            tm(Wt, RW, 128, 2, va, 0)
                        elif 9 <= ch <= 12:
                            h0 = (ch - 9) * 4
                            for sub in range(2):
                                fm(Wt, RW, sub * 128, full_tiles, 1.0,
                                   lambda hh, t0, tw, h0=h0, sub=sub:
                                   kfT[h0 + 2 * sub + hh, 0:64, tok0 + t0: tok0 + t0 + tw])
                        elif 13 <= ch <= 16:
                            tm(Wt, RW, 0, 4, vf, (ch - 13) * 4 * 128)
                        else:
                            for (t0, tw) in full_tiles:
                                ps, Rps = mmring.next()
                                for c in range(16):
                                    P.op("pe", lambda e, ps=ps, c=c, t0=t0, tw=tw, Wt=Wt: e.matmul(
                                        ps[0:16, 0:tw], lhsT=Wt[:, c, 0:16], rhs=hT[:, c, t0:t0 + tw],
                                        start=(c == 0), stop=(c == 15)), r=[RW, RhT], w=[Rps])
                                et, Ret = etring.next()
                                P.op("act", lambda e, ps=ps, et=et, tw=tw: e.activation(
                                    out=et[:, 0:tw], in_=ps[0:16, 0:tw], func=AF.Exp, scale=-1.0, bias=negb[:, 0:1]),
                                    r=[Rps, Rc], w=[Ret])
                                P.op("act", lambda e, et=et, t0=t0, tw=tw, tok0=tok0: e.activation(
                                    out=spT[:, tok0 + t0: tok0 + t0 + tw], in_=et[:, 0:tw], func=AF.Ln, bias=1.0),
                                    r=[Ret], w=[RspT])

            with P.scope():
                zeros = P.sb([16, T_ALL], F32)
                cs = P.sb([16, T_ALL], F32)
                r1 = P.sb([16, T_ALL], F32)
                hi = P.sb([16, T_ALL], BF16)
                mid = P.sb([16, T_ALL], BF16)
                lo = P.sb([16, T_ALL], BF16)
                nq3 = P.sb([16, 3, NQ], BF16)
                ones = P.sb([16, T_ALL], BF16)
                row70 = P.sb([16, NQ], BF16)
                ctxb = P.sb([16, T_ALL], BF16)
                Rz = Res()
                dz = P.dsem(sw=True)
                dz2 = P.dsem()
                P.op("pool", lambda e: e.memset(zeros[:], 0.0), w=[Rz])
                P.op("pool", lambda e: e.memset(ones[:], 1.0), w=[Rz])
                P.op("pool", lambda e: e.memset(row70[:], 1.0), w=[Rz])
                if HALO:
                    P.op("pool", lambda e: e.memset(row70[:, 0:HALO], 0.0), w=[Rz])
                P.dma("pool", ctxb[:], ctxrow_d.broadcast_to([16, T_ALL]), dz, w=[Rz])
                V_ = "dve"
                P.op(V_, lambda e: e.tensor_tensor_scan(out=cs[:], data0=spT[:], data1=zeros[:], initial=0.0,
                                                        op0=ALU.add, op1=ALU.add), r=[RspT, Rz], w=[Rz])
                P.op(V_, lambda e: e.tensor_copy(out=hi[:], in_=cs[:]), r=[Rz], w=[Rz])
                P.op(V_, lambda e: e.tensor_tensor(out=r1[:], in0=cs[:], in1=hi[:], op=ALU.subtract), r=[Rz], w=[Rz])
                P.op(V_, lambda e: e.tensor_copy(out=mid[:], in_=r1[:]), r=[Rz], w=[Rz])
                P.op(V_, lambda e: e.tensor_tensor(out=cs[:], in0=r1[:], in1=mid[:], op=ALU.subtract), r=[Rz], w=[Rz])
                P.op(V_, lambda e: e.tensor_copy(out=lo[:], in_=cs[:]), r=[Rz], w=[Rz])
                qs = T_CTX - HALO
                for j, src in enumerate((hi, mid, lo)):
                    P.op(V_, lambda e, j=j, src=src: e.tensor_scalar(
                        out=nq3[:, j, :], in0=src[:, qs:T_ALL], scalar1=-1.0, scalar2=None, op0=ALU.mult),
                        r=[Rz], w=[Rz])
                P.dma("sp", qfT[:, 64:67, :], nq3[:], dz2, r=[Rz])
                for j, src in enumerate((hi, mid, lo)):
                    P.dma("sp", kfT[:, 67 + j, :], src[:], dz2, r=[Rz])
                for j in range(3):
                    P.dma("sp", qfT[:, 67 + j, :], ones[:, 0:NQ], dz2, r=[Rz])
                    P.dma("sp", kfT[:, 64 + j, :], ones[:], dz2, r=[Rz])
                P.dma("sp", qfT[:, 70, :], row70[:], dz2, r=[Rz])
                P.dma("sp", qaT[:, 64, :], row70[:], dz2, r=[Rz])
                P.dma("sp", kfT[:, 70, :], ctxb[:], dz2, r=[Rz])
                P.dma("sp", kaT[:, 64, :], ctxb[0:2, :], dz2, r=[Rz])
                zb = P.sb([16, T_ALL], BF16)
                P.op("pool", lambda e: e.memset(zb[:], 0.0), w=[Rz])
                P.dma("sp", qfT[:, 71, :], zb[:, 0:NQ], dz2, r=[Rz])
                P.dma("sp", kfT[:, 71, :], zb[:], dz2, r=[Rz])
                P.dma("sp", qaT[:, 65, :], zb[:, 0:NQ], dz2, r=[Rz])
                P.dma("sp", kaT[:, 65, :], zb[0:2, :], dz2, r=[Rz])

        tiles = []
        if HALO:
            tiles.append((0, 128, True))
        for i in range(T_OWN // 512):
            tiles.append((HALO + i * 512, 512, False))

        with P.scope():
            h2T_halo = P.sb([128, 16, 128], BF16)
            Rh2h = Res()
            have_halo = [False]

            def do_tile(ti, q0, W, is_halo):
                ts = T_CTX - HALO + q0
                kb0 = ts // 128
                nqb = W // 128
                nkb = kb0 + nqb
                with P.scope():
                    OA = P.sb([128, 16, 512], F32)
                    ROA = Res()

                    with P.scope():
                        sring = Ring([(P.ps([128, 512], F32), Res()) for _ in range(3)])
                        oring = Ring([(P.ps([128, 512], F32), Res()) for _ in range(2)])
                        oring_swa = Ring([(P.ps([128, 512], F32), Res()) for _ in range(1)])
                        sring_swa = Ring([(P.ps([128, 512], F32), Res()) for _ in range(2)])
                        ptring_swa = Ring([(P.sb([128, 512], BF16), Res()) for _ in range(2)])
                        ptring = Ring([(P.sb([128, 512], BF16), Res()) for _ in range(6)])
                        ktring = Ring([(P.sb([72, T_ALL], BF16), Res(), P.dsem()) for _ in range(2)])
                        qtring = Ring([(P.sb([72, 512], BF16), Res(), P.dsem()) for _ in range(3)])
                        vgring = Ring([(P.sb([128, NKB, 512], BF16), (Res(), Res()), (P.dsem(), P.dsem()))
                                       for _ in range(2)])
                        rdring = Ring([(P.sb([64, 512], F32), Res()) for _ in range(2)])
                        alhi = P.sb([128, 16, 256], BF16)
                        allo = P.sb([128, 16, 256], BF16)
                        kbase = max(kb0 - 1, 0)
                        nka = nkb - kbase
                        KaT = P.sb([66, 2, nka * 128], BF16)
                        VA = P.sb([128, nka, 256], BF16)
                        Rtab = Res()
                        dtab = P.dsem()
                        dtabs = P.dsem(sw=True)
                        P.dma("sp", KaT[:], kaT.rearrange("k r t -> r k t")[:, :, kbase * 128: nkb * 128], dtab, w=[Rtab])
                        P.dma("sp", VA[:], va[kbase * 128: nkb * 128, :].rearrange("(kb p) c -> p kb c", p=128),
                              dtab, w=[Rtab])

                        def finalize(h, O, RO, ch, pb, is_swa):
                            rd, Rrd = rdring.next()
                            if is_swa:
                                P.op("dve", lambda e: e.tensor_scalar(
                                    out=rd[:, 0:W], in0=O[64:128, 0:W], scalar1=esink[64:128, h:h + 1],
                                    scalar2=None, op0=ALU.add), r=[RO, Rc], w=[Rrd])
                                P.op("dve", lambda e: e.reciprocal(out=rd[:, 0:W], in_=rd[:, 0:W]),
                                     r=[Rrd], w=[Rrd])
                            else:
                                P.op("dve", lambda e: e.reciprocal(out=rd[:, 0:W], in_=O[64:128, 0:W]),
                                     r=[RO], w=[Rrd])
                            P.op("dve", lambda e: e.tensor_tensor(
                                out=OA[pb:pb + 64, ch, 0:W], in0=O[0:64, 0:W], in1=rd[:, 0:W], op=ALU.mult),
                                r=[RO, Rrd], w=[ROA])

                        def swa_head(h):
                            kv = h // 8
                            qt, Rqt, dqt = qtring.next()
                            P.dma("sp", qt[0:66, 0:W], qaT[h, :, q0:q0 + W], dqt, w=[Rqt])
                            O, RO = oring_swa.next()
                            pvs = []
                            for n0 in range(0, nqb, 2):
                                S, RS = sring_swa.next()
                                PT, RPT = ptring_swa.next()
                                ns = [n for n in (n0, n0 + 1) if n < nqb]
                                lo_c = None
                                for n in ns:
                                    cur = kb0 + n
                                    prev = cur - 1
                                    base = (n % 2) * 256
                                    a0 = 0 if prev >= 0 else 128
                                    if lo_c is None:
                                        lo_c = base + a0
                                    P.op("pe", lambda e, base=base, a0=a0, S=S: e.matmul(
                                        S[:, base + a0:base + 256], lhsT=identb[:], rhs=alhi[:, h, a0:256],
                                        start=True, stop=False), r=[Rc, Rtab], w=[RS])
                                    P.op("pe", lambda e, base=base, a0=a0, S=S: e.matmul(
                                        S[:, base + a0:base + 256], lhsT=identb[:], rhs=allo[:, h, a0:256],
                                        start=False, stop=False), r=[Rc, Rtab], w=[RS])
                                    if prev >= 0:
                                        P.op("pe", lambda e, base=base, prev=prev, n=n, S=S: e.matmul(
                                            S[:, base:base + 128],
                                            lhsT=KaT[0:65, kv, (prev - kbase) * 128:(prev - kbase + 1) * 128],
                                            rhs=qt[0:65, n * 128:(n + 1) * 128], start=False, stop=False),
                                            r=[Rtab, Rqt], w=[RS])
                                    P.op("pe", lambda e, base=base, cur=cur, n=n, S=S: e.matmul(
                                        S[:, base + 128:base + 256],
                                        lhsT=KaT[0:65, kv, (cur - kbase) * 128:(cur - kbase + 1) * 128],
                                        rhs=qt[0:65, n * 128:(n + 1) * 128], start=False, stop=True),
                                        r=[Rtab, Rqt], w=[RS])
                                hi_c = (ns[-1] % 2) * 256 + 256
                                P.op("act", lambda e, lo_c=lo_c, hi_c=hi_c, S=S, PT=PT: e.activation(
                                    out=PT[:, lo_c:hi_c], in_=S[:, lo_c:hi_c], func=AF.Exp), r=[RS], w=[RPT])
                                pvs.append((ns, PT, RPT))

                            def part_b():
                                for (ns, PT, RPT) in pvs:
                                    for n in ns:
                                        cur = kb0 + n
                                        prev = cur - 1
                                        base = (n % 2) * 256
                                        if prev >= 0:
                                            P.op("pe", lambda e, base=base, prev=prev, n=n, PT=PT: e.matmul(
                                                O[:, n * 128:(n + 1) * 128],
                                                lhsT=VA[:, prev - kbase, kv * 128:(kv + 1) * 128],
                                                rhs=PT[:, base:base + 128], start=True, stop=False),
                                                r=[Rtab, RPT], w=[RO])
                                        P.op("pe", lambda e, base=base, cur=cur, n=n, PT=PT, prev=prev: e.matmul(
                                            O[:, n * 128:(n + 1) * 128],
                                            lhsT=VA[:, cur - kbase, kv * 128:(kv + 1) * 128],
                                            rhs=PT[:, base + 128:base + 256], start=(prev < 0), stop=True),
                                            r=[Rtab, RPT], w=[RO])
                                finalize(h, O, RO, h // 2, (h % 2) * 64, True)
                            return part_b
                        units = [(h, j) for h in range(NH) for j in range(nkb)]
                        state = {}
                        headres = {}
                        deferred = []
                        swa_b = {}
                        LOOK = 3

                        vhalf = (nkb + 1) // 2

                        def head_setup(h):
                            kt, Rkt, dkt = ktring.next()
                            P.dma("sp", kt[:, 0:nkb * 128], kfT[h, :, 0:nkb * 128], dkt, w=[Rkt])
                            qt, Rqt, dqt = qtring.next()
                            P.dma("sp", qt[:, 0:W], qfT[h, :, q0:q0 + W], dqt, w=[Rqt])
                            if h % 4 == 0:
                                vg, Rvg, dvg = vgring.next()
                                g = h // 4
                                for k, (a, b) in enumerate(((0, vhalf), (vhalf, nkb))):
                                    if b > a:
                                        P.dma("sp", vg[:, a:b, :],
                                              vf[a * 128:b * 128, g * 512:(g + 1) * 512].rearrange(
                                                  "(kb p) c -> p kb c", p=128), dvg[k], w=[Rvg[k]])
                                headres["vg"] = (vg, Rvg)
                                if h == 0:
                                    P.dma("pool", alhi[:], alhi_d.rearrange("p (h t) -> p h t", t=256), dtabs,
                                          r=[Rvg[0]], w=[Rtab])
                                    P.dma("pool", allo[:], allo_d.rearrange("p (h t) -> p h t", t=256), dtabs,
                                          r=[Rvg[0]], w=[Rtab])
                            O, RO = oring.next()
                            headres[h] = (kt, Rkt, qt, Rqt, O, RO) + headres["vg"]

                        def emit_S(h, j):
                            if j == 0:
                                head_setup(h)
                            kt, Rkt, qt, Rqt, O, RO, vg, Rvg = headres[h]
                            o = j - kb0
                            c0 = max(o, 0) * 128
                            N = W - c0
                            S, RS = sring.next()
                            if o >= 0:
                                P.op("pe", lambda e: e.matmul(S[:, c0:W], lhsT=identb[:], rhs=masktab[:, 0:N],
                                                              start=True, stop=False), r=[Rc], w=[RS])
                                P.op("pe", lambda e: e.matmul(S[:, c0:W], lhsT=kt[0:71, j * 128:(j + 1) * 128],
                                                              rhs=qt[0:71, c0:W], start=False, stop=True),
                                     r=[Rkt, Rqt], w=[RS])
                            else:
                                P.op("pe", lambda e: e.matmul(S[:, 0:W], lhsT=kt[0:71, j * 128:(j + 1) * 128],
                                                              rhs=qt[0:71, 0:W], start=True, stop=True),
                                     r=[Rkt, Rqt], w=[RS])
                            PT, RPT = ptring.next()
                            P.op("act", lambda e: e.activation(out=PT[:, c0:W], in_=S[:, c0:W], func=AF.Exp),
                                 r=[RS], w=[RPT])
                            state[(h, j)] = (PT, RPT, c0)

                        def emit_PV(h, j):
                            kt, Rkt, qt, Rqt, O, RO, vg, Rvg = headres[h]
                            PT, RPT, c0 = state.pop((h, j))
                            hl = h % 4
                            P.op("pe", lambda e: e.matmul(O[:, c0:W], lhsT=vg[:, j, hl * 128:(hl + 1) * 128],
                                                          rhs=PT[:, c0:W], start=(j == 0), stop=(j == nkb - 1)),
                                 r=[Rvg[0 if j < vhalf else 1], RPT], w=[RO])
                            if j == nkb - 1:
                                finalize(h, O, RO, 8 + h // 2, (h % 2) * 64, False)

                        def run_deferred(force=False):
                            for d in list(deferred):
                                d[0] -= 1
                                if d[0] <= 0 or force:
                                    d[1]()
                                    deferred.remove(d)

                        for i in range(len(units) + LOOK):
                            if i < len(units):
                                emit_S(*units[i])
                                if units[i][1] == nkb // 2:
                                    swa_b[units[i][0]] = swa_head(units[i][0])
                                if units[i][1] == min(nkb // 2 + 3, nkb - 1):
                                    swa_b.pop(units[i][0])()
                            if i - LOOK >= 0:
                                emit_PV(*units[i - LOOK])
                            run_deferred()
                        run_deferred(force=True)


                    if debug:
                        dd = P.dsem()
                        P.dma("sp", dbg_oa[ti].rearrange("p (c t) -> p c t", t=512), OA[:], dd, r=[ROA])

                    x1 = P.sb([128, 4, D], F32)
                    Rx1 = [Res() for _ in range(4)]
                    h2T = P.sb([128, 16, 512], BF16)
                    Rh2T = Res()
                    wu0 = None
                    if not is_halo:
                        wu0 = (P.sb([128, 16, 512], BF16), Res(), P.dsem(sw=True))
                    with P.scope():
                      gpost = P.sb([128, D], F32)
                      gpre2 = P.sb([128, D], F32)
                      Rg = Res()
                      dg = P.dsem()
                      P.dma("sp", gpost[:], g_post.broadcast_to([128, D]), dg, w=[Rg])
                      P.dma("sp", gpre2[:], g_pre2.broadcast_to([128, D]), dg, w=[Rg])
                      xring = Ring([(P.sb([128, D], F32), Res(), P.dsem()) for _ in range(2)])
                      with P.scope():
                        sqs = [(P.sb([128, 8, 512], BF16), Res()) for _ in range(2)]
                        onT = P.sb([128, 16, 512], BF16)
                        RonT = Res()
                        rsbs = [(P.sb([128, 512], F32), Res()) for _ in range(2)]
                        ssbs = [(P.ps([128, 512], F32), Res()) for _ in range(2)]
                        for g in range(2):
                            sq, Rsq = sqs[g]
                            P.op("act", lambda e, g=g, sq=sq: e.activation(
                                out=sq[:, :, 0:W], in_=OA[:, g * 8:(g + 1) * 8, 0:W], func=AF.Square),
                                r=[ROA], w=[Rsq])
                        for g in range(2):
                            sq, Rsq = sqs[g]
                            ssb, Rssb = ssbs[g]
                            for k in range(8):
                                P.op("pe", lambda e, k=k, sq=sq, ssb=ssb: e.matmul(
                                    ssb[:, 0:W], lhsT=onesb[:], rhs=sq[:, k, 0:W], start=(k == 0), stop=(k == 7)),
                                    r=[Rsq, Rc], w=[Rssb])
                        for g in range(2):
                            ssb, Rssb = ssbs[g]
                            rsb, Rrsb = rsbs[g]
                            P.op("act", lambda e, ssb=ssb, rsb=rsb: e.activation(
                                out=rsb[:, 0:W], in_=ssb[:, 0:W], func=AF.Sqrt, scale=1.0 / 1024, bias=EPS),
                                r=[Rssb], w=[Rrsb])
                            P.op("dve", lambda e, rsb=rsb: e.reciprocal(out=rsb[:, 0:W], in_=rsb[:, 0:W]),
                                 r=[Rrsb], w=[Rrsb])
                            for k in range(8):
                                c = g * 8 + k
                                P.op("dve", lambda e, c=c, rsb=rsb: e.scalar_tensor_tensor(
                                    out=onT[:, c, 0:W], in0=OA[:, c, 0:W], scalar=ggT[:, c:c + 1], in1=rsb[:, 0:W],
                                    op0=ALU.mult, op1=ALU.mult), r=[ROA, Rrsb, Rc], w=[RonT])
                        woring = Ring([(P.sb([128, 16, 512], BF16), Res(), P.dsem(sw=True)) for _ in range(2)])
                        mmring = Ring([(P.ps([128, 512], F32), Res()) for _ in range(3)])
                        evc = 0
                        for nt in range(4):
                            Wo, RWo, dWo = woring.next()
                            P.dma("pool", Wo[:], w_out_v[:, :, nt * 512:(nt + 1) * 512], dWo, w=[RWo])
                            for tb in range(nqb):
                                ps, Rps = mmring.next()
                                for c in range(16):
                                    P.op("pe", lambda e, ps=ps, c=c, tb=tb, Wo=Wo: e.matmul(
                                        ps[:, :], lhsT=onT[:, c, tb * 128:(tb + 1) * 128], rhs=Wo[:, c, :],
                                        start=(c == 0), stop=(c == 15)), r=[RonT, RWo], w=[Rps])
                                evc += 1
                                if evc % 2 == 0:
                                    P.op("act", lambda e, ps=ps, tb=tb, nt=nt: e.activation(
                                        out=x1[:, tb, nt * 512:(nt + 1) * 512], in_=ps[:, :], func=AF.Copy),
                                        r=[Rps], w=[Rx1[tb]])
                                else:
                                    P.op("dve", lambda e, ps=ps, tb=tb, nt=nt: e.tensor_copy(
                                        out=x1[:, tb, nt * 512:(nt + 1) * 512], in_=ps[:, :]), r=[Rps], w=[Rx1[tb]])
                      if True:
                        if wu0 is not None:
                            P.dma("pool", wu0[0][:, :, 0:256], w_up_v[:, :, 0:256], wu0[2], w=[wu0[1]])
                            P.dma("pool", wu0[0][:, :, 256:512], w_up_v[:, :, DFF:DFF + 256], wu0[2], w=[wu0[1]])
                        junk = P.sb([128, D], BF16)
                        Rjunk = Res()
                        junk2 = P.sb([128, D], BF16)
                        Rjunk2 = Res()
                        small = make_small_ring(8)
                        hbring = Ring([(P.sb([128, D], BF16), Res()) for _ in range(2)])
                        tpring = Ring([(P.ps([128, 8, 128], BF16), Res()) for _ in range(2)])
                        dst_h2T, Rdst = (h2T_halo, Rh2h) if is_halo else (h2T, Rh2T)
                        hbs = {}

                        def stA(tb):
                            xs, Rxs, dxs = xring.next()
                            P.dma("sp", xs[:], xx[ts + tb * 128: ts + (tb + 1) * 128, :], dxs, w=[Rxs])
                            rstd, Rs = norm_stats(x1[:, tb, :], Rx1[tb], junk, Rjunk, small)
                            P.op("dve", lambda e: e.scalar_tensor_tensor(
                                out=x1[:, tb, :], in0=x1[:, tb, :], scalar=rstd[:, 0:1], in1=gpost[:],
                                op0=ALU.mult, op1=ALU.mult), r=[Rx1[tb], Rs, Rg], w=[Rx1[tb]])
                            P.op("pool", lambda e: e.tensor_tensor(
                                out=x1[:, tb, :], in0=x1[:, tb, :], in1=xs[:], op=ALU.add),
                                r=[Rx1[tb], Rxs], w=[Rx1[tb]])
                            if debug:
                                dd = P.dsem()
                                P.dma("sp", dbg_x1[q0 + tb * 128: q0 + (tb + 1) * 128, :], x1[:, tb, :], dd, r=[Rx1[tb]])

                        def stB(tb):
                            rstd2, Rs2 = norm_stats(x1[:, tb, :], Rx1[tb], junk2, Rjunk2, small)
                            hb, Rhb = hbring.next()
                            hbs[tb] = (hb, Rhb)
                            P.op("dve", lambda e: e.scalar_tensor_tensor(
                                out=hb[:], in0=x1[:, tb, :], scalar=rstd2[:, 0:1], in1=gpre2[:],
                                op0=ALU.mult, op1=ALU.mult), r=[Rx1[tb], Rs2, Rg], w=[Rhb])

                        def stC(tb):
                            hb, Rhb = hbs.pop(tb)
                            transpose_block(hb, Rhb, dst_h2T, Rdst, tb * 128, tpring, 0, evac="dve")

                        for step in range(nqb + 3):
                            for fn, lag in ((stA, 0), (stB, 2), (stC, 3)):
                                if 0 <= step - lag < nqb:
                                    fn(step - lag)
                    if is_halo:
                        have_halo[0] = True
                        return

                    with P.scope():
                        aT = P.sb([128, NFC, 512], BF16)
                        RaT = Res()
                        use_halo = have_halo[0]
                        have_halo[0] = False
                        with P.scope():
                            wuring = Ring([wu0] + [(P.sb([128, 16, 512], BF16), Res(), P.dsem(sw=True))
                                                   for _ in range(2)])
                            wuring.next()
                            wu_slots = {0: (wu0[0], wu0[1])}
                            upring = Ring([(P.ps([128, 512], F32), Res()) for _ in range(4)])
                            phring = Ring([(P.ps([128, 2], F32), Res()) for _ in range(2)])
                            uering = Ring([(P.sb([128, 514], F32), Res()) for _ in range(3)])
                            ybring = Ring([(P.sb([128, 512], F32), Res()) for _ in range(4)])
                            glring = Ring([(P.sb([128, 512], F32), Res()) for _ in range(2)])
                            for fc in range(NFC):
                                if fc % 2 == 0:
                                    for g in ([1, 2] if fc == 0 else [fc // 2 + 2]):
                                        if g < NFC // 2:
                                            Wn, RWn, dWn = wuring.next()
                                            wu_slots[g] = (Wn, RWn)
                                            P.dma("pool", Wn[:, :, 0:256], w_up_v[:, :, g * 256:(g + 1) * 256],
                                                  dWn, w=[RWn])
                                            P.dma("pool", Wn[:, :, 256:512],
                                                  w_up_v[:, :, DFF + g * 256: DFF + (g + 1) * 256], dWn, w=[RWn])
                                    Wu, RWu = wu_slots.pop(fc // 2)
                                wo = (fc % 2) * 128
                                ys = []
                                for part in range(2):
                                    idx = part * NFC + fc
                                    ue, Rue = uering.next()
                                    if use_halo:
                                        ph, Rph = phring.next()
                                        for c in range(16):
                                            P.op("pe", lambda e, ph=ph, c=c, Wu=Wu, part=part, wo=wo: e.matmul(
                                                ph[:, :], lhsT=Wu[:, c, part * 256 + wo:part * 256 + wo + 128],
                                                rhs=h2T_halo[:, c, 126:128], start=(c == 0), stop=(c == 15)),
                                                r=[RWu, Rh2h], w=[Rph])
                                        P.op("dve", lambda e, ph=ph, ue=ue: e.tensor_scalar(
                                            out=ue[:, 0:2], in0=ph[:, 0:2], scalar1=flag[:, 0:1], scalar2=None,
                                            op0=ALU.mult), r=[Rph, Rc], w=[Rue])
                                    else:
                                        P.op("act", lambda e, ue=ue, idx=idx: e.activation(
                                            out=ue[:, 0:2], in_=carry[:, idx, :], func=AF.Copy), r=[Rcarry], w=[Rue])
                                    ps, Rps = upring.next()
                                    for c in range(16):
                                        P.op("pe", lambda e, ps=ps, c=c, Wu=Wu, part=part, wo=wo: e.matmul(
                                            ps[:, :], lhsT=Wu[:, c, part * 256 + wo:part * 256 + wo + 128], rhs=h2T[:, c, :],
                                            start=(c == 0), stop=(c == 15)), r=[RWu, Rh2T], w=[Rps])
                                    P.op("act", lambda e, ps=ps, ue=ue: e.activation(
                                        out=ue[:, 2:514], in_=ps[:, :], func=AF.Copy), r=[Rps], w=[Rue])
                                    P.op("act", lambda e, ue=ue, idx=idx: e.activation(
                                        out=carry[:, idx, :], in_=ue[:, 512:514], func=AF.Copy), r=[Rue], w=[Rcarry])
                                    yb, Ryb = ybring.next()
                                    P.op("act", lambda e, ps=ps, yb=yb, idx=idx: e.activation(
                                        out=yb[:], in_=ps[:, :], func=AF.Identity, scale=cwT[:, idx, 2:3],
                                        bias=cbT[:, idx:idx + 1]), r=[Rps, Rc], w=[Ryb])
                                    P.op("dve", lambda e, ue=ue, yb=yb, idx=idx: e.scalar_tensor_tensor(
                                        out=yb[:], in0=ue[:, 1:513], scalar=cwT[:, idx, 1:2], in1=yb[:],
                                        op0=ALU.mult, op1=ALU.add), r=[Rue, Rc, Ryb], w=[Ryb])
                                    P.op("dve", lambda e, ue=ue, yb=yb, idx=idx: e.scalar_tensor_tensor(
                                        out=yb[:], in0=ue[:, 0:512], scalar=cwT[:, idx, 0:1], in1=yb[:],
                                        op0=ALU.mult, op1=ALU.add), r=[Rue, Rc, Ryb], w=[Ryb])
                                    ys.append((yb, Ryb))
                                gl, Rgl = glring.next()
                                (yg, Ryg), (yv, Ryv) = ys
                                P.op("act", lambda e, gl=gl, yg=yg: e.activation(
                                    out=gl[:], in_=yg[:], func=AF.Gelu_apprx_tanh), r=[Ryg], w=[Rgl])
                                P.op("dve", lambda e, gl=gl, yv=yv, fc=fc: e.tensor_tensor(
                                    out=aT[:, fc, :], in0=gl[:], in1=yv[:], op=ALU.mult), r=[Rgl, Ryv], w=[RaT])

                        with P.scope():
                            yt = P.sb([128, 4, D], F32)
                            Ryt = [Res() for _ in range(4)]
                            wdring = Ring([(P.sb([128, 4, 512], BF16), Res(), P.dsem(sw=True)) for _ in range(4)])
                            acc = [(P.ps([128, 512], F32), Res()) for _ in range(4)]
                            gpost2 = P.sb([128, D], F32)
                            Rg = Res()
                            dg = P.dsem()
                            P.dma("sp", gpost2[:], g_post2.broadcast_to([128, D]), dg, w=[Rg])
                            small = make_small_ring(4)
                            dout = [P.dsem() for _ in range(4)]
                            evc = 0
                            for nt in range(4):
                                for piece in range(11):
                                    Wd, RWd, dWd = wdring.next()
                                    P.dma("pool", Wd[:], w_down_v[:, piece * 4:(piece + 1) * 4,
                                                                   nt * 512:(nt + 1) * 512], dWd, w=[RWd])
                                    for tb in range(4):
                                        a_ps, Ra = acc[tb]
                                        for k in range(4):
                                            fc = piece * 4 + k
                                            P.op("pe", lambda e, a_ps=a_ps, fc=fc, tb=tb, Wd=Wd, k=k: e.matmul(
                                                a_ps[:, :], lhsT=aT[:, fc, tb * 128:(tb + 1) * 128], rhs=Wd[:, k, :],
                                                start=(fc == 0), stop=(fc == NFC - 1)), r=[RaT, RWd], w=[Ra])
                                for tb in range(4):
                                    a_ps, Ra = acc[tb]
                                    evc += 1
                                    if evc % 2 == 0:
                                        P.op("act", lambda e, a_ps=a_ps, tb=tb, nt=nt: e.activation(
                                            out=yt[:, tb, nt * 512:(nt + 1) * 512], in_=a_ps[:, :], func=AF.Copy),
                                            r=[Ra], w=[Ryt[tb]])
                                    else:
                                        P.op("dve", lambda e, a_ps=a_ps, tb=tb, nt=nt: e.tensor_copy(
                                            out=yt[:, tb, nt * 512:(nt + 1) * 512], in_=a_ps[:, :]), r=[Ra], w=[Ryt[tb]])
                            for tb in range(4):
                                rstd, Rs = norm_stats(yt[:, tb, :].rearrange("p (a b) -> p a b", b=512), Ryt[tb],
                                                      aT[:, 0:4, :], RaT, small)
                                P.op("dve", lambda e, tb=tb, rstd=rstd: e.scalar_tensor_tensor(
                                    out=yt[:, tb, :], in0=yt[:, tb, :], scalar=rstd[:, 0:1], in1=gpost2[:],
                                    op0=ALU.mult, op1=ALU.mult), r=[Ryt[tb], Rs, Rg], w=[Ryt[tb]])
                                P.op("pool" if tb % 2 == 0 else "dve", lambda e, tb=tb: e.tensor_tensor(
                                    out=yt[:, tb, :], in0=yt[:, tb, :], in1=x1[:, tb, :], op=ALU.add),
                                    r=[Ryt[tb], Rx1[tb]], w=[Ryt[tb]])
                                orow = q0 - HALO + tb * 128
                                P.dma("sp", out[orow:orow + 128, :], yt[:, tb, :], dout[tb], r=[Ryt[tb]])
            for ti, (q0, W, is_halo) in enumerate(tiles):
                do_tile(ti, q0, W, is_halo)
        P.emit()
    return nc


_CACHE = {}


def _consts():
    ident = np.eye(128, dtype=np.float32)
    s = np.arange(128)[:, None]
    t = np.arange(128)[None, :]
    masktab = np.zeros((128, 512), np.float32)
    masktab[:, 0:128] = np.where(t >= s, 0.0, NEG)
    slopes = (2.0 ** (-8.0 * np.arange(1, NH + 1) / NH)).astype(np.float32)
    al = np.zeros((128, NH, 256), np.float32)
    for h in range(NH):
        dist_prev = (t + 128 - s).astype(np.float32)
        al[:, h, 0:128] = np.where(s > t, -slopes[h] * dist_prev, NEG)
        dist_cur = (t - s).astype(np.float32)
        al[:, h, 128:256] = np.where(t >= s, -slopes[h] * dist_cur, NEG)
    hi = al.astype(ml_dtypes.bfloat16).astype(np.float32)
    lo = (al - hi).astype(ml_dtypes.bfloat16).astype(np.float32)
    return ident, masktab, hi.reshape(128, NH * 256), lo.reshape(128, NH * 256)


def make_in_maps(inputs, T_CTX, T_OWN, n_cores):
    x = np.asarray(inputs["x"], np.float32)
    B, S, _ = x.shape
    ident, masktab, alhi, allo = _consts()
    f32 = lambda a: np.ascontiguousarray(np.asarray(a, np.float32))
    g_grp = np.concatenate([np.asarray(inputs["grp_swa_g"])[0], np.asarray(inputs["grp_fox_g"])[0]])
    conv_w = np.asarray(inputs["conv_w"], np.float32)[0]
    conv_b = np.asarray(inputs["conv_b"], np.float32)[0]
    common = {
        "w_in": f32(inputs["w_in"][0]), "w_out": f32(inputs["w_out"][0]),
        "w_up": f32(inputs["w_up"][0]), "w_down": f32(inputs["w_down"][0]),
        "g_pre": f32(inputs["pre_mix_g"][0]).reshape(1, D), "g_post": f32(inputs["post_mix_g"][0]).reshape(1, D),
        "g_pre2": f32(inputs["pre_ffn_g"][0]).reshape(1, D), "g_post2": f32(inputs["post_ffn_g"][0]).reshape(1, D),
        "g_grpT": f32(g_grp.reshape(16, 128).T),
        "b_forget": f32(inputs["b_forget"][0]).reshape(16, 1),
        "sinks": f32(inputs["sinks"][0]).reshape(1, 16),
        "cwT": f32(conv_w.reshape(3, 2 * NFC, 128).transpose(2, 1, 0).reshape(128, 2 * NFC * 3)),
        "cbT": f32(conv_b.reshape(2 * NFC, 128).T),
        "ident": ident, "masktab": masktab, "alhi": alhi, "allo": allo,
    }
    nhalf = S // T_OWN
    maps = []
    for c in range(n_cores):
        b, half = c // nhalf, c % nhalf
        m = dict(common)
        own = x[b, half * T_OWN:(half + 1) * T_OWN]
        T_ALL = T_CTX + T_OWN
        ctxrow = np.zeros((1, T_ALL), np.float32)
        if T_CTX:
            if half == 0:
                ctx = x[b, 0:T_CTX]
                ctxrow[0, 0:T_CTX] = NEG
            else:
                ctx = x[b, half * T_OWN - T_CTX: half * T_OWN]
            m["xx"] = np.ascontiguousarray(np.concatenate([ctx, own], axis=0))
        else:
            m["xx"] = np.ascontiguousarray(own)
        m["ctxrow"] = ctxrow
        m["flag"] = np.full((128, 1), 0.0 if half == 0 else 1.0, np.float32)
        maps.append(m)
    return maps


T_CTX_CFG = 2048
T_OWN_CFG = 2048


def kernel(**inputs):
    x = np.asarray(inputs["x"])
    B, S, _ = x.shape
    n_cores = B * (S // T_OWN_CFG)
    key = (T_CTX_CFG, T_OWN_CFG)
    if key not in _CACHE:
        _CACHE[key] = build_program(T_CTX_CFG, T_OWN_CFG)
    nc = _CACHE[key]
    maps = make_in_maps(inputs, T_CTX_CFG, T_OWN_CFG, n_cores)
    res = run_bass_kernel_spmd(nc, maps, core_ids=list(range(n_cores)))
    outs = [np.asarray(r["out"], np.float32) for r in res.results]
    nhalf = S // T_OWN_CFG
    full = np.stack([np.concatenate(outs[b * nhalf:(b + 1) * nhalf], axis=0) for b in range(B)], axis=0)
    return full.astype(np.float32)
```

```python
import numpy as np
import ml_dtypes
from contextlib import ExitStack, contextmanager
import concourse.bass as bass
import concourse.mybir as mybir
from concourse.bass_utils import run_bass_kernel_spmd

F32 = mybir.dt.float32
BF16 = mybir.dt.bfloat16
AF = mybir.ActivationFunctionType
ALU = mybir.AluOpType

D = 2048
DIN = 4368
DFF = 5632
NH = 16
HD = 64
EPS = 1e-6
NEG = -30000.0
NFC = DFF // 128

ENGS = ("pe", "act", "dve", "pool", "sp")
CENGS = ("pe", "act", "dve", "pool")


class Res:
    __slots__ = ("last_w", "readers")

    def __init__(self):
        self.last_w = None
        self.readers = {}


class DSem:
    __slots__ = ("sem", "count", "last", "sw")

    def __init__(self, sem, sw):
        self.sem = sem
        self.count = 0
        self.last = None
        self.sw = sw


class Op:
    __slots__ = ("eng", "fn", "deps", "is_dma", "dsem", "dval", "sigval", "need_sig")


class Ring:
    def __init__(self, items):
        self.items = items
        self.i = 0

    def next(self):
        it = self.items[self.i % len(self.items)]
        self.i += 1
        return it


class Prog:
    def __init__(self, nc, root, same_eng_sync=True):
        self.nc = nc
        self.root = root
        self.stack = root
        self.ops = {e: [] for e in ENGS}
        self.same_eng_sync = same_eng_sync
        self.csem = {e: root.enter_context(nc.semaphore("cs_" + e)) for e in CENGS}
        self.all_dsems = []
        self.free_dsems = {False: [], True: []}
        self.scope_dsems = [[]]
        self.pending_bar = {e: None for e in ENGS}
        self.uid = 0

    def sb(self, shape, dt, name=None):
        self.uid += 1
        return self.stack.enter_context(self.nc.sbuf_tensor(name or ("t%d" % self.uid), list(shape), dt))

    def ps(self, shape, dt, name=None):
        self.uid += 1
        return self.stack.enter_context(self.nc.psum_tensor(name or ("p%d" % self.uid), list(shape), dt))

    def dsem(self, sw=False):
        if self.free_dsems[sw]:
            d = self.free_dsems[sw].pop()
        else:
            d = DSem(self.root.enter_context(self.nc.semaphore("ds%d" % len(self.all_dsems))), sw)
            self.all_dsems.append(d)
        self.scope_dsems[-1].append(d)
        return d

    @contextmanager
    def scope(self):
        old = self.stack
        self.scope_dsems.append([])
        with ExitStack() as st:
            self.stack = st
            yield
            self.barrier()
        self.stack = old
        for d in self.scope_dsems.pop():
            self.free_dsems[d.sw].append(d)

    def barrier(self):
        bar = []
        for e in CENGS:
            for o in reversed(self.ops[e]):
                if not o.is_dma:
                    bar.append(o)
                    break
        for d in self.all_dsems:
            if d.last is not None:
                bar.append(d.last)
        for e in ENGS:
            self.pending_bar[e] = bar

    def _deps(self, op, r, w):
        deps = []
        pb = self.pending_bar[op.eng]
        if pb is not None:
            deps.extend((o, True) for o in pb)
            self.pending_bar[op.eng] = None
        for res in r:
            if res.last_w is not None:
                deps.append((res.last_w, True))
        for res in w:
            if res.last_w is not None:
                deps.append((res.last_w, False))
            deps.extend((o, False) for o in res.readers.values())
        key = id(op.dsem) if op.is_dma else op.eng
        for res in r:
            res.readers[key] = op
        for res in w:
            res.last_w = op
            res.readers = {}
        op.deps = deps

    def op(self, eng, fn, r=(), w=()):
        o = Op()
        o.eng = eng
        o.fn = fn
        o.is_dma = False
        o.dsem = None
        o.dval = 0
        o.sigval = 0
        o.need_sig = False
        self._deps(o, r, w)
        self.ops[eng].append(o)
        return o

    def dma(self, q, out, in_, dsem, r=(), w=()):
        assert dsem.sw == (q == "pool"), "semaphore / DMA queue kind mismatch"
        o = Op()
        o.eng = q
        o.fn = (out, in_)
        o.is_dma = True
        o.dsem = dsem
        dsem.count += 1
        o.dval = dsem.count * 16
        dsem.last = o
        o.sigval = 0
        o.need_sig = False
        self._deps(o, r, w)
        self.ops[q].append(o)
        return o

    def _skip(self, d, o, raw=True):
        return (not d.is_dma) and (not o.is_dma) and d.eng == o.eng and \
            (d.eng == "pe" or (not raw) or not self.same_eng_sync)

    def emit(self):
        nc = self.nc
        self.barrier()
        final = self.pending_bar["sp"]
        for o in final:
            if not o.is_dma:
                o.need_sig = True
        for e in ENGS:
            for o in self.ops[e]:
                for (d, raw) in o.deps:
                    if d.is_dma or self._skip(d, o, raw):
                        continue
                    d.need_sig = True
        for e in CENGS:
            c = 0
            for o in self.ops[e]:
                if (not o.is_dma) and o.need_sig:
                    c += 1
                    o.sigval = c

        def run_stream(e, eh, extra):
            waited = {}

            def do_waits(deps, o):
                need = {}
                for (d, raw) in deps:
                    if d.is_dma:
                        key = id(d.dsem)
                        sem = d.dsem.sem
                        val = d.dval
                    else:
                        if o is not None and self._skip(d, o, raw):
                            continue
                        key = d.eng
                        sem = self.csem[d.eng]
                        val = d.sigval
                    if waited.get(key, 0) >= val:
                        continue
                    if key not in need or need[key][1] < val:
                        need[key] = (sem, val)
                for key, (sem, val) in need.items():
                    eh.wait_ge(sem, val)
                    waited[key] = val

            for o in self.ops[e]:
                do_waits(o.deps, o)
                if o.is_dma:
                    out, in_ = o.fn
                    eh.dma_start(out=out, in_=in_).then_inc(o.dsem.sem, 16)
                else:
                    ins = o.fn(eh)
                    if o.need_sig:
                        ins.then_inc(self.csem[e], 1)
            if extra:
                do_waits([(x, True) for x in extra], None)

        with nc.Block() as block:
            @block.sync
            def _(eh):
                run_stream("sp", eh, final)

            @block.tensor
            def _(eh):
                run_stream("pe", eh, None)

            @block.scalar
            def _(eh):
                run_stream("act", eh, None)

            @block.vector
            def _(eh):
                run_stream("dve", eh, None)

            @block.gpsimd
            def _(eh):
                run_stream("pool", eh, None)


def build_program(T_CTX, T_OWN, debug=False):
    HALO = 128 if T_CTX > 0 else 0
    T_ALL = T_CTX + T_OWN
    NQ = HALO + T_OWN
    NKB = T_ALL // 128
    nc = bass.Bass("TRN2", target_bir_lowering=False)

    def din(name, shape):
        return nc.dram_tensor(name, list(shape), F32, kind="ExternalInput").ap()

    xx = din("xx", [T_ALL, D])
    w_in = din("w_in", [D, DIN])
    w_out = din("w_out", [D, D])
    w_up = din("w_up", [D, 2 * DFF])
    w_down = din("w_down", [DFF, D])
    g_pre = din("g_pre", [1, D])
    g_post = din("g_post", [1, D])
    g_pre2 = din("g_pre2", [1, D])
    g_post2 = din("g_post2", [1, D])
    g_grpT = din("g_grpT", [128, 16])
    b_forget = din("b_forget", [16, 1])
    sinks = din("sinks", [1, 16])
    cwT_d = din("cwT", [128, 2 * NFC * 3])
    cbT_d = din("cbT", [128, 2 * NFC])
    ident_d = din("ident", [128, 128])
    masktab_d = din("masktab", [128, 512])
    alhi_d = din("alhi", [128, 16 * 256])
    allo_d = din("allo", [128, 16 * 256])
    ctxrow_d = din("ctxrow", [1, T_ALL])
    flag_d = din("flag", [128, 1])
    out = nc.dram_tensor("out", [T_OWN, D], F32, kind="ExternalOutput").ap()

    skind = "ExternalOutput" if debug else "Internal"
    qfT = nc.dram_tensor("qfT", [NH, 72, NQ], BF16, kind=skind).ap()
    kfT = nc.dram_tensor("kfT", [NH, 72, T_ALL], BF16, kind=skind).ap()
    vf = nc.dram_tensor("vf", [T_ALL, NH * 128], BF16, kind=skind).ap()
    qaT = nc.dram_tensor("qaT", [NH, 66, NQ], BF16, kind=skind).ap()
    kaT = nc.dram_tensor("kaT", [2, 66, T_ALL], BF16, kind=skind).ap()
    va = nc.dram_tensor("va", [T_ALL, 2 * 128], BF16, kind=skind).ap()
    if debug:
        dbg_oa = nc.dram_tensor("dbg_oa", [NQ // 128 + 4, 128, 16 * 512], F32, kind="ExternalOutput").ap()
        dbg_x1 = nc.dram_tensor("dbg_x1", [NQ, D], F32, kind="ExternalOutput").ap()

    w_in_v = w_in.rearrange("(co ci) n -> ci co n", ci=128)
    w_out_v = w_out.rearrange("(co ci) n -> ci co n", ci=128)
    w_up_v = w_up.rearrange("(co ci) n -> ci co n", ci=128)
    w_down_v = w_down.rearrange("(fo fi) n -> fi fo n", fi=128)

    with ExitStack() as root:
        P = Prog(nc, root)

        identf = P.sb([128, 128], F32)
        identb = P.sb([128, 128], BF16)
        onesf = P.sb([128, 128], F32)
        onesb = P.sb([128, 128], BF16)
        masktab = P.sb([128, 512], BF16)
        ggT = P.sb([128, 16], F32)
        esink = P.sb([128, 16], F32)
        negb = P.sb([16, 1], F32)
        cwT = P.sb([128, 2 * NFC, 3], F32)
        cbT = P.sb([128, 2 * NFC], F32)
        carry = P.sb([128, 2 * NFC, 2], F32)
        flag = P.sb([128, 1], F32)
        Rc = Res()
        Rcarry = Res()
        dc = P.dsem()
        dcs = P.dsem(sw=True)
        P.dma("sp", identf[:], ident_d[:, :], dc, w=[Rc])
        P.dma("pool", identb[:], ident_d[:, :], dcs, w=[Rc])
        P.dma("pool", masktab[:], masktab_d[:, :], dcs, w=[Rc])
        P.dma("sp", ggT[:], g_grpT[:, :], dc, w=[Rc])
        P.dma("sp", esink[:], sinks.broadcast_to([128, 16]), dc, w=[Rc])
        P.dma("sp", negb[:], b_forget[:, :], dc, w=[Rc])
        P.dma("sp", cwT[:], cwT_d.rearrange("p (c k) -> p c k", k=3), dc, w=[Rc])
        P.dma("sp", cbT[:], cbT_d[:, :], dc, w=[Rc])
        P.dma("sp", flag[:], flag_d[:, :], dc, w=[Rc])
        P.op("pool", lambda e: e.memset(onesf[:], 1.0), w=[Rc])
        P.op("pool", lambda e: e.memset(onesb[:], 1.0), w=[Rc])
        P.op("pool", lambda e: e.memset(carry[:], 0.0), w=[Rcarry])
        P.op("act", lambda e: e.activation(out=esink[:], in_=esink[:], func=AF.Exp), r=[Rc], w=[Rc])
        P.op("dve", lambda e: e.tensor_scalar(out=negb[:], in0=negb[:], scalar1=-1.0, scalar2=None, op0=ALU.mult),
             r=[Rc], w=[Rc])
        P.barrier()

        def norm_stats(src_ap, Rsrc, junk, Rjunk, small):
            ss, sd, rstd, Rs = small.next()
            P.op("act", lambda e: e.activation(out=junk[:], in_=src_ap, func=AF.Square, accum_out=ss[:]),
                 r=[Rsrc], w=[Rjunk, Rs])
            P.op("act", lambda e: e.activation(out=sd[:], in_=ss[:], func=AF.Sqrt, scale=1.0 / D, bias=EPS),
                 r=[Rs], w=[Rs])
            P.op("dve", lambda e: e.reciprocal(out=rstd[:], in_=sd[:]), r=[Rs], w=[Rs])
            return rstd, Rs

        def make_small_ring(n):
            items = []
            for _ in range(n):
                items.append((P.sb([128, 1], F32), P.sb([128, 1], F32), P.sb([128, 1], F32), Res()))
            return Ring(items)

        def transpose_block(hb, Rhb, dstT, RdstT, col0, tpring, evi, evac=None):
            for half in range(2):
                tp, Rtp = tpring.next()
                for k in range(8):
                    c = half * 8 + k
                    P.op("pe", lambda e, tp=tp, k=k, c=c: e.transpose(
                        out=tp[:, k, :], in_=hb[:, c * 128:(c + 1) * 128], identity=identb[:]),
                        r=[Rhb, Rc], w=[Rtp])
                eng = evac or ("dve" if (evi + half) % 2 == 0 else "act")
                if eng == "dve":
                    P.op("dve", lambda e, tp=tp, half=half: e.tensor_copy(
                        out=dstT[:, half * 8:(half + 1) * 8, col0:col0 + 128], in_=tp[:]),
                        r=[Rtp], w=[RdstT])
                else:
                    P.op("act", lambda e, tp=tp, half=half: e.activation(
                        out=dstT[:, half * 8:(half + 1) * 8, col0:col0 + 128], in_=tp[:], func=AF.Copy),
                        r=[Rtp], w=[RdstT])

        with P.scope():
            spT = P.sb([16, T_ALL], F32)
            RspT = Res()
            passes = []
            if T_CTX:
                passes.append((0, T_CTX, True))
            passes.append((T_CTX, T_OWN, False))
            TP = max(T_CTX, T_OWN)
            with P.scope():
                hT = P.sb([128, 16, TP], BF16)
                RhT = Res()
                gpre = P.sb([128, D], F32)
                Rg = Res()
                dg = P.dsem()
                P.dma("sp", gpre[:], g_pre.broadcast_to([128, D]), dg, w=[Rg])
                junk = P.sb([128, D], BF16)
                Rjunk = Res()
                small = make_small_ring(3)
                xring = Ring([(P.sb([128, D], F32), Res(), P.dsem()) for _ in range(4)])
                hbring = Ring([(P.sb([128, D], BF16), Res()) for _ in range(2)])
                tpring = Ring([(P.ps([128, 8, 128], BF16), Res()) for _ in range(2)])
                mmring = Ring([(P.ps([128, 512], F32), Res()) for _ in range(4)])
                wring = Ring([(P.sb([128, 16, 256], BF16), Res(), P.dsem(sw=True)) for _ in range(3)])
                stgring = Ring([(P.sb([128, 512], BF16), Res(), P.dsem()) for _ in range(4)])
                vstring = Ring([(P.sb([128, 4, 128], BF16), Res(), P.dsem()) for _ in range(3)])
                etring = Ring([(P.sb([16, 512], F32), Res()) for _ in range(2)])
                for (vst, Rv, _d) in vstring.items:
                    P.op("pool", lambda e, vst=vst: e.memset(vst[:], 1.0), w=[Rv])
                evc = [0]

                for (tok0, ntok, is_ctx) in passes:
                    pend = None
                    for tb in range(ntok // 128):
                        xs, Rxs, dxs = xring.next()
                        P.dma("sp", xs[:], xx[tok0 + tb * 128: tok0 + (tb + 1) * 128, :], dxs, w=[Rxs])
                        rstd, Rs = norm_stats(xs[:], Rxs, junk, Rjunk, small)
                        hb, Rhb = hbring.next()
                        P.op("dve", lambda e, hb=hb, xs=xs, rstd=rstd: e.scalar_tensor_tensor(
                            out=hb[:], in0=xs[:], scalar=rstd[:, 0:1], in1=gpre[:], op0=ALU.mult, op1=ALU.mult),
                            r=[Rxs, Rs, Rg], w=[Rhb])
                        if pend is not None:
                            transpose_block(*pend)
                        pend = (hb, Rhb, hT, RhT, tb * 128, tpring, tb)
                    transpose_block(*pend)

                    full_tiles = [(t0, 512) for t0 in range(0, ntok, 512)]
                    halo_tiles = [(ntok - 128, 128)]

                    def qcol(t0):
                        return (t0 - (ntok - 128)) if is_ctx else (HALO + t0)

                    def fm(Wt, RW, col_lo, tiles, scale, dest_fn):
                        for (t0, tw) in tiles:
                            ps, Rps = mmring.next()
                            for c in range(16):
                                P.op("pe", lambda e, ps=ps, c=c, t0=t0, tw=tw: e.matmul(
                                    ps[:, 0:tw], lhsT=Wt[:, c, col_lo:col_lo + 128], rhs=hT[:, c, t0:t0 + tw],
                                    start=(c == 0), stop=(c == 15)), r=[RW, RhT], w=[Rps])
                            stg, Rstg, dstg = stgring.next()
                            evc[0] += 1
                            if evc[0] % 2 == 0:
                                P.op("act", lambda e, ps=ps, stg=stg, tw=tw: e.activation(
                                    out=stg[:, 0:tw], in_=ps[:, 0:tw], func=AF.Copy, scale=scale),
                                    r=[Rps], w=[Rstg])
                            else:
                                P.op("dve", lambda e, ps=ps, stg=stg, tw=tw: e.tensor_scalar(
                                    out=stg[:, 0:tw], in0=ps[:, 0:tw], scalar1=scale, scalar2=None, op0=ALU.mult),
                                    r=[Rps], w=[Rstg])
                            for hh in range(2):
                                P.dma("sp", dest_fn(hh, t0, tw), stg[hh * 64:(hh + 1) * 64, 0:tw], dstg, r=[Rstg])

                    def tm(Wt, RW, col_lo, nheads, dst, dcol0):
                        ncols = nheads * 64
                        for tb in range(ntok // 128):
                            ps, Rps = mmring.next()
                            for c in range(16):
                                P.op("pe", lambda e, ps=ps, c=c, tb=tb: e.matmul(
                                    ps[:, 0:ncols], lhsT=hT[:, c, tb * 128:(tb + 1) * 128],
                                    rhs=Wt[:, c, col_lo:col_lo + ncols], start=(c == 0), stop=(c == 15)),
                                    r=[RW, RhT], w=[Rps])
                            vst, Rvst, dvst = vstring.next()
                            evc[0] += 1
                            src = ps[:, 0:ncols].rearrange("p (h d) -> p h d", d=64)
                            if evc[0] % 2 == 0:
                                P.op("act", lambda e, vst=vst, src=src: e.activation(
                                    out=vst[:, 0:nheads, 0:64], in_=src, func=AF.Copy), r=[Rps], w=[Rvst])
                            else:
                                P.op("dve", lambda e, vst=vst, src=src: e.tensor_copy(
                                    out=vst[:, 0:nheads, 0:64], in_=src), r=[Rps], w=[Rvst])
                            r0 = tok0 + tb * 128
                            P.dma("sp", dst[r0:r0 + 128, dcol0:dcol0 + nheads * 128].rearrange("p (h d) -> p h d", d=128),
                                  vst[:, 0:nheads, :], dvst, r=[Rvst])

                    for ch in range(18):
                        if is_ctx and False:
                            continue
                        Wt, RW, dW = wring.next()
                        ncol = 256 if ch < 17 else 16
                        P.dma("pool", Wt[:, :, 0:ncol], w_in_v[:, :, ch * 256: ch * 256 + ncol], dW, w=[RW])
                        if ch < 4 or 5 <= ch <= 8:
                            dstT = qaT if ch < 4 else qfT
                            h0 = (ch if ch < 4 else ch - 5) * 4
                            tiles = halo_tiles if is_ctx else full_tiles
                            for sub in range(2):
                                fm(Wt, RW, sub * 128, tiles, 0.125,
                                   lambda hh, t0, tw, h0=h0, sub=sub, dstT=dstT:
                                   dstT[h0 + 2 * sub + hh, 0:64, qcol(t0):qcol(t0) + tw])
                        elif ch == 4:
                            fm(Wt, RW, 0, full_tiles, 1.0,
                               lambda hh, t0, tw: kaT[hh, 0:64, tok0 + t0: tok0 + t0 + tw])
                            tm(Wt, RW, 128, 2, va, 0)
                        elif 9 <= ch <= 12:
                            h0 = (ch - 9) * 4
                            for sub in range(2):
                                fm(Wt, RW, sub * 128, full_tiles, 1.0,
                                   lambda hh, t0, tw, h0=h0, sub=sub:
                                   kfT[h0 + 2 * sub + hh, 0:64, tok0 + t0: tok0 + t0 + tw])
                        elif 13 <= ch <= 16:
                            tm(Wt, RW, 0, 4, vf, (ch - 13) * 4 * 128)
                        else:
                            for (t0, tw) in full_tiles:
                                ps, Rps = mmring.next()
                                for c in range(16):
                                    P.op("pe", lambda e, ps=ps, c=c, t0=t0, tw=tw, Wt=Wt: e.matmul(
                                        ps[0:16, 0:tw], lhsT=Wt[:, c, 0:16], rhs=hT[:, c, t0:t0 + tw],
                                        start=(c == 0), stop=(c == 15)), r=[RW, RhT], w=[Rps])
                                et, Ret = etring.next()
                                P.op("act", lambda e, ps=ps, et=et, tw=tw: e.activation(
                                    out=et[:, 0:tw], in_=ps[0:16, 0:tw], func=AF.Exp, scale=-1.0, bias=negb[:, 0:1]),
                                    r=[Rps, Rc], w=[Ret])
                                P.op("act", lambda e, et=et, t0=t0, tw=tw, tok0=tok0: e.activation(
                                    out=spT[:, tok0 + t0: tok0 + t0 + tw], in_=et[:, 0:tw], func=AF.Ln, bias=1.0),
                                    r=[Ret], w=[RspT])

            with P.scope():
                zeros = P.sb([16, T_ALL], F32)
                cs = P.sb([16, T_ALL], F32)
                r1 = P.sb([16, T_ALL], F32)
                hi = P.sb([16, T_ALL], BF16)
                mid = P.sb([16, T_ALL], BF16)
                lo = P.sb([16, T_ALL], BF16)
                nq3 = P.sb([16, 3, NQ], BF16)
                ones = P.sb([16, T_ALL], BF16)
                row70 = P.sb([16, NQ], BF16)
                ctxb = P.sb([16, T_ALL], BF16)
                Rz = Res()
                dz = P.dsem(sw=True)
                dz2 = P.dsem()
                P.op("pool", lambda e: e.memset(zeros[:], 0.0), w=[Rz])
                P.op("pool", lambda e: e.memset(ones[:], 1.0), w=[Rz])
                P.op("pool", lambda e: e.memset(row70[:], 1.0), w=[Rz])
                if HALO:
                    P.op("pool", lambda e: e.memset(row70[:, 0:HALO], 0.0), w=[Rz])
                P.dma("pool", ctxb[:], ctxrow_d.broadcast_to([16, T_ALL]), dz, w=[Rz])
                V_ = "dve"
                P.op(V_, lambda e: e.tensor_tensor_scan(out=cs[:], data0=spT[:], data1=zeros[:], initial=0.0,
                                                        op0=ALU.add, op1=ALU.add), r=[RspT, Rz], w=[Rz])
                P.op(V_, lambda e: e.tensor_copy(out=hi[:], in_=cs[:]), r=[Rz], w=[Rz])
                P.op(V_, lambda e: e.tensor_tensor(out=r1[:], in0=cs[:], in1=hi[:], op=ALU.subtract), r=[Rz], w=[Rz])
                P.op(V_, lambda e: e.tensor_copy(out=mid[:], in_=r1[:]), r=[Rz], w=[Rz])
                P.op(V_, lambda e: e.tensor_tensor(out=cs[:], in0=r1[:], in1=mid[:], op=ALU.subtract), r=[Rz], w=[Rz])
                P.op(V_, lambda e: e.tensor_copy(out=lo[:], in_=cs[:]), r=[Rz], w=[Rz])
                qs = T_CTX - HALO
                for j, src in enumerate((hi, mid, lo)):
                    P.op(V_, lambda e, j=j, src=src: e.tensor_scalar(
                        out=nq3[:, j, :], in0=src[:, qs:T_ALL], scalar1=-1.0, scalar2=None, op0=ALU.mult),
                        r=[Rz], w=[Rz])
                P.dma("sp", qfT[:, 64:67, :], nq3[:], dz2, r=[Rz])
                for j, src in enumerate((hi, mid, lo)):
                    P.dma("sp", kfT[:, 67 + j, :], src[:], dz2, r=[Rz])
                for j in range(3):
                    P.dma("sp", qfT[:, 67 + j, :], ones[:, 0:NQ], dz2, r=[Rz])
                    P.dma("sp", kfT[:, 64 + j, :], ones[:], dz2, r=[Rz])
                P.dma("sp", qfT[:, 70, :], row70[:], dz2, r=[Rz])
                P.dma("sp", qaT[:, 64, :], row70[:], dz2, r=[Rz])
                P.dma("sp", kfT[:, 70, :], ctxb[:], dz2, r=[Rz])
                P.dma("sp", kaT[:, 64, :], ctxb[0:2, :], dz2, r=[Rz])
                zb = P.sb([16, T_ALL], BF16)
                P.op("pool", lambda e: e.memset(zb[:], 0.0), w=[Rz])
                P.dma("sp", qfT[:, 71, :], zb[:, 0:NQ], dz2, r=[Rz])
                P.dma("sp", kfT[:, 71, :], zb[:], dz2, r=[Rz])
                P.dma("sp", qaT[:, 65, :], zb[:, 0:NQ], dz2, r=[Rz])
                P.dma("sp", kaT[:, 65, :], zb[0:2, :], dz2, r=[Rz])

        tiles = []
        if HALO:
            tiles.append((0, 128, True))
        for i in range(T_OWN // 512):
            tiles.append((HALO + i * 512, 512, False))

        with P.scope():
            h2T_halo = P.sb([128, 16, 128], BF16)
            Rh2h = Res()
            have_halo = [False]

            def do_tile(ti, q0, W, is_halo):
                ts = T_CTX - HALO + q0
                kb0 = ts // 128
                nqb = W // 128
                nkb = kb0 + nqb
                with P.scope():
                    OA = P.sb([128, 16, 512], F32)
                    ROA = Res()

                    with P.scope():
                        sring = Ring([(P.ps([128, 512], F32), Res()) for _ in range(3)])
                        oring = Ring([(P.ps([128, 512], F32), Res()) for _ in range(2)])
                        oring_swa = Ring([(P.ps([128, 512], F32), Res()) for _ in range(1)])
                        sring_swa = Ring([(P.ps([128, 512], F32), Res()) for _ in range(2)])
                        ptring_swa = Ring([(P.sb([128, 512], BF16), Res()) for _ in range(2)])
                        ptring = Ring([(P.sb([128, 512], BF16), Res()) for _ in range(4)])
                        ktring = Ring([(P.sb([72, T_ALL], BF16), Res(), P.dsem()) for _ in range(2)])
                        qtring = Ring([(P.sb([72, 512], BF16), Res(), P.dsem()) for _ in range(3)])
                        vgring = Ring([(P.sb([128, NKB, 512], BF16), (Res(), Res()), (P.dsem(), P.dsem()))
                                       for _ in range(2)])
                        rdring = Ring([(P.sb([64, 512], F32), Res()) for _ in range(2)])
                        alhi = P.sb([128, 16, 256], BF16)
                        allo = P.sb([128, 16, 256], BF16)
                        kbase = max(kb0 - 1, 0)
                        nka = nkb - kbase
                        KaT = P.sb([66, 2, nka * 128], BF16)
                        VA = P.sb([128, nka, 256], BF16)
                        Rtab = Res()
                        dtab = P.dsem()
                        dtabs = P.dsem(sw=True)
                        P.dma("sp", KaT[:], kaT.rearrange("k r t -> r k t")[:, :, kbase * 128: nkb * 128], dtab, w=[Rtab])
                        P.dma("sp", VA[:], va[kbase * 128: nkb * 128, :].rearrange("(kb p) c -> p kb c", p=128),
                              dtab, w=[Rtab])

                        def finalize(h, O, RO, ch, pb, is_swa):
                            rd, Rrd = rdring.next()
                            if is_swa:
                                P.op("dve", lambda e: e.tensor_scalar(
                                    out=rd[:, 0:W], in0=O[64:128, 0:W], scalar1=esink[64:128, h:h + 1],
                                    scalar2=None, op0=ALU.add), r=[RO, Rc], w=[Rrd])
                                P.op("dve", lambda e: e.reciprocal(out=rd[:, 0:W], in_=rd[:, 0:W]),
                                     r=[Rrd], w=[Rrd])
                            else:
                                P.op("dve", lambda e: e.reciprocal(out=rd[:, 0:W], in_=O[64:128, 0:W]),
                                     r=[RO], w=[Rrd])
                            P.op("dve", lambda e: e.tensor_tensor(
                                out=OA[pb:pb + 64, ch, 0:W], in0=O[0:64, 0:W], in1=rd[:, 0:W], op=ALU.mult),
                                r=[RO, Rrd], w=[ROA])

                        def swa_head(h):
                            kv = h // 8
                            qt, Rqt, dqt = qtring.next()
                            P.dma("sp", qt[0:66, 0:W], qaT[h, :, q0:q0 + W], dqt, w=[Rqt])
                            O, RO = oring_swa.next()
                            pvs = []
                            for n0 in range(0, nqb, 2):
                                S, RS = sring_swa.next()
                                PT, RPT = ptring_swa.next()
                                ns = [n for n in (n0, n0 + 1) if n < nqb]
                                lo_c = None
                                for n in ns:
                                    cur = kb0 + n
                                    prev = cur - 1
                                    base = (n % 2) * 256
                                    a0 = 0 if prev >= 0 else 128
                                    if lo_c is None:
                                        lo_c = base + a0
                                    P.op("pe", lambda e, base=base, a0=a0, S=S: e.matmul(
                                        S[:, base + a0:base + 256], lhsT=identb[:], rhs=alhi[:, h, a0:256],
                                        start=True, stop=False), r=[Rc, Rtab], w=[RS])
                                    P.op("pe", lambda e, base=base, a0=a0, S=S: e.matmul(
                                        S[:, base + a0:base + 256], lhsT=identb[:], rhs=allo[:, h, a0:256],
                                        start=False, stop=False), r=[Rc, Rtab], w=[RS])
                                    if prev >= 0:
                                        P.op("pe", lambda e, base=base, prev=prev, n=n, S=S: e.matmul(
                                            S[:, base:base + 128],
                                            lhsT=KaT[0:65, kv, (prev - kbase) * 128:(prev - kbase + 1) * 128],
                                            rhs=qt[0:65, n * 128:(n + 1) * 128], start=False, stop=False),
                                            r=[Rtab, Rqt], w=[RS])
                                    P.op("pe", lambda e, base=base, cur=cur, n=n, S=S: e.matmul(
                                        S[:, base + 128:base + 256],
                                        lhsT=KaT[0:65, kv, (cur - kbase) * 128:(cur - kbase + 1) * 128],
                                        rhs=qt[0:65, n * 128:(n + 1) * 128], start=False, stop=True),
                                        r=[Rtab, Rqt], w=[RS])
                                hi_c = (ns[-1] % 2) * 256 + 256
                                P.op("act", lambda e, lo_c=lo_c, hi_c=hi_c, S=S, PT=PT: e.activation(
                                    out=PT[:, lo_c:hi_c], in_=S[:, lo_c:hi_c], func=AF.Exp), r=[RS], w=[RPT])
                                pvs.append((ns, PT, RPT))

                            def part_b():
                                for (ns, PT, RPT) in pvs:
                                    for n in ns:
                                        cur = kb0 + n
                                        prev = cur - 1
                                        base = (n % 2) * 256
                                        if prev >= 0:
                                            P.op("pe", lambda e, base=base, prev=prev, n=n, PT=PT: e.matmul(
                                                O[:, n * 128:(n + 1) * 128],
                                                lhsT=VA[:, prev - kbase, kv * 128:(kv + 1) * 128],
                                                rhs=PT[:, base:base + 128], start=True, stop=False),
                                                r=[Rtab, RPT], w=[RO])
                                        P.op("pe", lambda e, base=base, cur=cur, n=n, PT=PT, prev=prev: e.matmul(
                                            O[:, n * 128:(n + 1) * 128],
                                            lhsT=VA[:, cur - kbase, kv * 128:(kv + 1) * 128],
                                            rhs=PT[:, base + 128:base + 256], start=(prev < 0), stop=True),
                                            r=[Rtab, RPT], w=[RO])
                                finalize(h, O, RO, h // 2, (h % 2) * 64, True)
                            return part_b
                        units = [(h, j) for h in range(NH) for j in range(nkb)]
                        state = {}
                        headres = {}
                        deferred = []
                        swa_b = {}
                        LOOK = 2

                        vhalf = (nkb + 1) // 2

                        def head_setup(h):
                            kt, Rkt, dkt = ktring.next()
                            P.dma("sp", kt[:, 0:nkb * 128], kfT[h, :, 0:nkb * 128], dkt, w=[Rkt])
                            qt, Rqt, dqt = qtring.next()
                            P.dma("sp", qt[:, 0:W], qfT[h, :, q0:q0 + W], dqt, w=[Rqt])
                            if h % 4 == 0:
                                vg, Rvg, dvg = vgring.next()
                                g = h // 4
                                for k, (a, b) in enumerate(((0, vhalf), (vhalf, nkb))):
                                    if b > a:
                                        P.dma("sp", vg[:, a:b, :],
                                              vf[a * 128:b * 128, g * 512:(g + 1) * 512].rearrange(
                                                  "(kb p) c -> p kb c", p=128), dvg[k], w=[Rvg[k]])
                                headres["vg"] = (vg, Rvg)
                                if h == 0:
                                    P.dma("pool", alhi[:], alhi_d.rearrange("p (h t) -> p h t", t=256), dtabs,
                                          r=[Rvg[0]], w=[Rtab])
                                    P.dma("pool", allo[:], allo_d.rearrange("p (h t) -> p h t", t=256), dtabs,
                                          r=[Rvg[0]], w=[Rtab])
                            O, RO = oring.next()
                            headres[h] = (kt, Rkt, qt, Rqt, O, RO) + headres["vg"]

                        def emit_S(h, j):
                            if j == 0:
                                head_setup(h)
                            kt, Rkt, qt, Rqt, O, RO, vg, Rvg = headres[h]
                            o = j - kb0
                            c0 = max(o, 0) * 128
                            N = W - c0
                            S, RS = sring.next()
                            if o >= 0:
                                P.op("pe", lambda e: e.matmul(S[:, c0:W], lhsT=identb[:], rhs=masktab[:, 0:N],
                                                              start=True, stop=False), r=[Rc], w=[RS])
                                P.op("pe", lambda e: e.matmul(S[:, c0:W], lhsT=kt[0:71, j * 128:(j + 1) * 128],
                                                              rhs=qt[0:71, c0:W], start=False, stop=True),
                                     r=[Rkt, Rqt], w=[RS])
                            else:
                                P.op("pe", lambda e: e.matmul(S[:, 0:W], lhsT=kt[0:71, j * 128:(j + 1) * 128],
                                                              rhs=qt[0:71, 0:W], start=True, stop=True),
                                     r=[Rkt, Rqt], w=[RS])
                            PT, RPT = ptring.next()
                            P.op("act", lambda e: e.activation(out=PT[:, c0:W], in_=S[:, c0:W], func=AF.Exp),
                                 r=[RS], w=[RPT])
                            state[(h, j)] = (PT, RPT, c0)

                        def emit_PV(h, j):
                            kt, Rkt, qt, Rqt, O, RO, vg, Rvg = headres[h]
                            PT, RPT, c0 = state.pop((h, j))
                            hl = h % 4
                            P.op("pe", lambda e: e.matmul(O[:, c0:W], lhsT=vg[:, j, hl * 128:(hl + 1) * 128],
                                                          rhs=PT[:, c0:W], start=(j == 0), stop=(j == nkb - 1)),
                                 r=[Rvg[0 if j < vhalf else 1], RPT], w=[RO])
                            if j == nkb - 1:
                                finalize(h, O, RO, 8 + h // 2, (h % 2) * 64, False)

                        def run_deferred(force=False):
                            for d in list(deferred):
                                d[0] -= 1
                                if d[0] <= 0 or force:
                                    d[1]()
                                    deferred.remove(d)

                        for i in range(len(units) + LOOK):
                            if i < len(units):
                                emit_S(*units[i])
                                if units[i][1] == nkb // 2:
                                    swa_b[units[i][0]] = swa_head(units[i][0])
                                if units[i][1] == min(nkb // 2 + 3, nkb - 1):
                                    swa_b.pop(units[i][0])()
                            if i - LOOK >= 0:
                                emit_PV(*units[i - LOOK])
                            run_deferred()
                        run_deferred(force=True)


                    if debug:
                        dd = P.dsem()
                        P.dma("sp", dbg_oa[ti].rearrange("p (c t) -> p c t", t=512), OA[:], dd, r=[ROA])

                    x1 = P.sb([128, 4, D], F32)
                    Rx1 = [Res() for _ in range(4)]
                    h2T = P.sb([128, 16, 512], BF16)
                    Rh2T = Res()
                    wu0 = None
                    if not is_halo:
                        wu0 = (P.sb([128, 16, 512], BF16), Res(), P.dsem(sw=True))
                    with P.scope():
                      gpost = P.sb([128, D], F32)
                      gpre2 = P.sb([128, D], F32)
                      Rg = Res()
                      dg = P.dsem()
                      P.dma("sp", gpost[:], g_post.broadcast_to([128, D]), dg, w=[Rg])
                      P.dma("sp", gpre2[:], g_pre2.broadcast_to([128, D]), dg, w=[Rg])
                      xring = Ring([(P.sb([128, D], F32), Res(), P.dsem()) for _ in range(2)])
                      with P.scope():
                        sqs = [(P.sb([128, 8, 512], BF16), Res()) for _ in range(2)]
                        onT = P.sb([128, 16, 512], BF16)
                        RonT = Res()
                        rsbs = [(P.sb([128, 512], F32), Res()) for _ in range(2)]
                        ssbs = [(P.ps([128, 512], F32), Res()) for _ in range(2)]
                        for g in range(2):
                            sq, Rsq = sqs[g]
                            P.op("act", lambda e, g=g, sq=sq: e.activation(
                                out=sq[:, :, 0:W], in_=OA[:, g * 8:(g + 1) * 8, 0:W], func=AF.Square),
                                r=[ROA], w=[Rsq])
                        for g in range(2):
                            sq, Rsq = sqs[g]
                            ssb, Rssb = ssbs[g]
                            for k in range(8):
                                P.op("pe", lambda e, k=k, sq=sq, ssb=ssb: e.matmul(
                                    ssb[:, 0:W], lhsT=onesb[:], rhs=sq[:, k, 0:W], start=(k == 0), stop=(k == 7)),
                                    r=[Rsq, Rc], w=[Rssb])
                        for g in range(2):
                            ssb, Rssb = ssbs[g]
                            rsb, Rrsb = rsbs[g]
                            P.op("act", lambda e, ssb=ssb, rsb=rsb: e.activation(
                                out=rsb[:, 0:W], in_=ssb[:, 0:W], func=AF.Sqrt, scale=1.0 / 1024, bias=EPS),
                                r=[Rssb], w=[Rrsb])
                            P.op("dve", lambda e, rsb=rsb: e.reciprocal(out=rsb[:, 0:W], in_=rsb[:, 0:W]),
                                 r=[Rrsb], w=[Rrsb])
                            for k in range(8):
                                c = g * 8 + k
                                P.op("dve", lambda e, c=c, rsb=rsb: e.scalar_tensor_tensor(
                                    out=onT[:, c, 0:W], in0=OA[:, c, 0:W], scalar=ggT[:, c:c + 1], in1=rsb[:, 0:W],
                                    op0=ALU.mult, op1=ALU.mult), r=[ROA, Rrsb, Rc], w=[RonT])
                        woring = Ring([(P.sb([128, 16, 512], BF16), Res(), P.dsem(sw=True)) for _ in range(2)])
                        mmring = Ring([(P.ps([128, 512], F32), Res()) for _ in range(3)])
                        evc = 0
                        for nt in range(4):
                            Wo, RWo, dWo = woring.next()
                            P.dma("pool", Wo[:], w_out_v[:, :, nt * 512:(nt + 1) * 512], dWo, w=[RWo])
                            for tb in range(nqb):
                                ps, Rps = mmring.next()
                                for c in range(16):
                                    P.op("pe", lambda e, ps=ps, c=c, tb=tb, Wo=Wo: e.matmul(
                                        ps[:, :], lhsT=onT[:, c, tb * 128:(tb + 1) * 128], rhs=Wo[:, c, :],
                                        start=(c == 0), stop=(c == 15)), r=[RonT, RWo], w=[Rps])
                                evc += 1
                                if evc % 2 == 0:
                                    P.op("act", lambda e, ps=ps, tb=tb, nt=nt: e.activation(
                                        out=x1[:, tb, nt * 512:(nt + 1) * 512], in_=ps[:, :], func=AF.Copy),
                                        r=[Rps], w=[Rx1[tb]])
                                else:
                                    P.op("dve", lambda e, ps=ps, tb=tb, nt=nt: e.tensor_copy(
                                        out=x1[:, tb, nt * 512:(nt + 1) * 512], in_=ps[:, :]), r=[Rps], w=[Rx1[tb]])
                      if True:
                        if wu0 is not None:
                            P.dma("pool", wu0[0][:, :, 0:256], w_up_v[:, :, 0:256], wu0[2], w=[wu0[1]])
                            P.dma("pool", wu0[0][:, :, 256:512], w_up_v[:, :, DFF:DFF + 256], wu0[2], w=[wu0[1]])
                        junk = P.sb([128, D], BF16)
                        Rjunk = Res()
                        junk2 = P.sb([128, D], BF16)
                        Rjunk2 = Res()
                        small = make_small_ring(8)
                        hbring = Ring([(P.sb([128, D], BF16), Res()) for _ in range(2)])
                        tpring = Ring([(P.ps([128, 8, 128], BF16), Res()) for _ in range(2)])
                        dst_h2T, Rdst = (h2T_halo, Rh2h) if is_halo else (h2T, Rh2T)
                        hbs = {}

                        def stA(tb):
                            xs, Rxs, dxs = xring.next()
                            P.dma("sp", xs[:], xx[ts + tb * 128: ts + (tb + 1) * 128, :], dxs, w=[Rxs])
                            rstd, Rs = norm_stats(x1[:, tb, :], Rx1[tb], junk, Rjunk, small)
                            P.op("dve", lambda e: e.scalar_tensor_tensor(
                                out=x1[:, tb, :], in0=x1[:, tb, :], scalar=rstd[:, 0:1], in1=gpost[:],
                                op0=ALU.mult, op1=ALU.mult), r=[Rx1[tb], Rs, Rg], w=[Rx1[tb]])
                            P.op("pool", lambda e: e.tensor_tensor(
                                out=x1[:, tb, :], in0=x1[:, tb, :], in1=xs[:], op=ALU.add),
                                r=[Rx1[tb], Rxs], w=[Rx1[tb]])
                            if debug:
                                dd = P.dsem()
                                P.dma("sp", dbg_x1[q0 + tb * 128: q0 + (tb + 1) * 128, :], x1[:, tb, :], dd, r=[Rx1[tb]])

                        def stB(tb):
                            rstd2, Rs2 = norm_stats(x1[:, tb, :], Rx1[tb], junk2, Rjunk2, small)
                            hb, Rhb = hbring.next()
                            hbs[tb] = (hb, Rhb)
                            P.op("dve", lambda e: e.scalar_tensor_tensor(
                                out=hb[:], in0=x1[:, tb, :], scalar=rstd2[:, 0:1], in1=gpre2[:],
                                op0=ALU.mult, op1=ALU.mult), r=[Rx1[tb], Rs2, Rg], w=[Rhb])

                        def stC(tb):
                            hb, Rhb = hbs.pop(tb)
                            transpose_block(hb, Rhb, dst_h2T, Rdst, tb * 128, tpring, 0, evac="dve")

                        for step in range(nqb + 3):
                            for fn, lag in ((stA, 0), (stB, 2), (stC, 3)):
                                if 0 <= step - lag < nqb:
                                    fn(step - lag)
                    if is_halo:
                        have_halo[0] = True
                        return

                    with P.scope():
                        aT = P.sb([128, NFC, 512], BF16)
                        RaT = Res()
                        use_halo = have_halo[0]
                        have_halo[0] = False
                        with P.scope():
                            wuring = Ring([wu0] + [(P.sb([128, 16, 512], BF16), Res(), P.dsem(sw=True))
                                                   for _ in range(2)])
                            wuring.next()
                            wu_slots = {0: (wu0[0], wu0[1])}
                            upring = Ring([(P.ps([128, 512], F32), Res()) for _ in range(4)])
                            phring = Ring([(P.ps([128, 2], F32), Res()) for _ in range(2)])
                            uering = Ring([(P.sb([128, 514], F32), Res()) for _ in range(3)])
                            ybring = Ring([(P.sb([128, 512], F32), Res()) for _ in range(4)])
                            glring = Ring([(P.sb([128, 512], F32), Res()) for _ in range(2)])
                            for fc in range(NFC):
                                if fc % 2 == 0:
                                    for g in ([1, 2] if fc == 0 else [fc // 2 + 2]):
                                        if g < NFC // 2:
                                            Wn, RWn, dWn = wuring.next()
                                            wu_slots[g] = (Wn, RWn)
                                            P.dma("pool", Wn[:, :, 0:256], w_up_v[:, :, g * 256:(g + 1) * 256],
                                                  dWn, w=[RWn])
                                            P.dma("pool", Wn[:, :, 256:512],
                                                  w_up_v[:, :, DFF + g * 256: DFF + (g + 1) * 256], dWn, w=[RWn])
                                    Wu, RWu = wu_slots.pop(fc // 2)
                                wo = (fc % 2) * 128
                                ys = []
                                for part in range(2):
                                    idx = part * NFC + fc
                                    ue, Rue = uering.next()
                                    if use_halo:
                                        ph, Rph = phring.next()
                                        for c in range(16):
                                            P.op("pe", lambda e, ph=ph, c=c, Wu=Wu, part=part, wo=wo: e.matmul(
                                                ph[:, :], lhsT=Wu[:, c, part * 256 + wo:part * 256 + wo + 128],
                                                rhs=h2T_halo[:, c, 126:128], start=(c == 0), stop=(c == 15)),
                                                r=[RWu, Rh2h], w=[Rph])
                                        P.op("dve", lambda e, ph=ph, ue=ue: e.tensor_scalar(
                                            out=ue[:, 0:2], in0=ph[:, 0:2], scalar1=flag[:, 0:1], scalar2=None,
                                            op0=ALU.mult), r=[Rph, Rc], w=[Rue])
                                    else:
                                        P.op("act", lambda e, ue=ue, idx=idx: e.activation(
                                            out=ue[:, 0:2], in_=carry[:, idx, :], func=AF.Copy), r=[Rcarry], w=[Rue])
                                    ps, Rps = upring.next()
                                    for c in range(16):
                                        P.op("pe", lambda e, ps=ps, c=c, Wu=Wu, part=part, wo=wo: e.matmul(
                                            ps[:, :], lhsT=Wu[:, c, part * 256 + wo:part * 256 + wo + 128], rhs=h2T[:, c, :],
                                            start=(c == 0), stop=(c == 15)), r=[RWu, Rh2T], w=[Rps])
                                    P.op("act", lambda e, ps=ps, ue=ue: e.activation(
                                        out=ue[:, 2:514], in_=ps[:, :], func=AF.Copy), r=[Rps], w=[Rue])
                                    P.op("act", lambda e, ue=ue, idx=idx: e.activation(
                                        out=carry[:, idx, :], in_=ue[:, 512:514], func=AF.Copy), r=[Rue], w=[Rcarry])
                                    yb, Ryb = ybring.next()
                                    P.op("act", lambda e, ps=ps, yb=yb, idx=idx: e.activation(
                                        out=yb[:], in_=ps[:, :], func=AF.Identity, scale=cwT[:, idx, 2:3],
                                        bias=cbT[:, idx:idx + 1]), r=[Rps, Rc], w=[Ryb])
                                    P.op("dve", lambda e, ue=ue, yb=yb, idx=idx: e.scalar_tensor_tensor(
                                        out=yb[:], in0=ue[:, 1:513], scalar=cwT[:, idx, 1:2], in1=yb[:],
                                        op0=ALU.mult, op1=ALU.add), r=[Rue, Rc, Ryb], w=[Ryb])
                                    P.op("dve", lambda e, ue=ue, yb=yb, idx=idx: e.scalar_tensor_tensor(
                                        out=yb[:], in0=ue[:, 0:512], scalar=cwT[:, idx, 0:1], in1=yb[:],
                                        op0=ALU.mult, op1=ALU.add), r=[Rue, Rc, Ryb], w=[Ryb])
                                    ys.append((yb, Ryb))
                                gl, Rgl = glring.next()
                                (yg, Ryg), (yv, Ryv) = ys
                                P.op("act", lambda e, gl=gl, yg=yg: e.activation(
                                    out=gl[:], in_=yg[:], func=AF.Gelu_apprx_tanh), r=[Ryg], w=[Rgl])
                                P.op("dve", lambda e, gl=gl, yv=yv, fc=fc: e.tensor_tensor(
                                    out=aT[:, fc, :], in0=gl[:], in1=yv[:], op=ALU.mult), r=[Rgl, Ryv], w=[RaT])

                        with P.scope():
                            yt = P.sb([128, 4, D], F32)
                            Ryt = [Res() for _ in range(4)]
                            wdring = Ring([(P.sb([128, 4, 512], BF16), Res(), P.dsem(sw=True)) for _ in range(4)])
                            acc = [(P.ps([128, 512], F32), Res()) for _ in range(4)]
                            gpost2 = P.sb([128, D], F32)
                            Rg = Res()
                            dg = P.dsem()
                            P.dma("sp", gpost2[:], g_post2.broadcast_to([128, D]), dg, w=[Rg])
                            small = make_small_ring(4)
                            dout = [P.dsem() for _ in range(4)]
                            evc = 0
                            for nt in range(4):
                                for piece in range(11):
                                    Wd, RWd, dWd = wdring.next()
                                    P.dma("pool", Wd[:], w_down_v[:, piece * 4:(piece + 1) * 4,
                                                                   nt * 512:(nt + 1) * 512], dWd, w=[RWd])
                                    for tb in range(4):
                                        a_ps, Ra = acc[tb]
                                        for k in range(4):
                                            fc = piece * 4 + k
                                            P.op("pe", lambda e, a_ps=a_ps, fc=fc, tb=tb, Wd=Wd, k=k: e.matmul(
                                                a_ps[:, :], lhsT=aT[:, fc, tb * 128:(tb + 1) * 128], rhs=Wd[:, k, :],
                                                start=(fc == 0), stop=(fc == NFC - 1)), r=[RaT, RWd], w=[Ra])
                                for tb in range(4):
                                    a_ps, Ra = acc[tb]
                                    evc += 1
                                    if evc % 2 == 0:
                                        P.op("act", lambda e, a_ps=a_ps, tb=tb, nt=nt: e.activation(
                                            out=yt[:, tb, nt * 512:(nt + 1) * 512], in_=a_ps[:, :], func=AF.Copy),
                                            r=[Ra], w=[Ryt[tb]])
                                    else:
                                        P.op("dve", lambda e, a_ps=a_ps, tb=tb, nt=nt: e.tensor_copy(
                                            out=yt[:, tb, nt * 512:(nt + 1) * 512], in_=a_ps[:, :]), r=[Ra], w=[Ryt[tb]])
                            for tb in range(4):
                                rstd, Rs = norm_stats(yt[:, tb, :].rearrange("p (a b) -> p a b", b=512), Ryt[tb],
                                                      aT[:, 0:4, :], RaT, small)
                                P.op("dve", lambda e, tb=tb, rstd=rstd: e.scalar_tensor_tensor(
                                    out=yt[:, tb, :], in0=yt[:, tb, :], scalar=rstd[:, 0:1], in1=gpost2[:],
                                    op0=ALU.mult, op1=ALU.mult), r=[Ryt[tb], Rs, Rg], w=[Ryt[tb]])
                                P.op("pool" if tb % 2 == 0 else "dve", lambda e, tb=tb: e.tensor_tensor(
                                    out=yt[:, tb, :], in0=yt[:, tb, :], in1=x1[:, tb, :], op=ALU.add),
                                    r=[Ryt[tb], Rx1[tb]], w=[Ryt[tb]])
                                orow = q0 - HALO + tb * 128
                                P.dma("sp", out[orow:orow + 128, :], yt[:, tb, :], dout[tb], r=[Ryt[tb]])
            for ti, (q0, W, is_halo) in enumerate(tiles):
                do_tile(ti, q0, W, is_halo)
        P.emit()
    return nc


_CACHE = {}


def _consts():
    ident = np.eye(128, dtype=np.float32)
    s = np.arange(128)[:, None]
    t = np.arange(128)[None, :]
    masktab = np.zeros((128, 512), np.float32)
    masktab[:, 0:128] = np.where(t >= s, 0.0, NEG)
    slopes = (2.0 ** (-8.0 * np.arange(1, NH + 1) / NH)).astype(np.float32)
    al = np.zeros((128, NH, 256), np.float32)
    for h in range(NH):
        dist_prev = (t + 128 - s).astype(np.float32)
        al[:, h, 0:128] = np.where(s > t, -slopes[h] * dist_prev, NEG)
        dist_cur = (t - s).astype(np.float32)
        al[:, h, 128:256] = np.where(t >= s, -slopes[h] * dist_cur, NEG)
    hi = al.astype(ml_dtypes.bfloat16).astype(np.float32)
    lo = (al - hi).astype(ml_dtypes.bfloat16).astype(np.float32)
    return ident, masktab, hi.reshape(128, NH * 256), lo.reshape(128, NH * 256)


def make_in_maps(inputs, T_CTX, T_OWN, n_cores):
    x = np.asarray(inputs["x"], np.float32)
    B, S, _ = x.shape
    ident, masktab, alhi, allo = _consts()
    f32 = lambda a: np.ascontiguousarray(np.asarray(a, np.float32))
    g_grp = np.concatenate([np.asarray(inputs["grp_swa_g"])[0], np.asarray(inputs["grp_fox_g"])[0]])
    conv_w = np.asarray(inputs["conv_w"], np.float32)[0]
    conv_b = np.asarray(inputs["conv_b"], np.float32)[0]
    common = {
        "w_in": f32(inputs["w_in"][0]), "w_out": f32(inputs["w_out"][0]),
        "w_up": f32(inputs["w_up"][0]), "w_down": f32(inputs["w_down"][0]),
        "g_pre": f32(inputs["pre_mix_g"][0]).reshape(1, D), "g_post": f32(inputs["post_mix_g"][0]).reshape(1, D),
        "g_pre2": f32(inputs["pre_ffn_g"][0]).reshape(1, D), "g_post2": f32(inputs["post_ffn_g"][0]).reshape(1, D),
        "g_grpT": f32(g_grp.reshape(16, 128).T),
        "b_forget": f32(inputs["b_forget"][0]).reshape(16, 1),
        "sinks": f32(inputs["sinks"][0]).reshape(1, 16),
        "cwT": f32(conv_w.reshape(3, 2 * NFC, 128).transpose(2, 1, 0).reshape(128, 2 * NFC * 3)),
        "cbT": f32(conv_b.reshape(2 * NFC, 128).T),
        "ident": ident, "masktab": masktab, "alhi": alhi, "allo": allo,
    }
    nhalf = S // T_OWN
    maps = []
    for c in range(n_cores):
        b, half = c // nhalf, c % nhalf
        m = dict(common)
        own = x[b, half * T_OWN:(half + 1) * T_OWN]
        T_ALL = T_CTX + T_OWN
        ctxrow = np.zeros((1, T_ALL), np.float32)
        if T_CTX:
            if half == 0:
                ctx = x[b, 0:T_CTX]
                ctxrow[0, 0:T_CTX] = NEG
            else:
                ctx = x[b, half * T_OWN - T_CTX: half * T_OWN]
            m["xx"] = np.ascontiguousarray(np.concatenate([ctx, own], axis=0))
        else:
            m["xx"] = np.ascontiguousarray(own)
        m["ctxrow"] = ctxrow
        m["flag"] = np.full((128, 1), 0.0 if half == 0 else 1.0, np.float32)
        maps.append(m)
    return maps


T_CTX_CFG = 2048
T_OWN_CFG = 2048


def kernel(**inputs):
    x = np.asarray(inputs["x"])
    B, S, _ = x.shape
    n_cores = B * (S // T_OWN_CFG)
    key = (T_CTX_CFG, T_OWN_CFG)
    if key not in _CACHE:
        _CACHE[key] = build_program(T_CTX_CFG, T_OWN_CFG)
    nc = _CACHE[key]
    maps = make_in_maps(inputs, T_CTX_CFG, T_OWN_CFG, n_cores)
    res = run_bass_kernel_spmd(nc, maps, core_ids=list(range(n_cores)))
    outs = [np.asarray(r["out"], np.float32) for r in res.results]
    nhalf = S // T_OWN_CFG
    full = np.stack([np.concatenate(outs[b * nhalf:(b + 1) * nhalf], axis=0) for b in range(B)], axis=0)
    return full.astype(np.float32)
```

```python
import numpy as np
import ml_dtypes
from contextlib import ExitStack, contextmanager
import concourse.bass as bass
import concourse.mybir as mybir
from concourse.bass_utils import run_bass_kernel_spmd

F32 = mybir.dt.float32
BF16 = mybir.dt.bfloat16
AF = mybir.ActivationFunctionType
ALU = mybir.AluOpType

D = 2048
DIN = 4368
DFF = 5632
NH = 16
HD = 64
EPS = 1e-6
NEG = -30000.0
NFC = DFF // 128

ENGS = ("pe", "act", "dve", "pool", "sp")
CENGS = ("pe", "act", "dve", "pool")


class Res:
    __slots__ = ("last_w", "readers")

    def __init__(self):
        self.last_w = None
        self.readers = {}


class DSem:
    __slots__ = ("sem", "count", "last", "sw")

    def __init__(self, sem, sw):
        self.sem = sem
        self.count = 0
        self.last = None
        self.sw = sw


class Op:
    __slots__ = ("eng", "fn", "deps", "is_dma", "dsem", "dval", "sigval", "need_sig")


class Ring:
    def __init__(self, items):
        self.items = items
        self.i = 0

    def next(self):
        it = self.items[self.i % len(self.items)]
        self.i += 1
        return it


class Prog:
    def __init__(self, nc, root, same_eng_sync=True):
        self.nc = nc
        self.root = root
        self.stack = root
        self.ops = {e: [] for e in ENGS}
        self.same_eng_sync = same_eng_sync
        self.csem = {e: root.enter_context(nc.semaphore("cs_" + e)) for e in CENGS}
        self.all_dsems = []
        self.free_dsems = {False: [], True: []}
        self.scope_dsems = [[]]
        self.pending_bar = {e: None for e in ENGS}
        self.uid = 0

    def sb(self, shape, dt, name=None):
        self.uid += 1
        return self.stack.enter_context(self.nc.sbuf_tensor(name or ("t%d" % self.uid), list(shape), dt))

    def ps(self, shape, dt, name=None):
        self.uid += 1
        return self.stack.enter_context(self.nc.psum_tensor(name or ("p%d" % self.uid), list(shape), dt))

    def dsem(self, sw=False):
        if self.free_dsems[sw]:
            d = self.free_dsems[sw].pop()
        else:
            d = DSem(self.root.enter_context(self.nc.semaphore("ds%d" % len(self.all_dsems))), sw)
            self.all_dsems.append(d)
        self.scope_dsems[-1].append(d)
        return d

    @contextmanager
    def scope(self):
        old = self.stack
        self.scope_dsems.append([])
        with ExitStack() as st:
            self.stack = st
            yield
            self.barrier()
        self.stack = old
        for d in self.scope_dsems.pop():
            self.free_dsems[d.sw].append(d)

    def barrier(self):
        bar = []
        for e in CENGS:
            for o in reversed(self.ops[e]):
                if not o.is_dma:
                    bar.append(o)
                    break
        for d in self.all_dsems:
            if d.last is not None:
                bar.append(d.last)
        for e in ENGS:
            self.pending_bar[e] = bar

    def _deps(self, op, r, w):
        deps = []
        pb = self.pending_bar[op.eng]
        if pb is not None:
            deps.extend((o, True) for o in pb)
            self.pending_bar[op.eng] = None
        for res in r:
            if res.last_w is not None:
                deps.append((res.last_w, True))
        for res in w:
            if res.last_w is not None:
                deps.append((res.last_w, False))
            deps.extend((o, False) for o in res.readers.values())
        key = id(op.dsem) if op.is_dma else op.eng
        for res in r:
            res.readers[key] = op
        for res in w:
            res.last_w = op
            res.readers = {}
        op.deps = deps

    def op(self, eng, fn, r=(), w=()):
        o = Op()
        o.eng = eng
        o.fn = fn
        o.is_dma = False
        o.dsem = None
        o.dval = 0
        o.sigval = 0
        o.need_sig = False
        self._deps(o, r, w)
        self.ops[eng].append(o)
        return o

    def dma(self, q, out, in_, dsem, r=(), w=()):
        assert dsem.sw == (q == "pool"), "semaphore / DMA queue kind mismatch"
        o = Op()
        o.eng = q
        o.fn = (out, in_)
        o.is_dma = True
        o.dsem = dsem
        dsem.count += 1
        o.dval = dsem.count * 16
        dsem.last = o
        o.sigval = 0
        o.need_sig = False
        self._deps(o, r, w)
        self.ops[q].append(o)
        return o

    def _skip(self, d, o, raw=True):
        return (not d.is_dma) and (not o.is_dma) and d.eng == o.eng and \
            (d.eng == "pe" or (not raw) or not self.same_eng_sync)

    def emit(self):
        nc = self.nc
        self.barrier()
        final = self.pending_bar["sp"]
        for o in final:
            if not o.is_dma:
                o.need_sig = True
        for e in ENGS:
            for o in self.ops[e]:
                for (d, raw) in o.deps:
                    if d.is_dma or self._skip(d, o, raw):
                        continue
                    d.need_sig = True
        for e in CENGS:
            c = 0
            for o in self.ops[e]:
                if (not o.is_dma) and o.need_sig:
                    c += 1
                    o.sigval = c

        def run_stream(e, eh, extra):
            waited = {}

            def do_waits(deps, o):
                need = {}
                for (d, raw) in deps:
                    if d.is_dma:
                        key = id(d.dsem)
                        sem = d.dsem.sem
                        val = d.dval
                    else:
                        if o is not None and self._skip(d, o, raw):
                            continue
                        key = d.eng
                        sem = self.csem[d.eng]
                        val = d.sigval
                    if waited.get(key, 0) >= val:
                        continue
                    if key not in need or need[key][1] < val:
                        need[key] = (sem, val)
                for key, (sem, val) in need.items():
                    eh.wait_ge(sem, val)
                    waited[key] = val

            for o in self.ops[e]:
                do_waits(o.deps, o)
                if o.is_dma:
                    out, in_ = o.fn
                    eh.dma_start(out=out, in_=in_).then_inc(o.dsem.sem, 16)
                else:
                    ins = o.fn(eh)
                    if o.need_sig:
                        ins.then_inc(self.csem[e], 1)
            if extra:
                do_waits([(x, True) for x in extra], None)

        with nc.Block() as block:
            @block.sync
            def _(eh):
                run_stream("sp", eh, final)

            @block.tensor
            def _(eh):
                run_stream("pe", eh, None)

            @block.scalar
            def _(eh):
                run_stream("act", eh, None)

            @block.vector
            def _(eh):
                run_stream("dve", eh, None)

            @block.gpsimd
            def _(eh):
                run_stream("pool", eh, None)


def build_program(T_CTX, T_OWN, debug=False):
    HALO = 128 if T_CTX > 0 else 0
    T_ALL = T_CTX + T_OWN
    NQ = HALO + T_OWN
    NKB = T_ALL // 128
    nc = bass.Bass("TRN2", target_bir_lowering=False)

    def din(name, shape):
        return nc.dram_tensor(name, list(shape), F32, kind="ExternalInput").ap()

    xx = din("xx", [T_ALL, D])
    w_in = din("w_in", [D, DIN])
    w_out = din("w_out", [D, D])
    w_up = din("w_up", [D, 2 * DFF])
    w_down = din("w_down", [DFF, D])
    g_pre = din("g_pre", [1, D])
    g_post = din("g_post", [1, D])
    g_pre2 = din("g_pre2", [1, D])
    g_post2 = din("g_post2", [1, D])
    g_grpT = din("g_grpT", [128, 16])
    b_forget = din("b_forget", [16, 1])
    sinks = din("sinks", [1, 16])
    cwT_d = din("cwT", [128, 2 * NFC * 3])
    cbT_d = din("cbT", [128, 2 * NFC])
    ident_d = din("ident", [128, 128])
    masktab_d = din("masktab", [128, 512])
    alhi_d = din("alhi", [128, 16 * 256])
    allo_d = din("allo", [128, 16 * 256])
    ctxrow_d = din("ctxrow", [1, T_ALL])
    flag_d = din("flag", [128, 1])
    out = nc.dram_tensor("out", [T_OWN, D], F32, kind="ExternalOutput").ap()

    skind = "ExternalOutput" if debug else "Internal"
    qfT = nc.dram_tensor("qfT", [NH, 72, NQ], BF16, kind=skind).ap()
    kfT = nc.dram_tensor("kfT", [NH, 72, T_ALL], BF16, kind=skind).ap()
    vf = nc.dram_tensor("vf", [T_ALL, NH * 128], BF16, kind=skind).ap()
    qaT = nc.dram_tensor("qaT", [NH, 66, NQ], BF16, kind=skind).ap()
    kaT = nc.dram_tensor("kaT", [2, 66, T_ALL], BF16, kind=skind).ap()
    va = nc.dram_tensor("va", [T_ALL, 2 * 128], BF16, kind=skind).ap()
    if debug:
        dbg_oa = nc.dram_tensor("dbg_oa", [NQ // 128 + 4, 128, 16 * 512], F32, kind="ExternalOutput").ap()
        dbg_x1 = nc.dram_tensor("dbg_x1", [NQ, D], F32, kind="ExternalOutput").ap()

    w_in_v = w_in.rearrange("(co ci) n -> ci co n", ci=128)
    w_out_v = w_out.rearrange("(co ci) n -> ci co n", ci=128)
    w_up_v = w_up.rearrange("(co ci) n -> ci co n", ci=128)
    w_down_v = w_down.rearrange("(fo fi) n -> fi fo n", fi=128)

    with ExitStack() as root:
        P = Prog(nc, root)

        identf = P.sb([128, 128], F32)
        identb = P.sb([128, 128], BF16)
        onesf = P.sb([128, 128], F32)
        onesb = P.sb([128, 128], BF16)
        masktab = P.sb([128, 512], BF16)
        ggT = P.sb([128, 16], F32)
        esink = P.sb([128, 16], F32)
        negb = P.sb([16, 1], F32)
        cwT = P.sb([128, 2 * NFC, 3], F32)
        cbT = P.sb([128, 2 * NFC], F32)
        carry = P.sb([128, 2 * NFC, 2], F32)
        flag = P.sb([128, 1], F32)
        Rc = Res()
        Rcarry = Res()
        dc = P.dsem()
        dcs = P.dsem(sw=True)
        P.dma("sp", identf[:], ident_d[:, :], dc, w=[Rc])
        P.dma("pool", identb[:], ident_d[:, :], dcs, w=[Rc])
        P.dma("pool", masktab[:], masktab_d[:, :], dcs, w=[Rc])
        P.dma("sp", ggT[:], g_grpT[:, :], dc, w=[Rc])
        P.dma("sp", esink[:], sinks.broadcast_to([128, 16]), dc, w=[Rc])
        P.dma("sp", negb[:], b_forget[:, :], dc, w=[Rc])
        P.dma("sp", cwT[:], cwT_d.rearrange("p (c k) -> p c k", k=3), dc, w=[Rc])
        P.dma("sp", cbT[:], cbT_d[:, :], dc, w=[Rc])
        P.dma("sp", flag[:], flag_d[:, :], dc, w=[Rc])
        P.op("pool", lambda e: e.memset(onesf[:], 1.0), w=[Rc])
        P.op("pool", lambda e: e.memset(onesb[:], 1.0), w=[Rc])
        P.op("pool", lambda e: e.memset(carry[:], 0.0), w=[Rcarry])
        P.op("act", lambda e: e.activation(out=esink[:], in_=esink[:], func=AF.Exp), r=[Rc], w=[Rc])
        P.op("dve", lambda e: e.tensor_scalar(out=negb[:], in0=negb[:], scalar1=-1.0, scalar2=None, op0=ALU.mult),
             r=[Rc], w=[Rc])
        P.barrier()

        def norm_stats(src_ap, Rsrc, junk, Rjunk, small):
            ss, sd, rstd, Rs = small.next()
            P.op("act", lambda e: e.activation(out=junk[:], in_=src_ap, func=AF.Square, accum_out=ss[:]),
                 r=[Rsrc], w=[Rjunk, Rs])
            P.op("act", lambda e: e.activation(out=sd[:], in_=ss[:], func=AF.Sqrt, scale=1.0 / D, bias=EPS),
                 r=[Rs], w=[Rs])
            P.op("dve", lambda e: e.reciprocal(out=rstd[:], in_=sd[:]), r=[Rs], w=[Rs])
            return rstd, Rs

        def make_small_ring(n):
            items = []
            for _ in range(n):
                items.append((P.sb([128, 1], F32), P.sb([128, 1], F32), P.sb([128, 1], F32), Res()))
            return Ring(items)

        def transpose_block(hb, Rhb, dstT, RdstT, col0, tpring, evi, evac=None):
            for half in range(2):
                tp, Rtp = tpring.next()
                for k in range(8):
                    c = half * 8 + k
                    P.op("pe", lambda e, tp=tp, k=k, c=c: e.transpose(
                        out=tp[:, k, :], in_=hb[:, c * 128:(c + 1) * 128], identity=identb[:]),
                        r=[Rhb, Rc], w=[Rtp])
                eng = evac or ("dve" if (evi + half) % 2 == 0 else "act")
                if eng == "dve":
                    P.op("dve", lambda e, tp=tp, half=half: e.tensor_copy(
                        out=dstT[:, half * 8:(half + 1) * 8, col0:col0 + 128], in_=tp[:]),
                        r=[Rtp], w=[RdstT])
                else:
                    P.op("act", lambda e, tp=tp, half=half: e.activation(
                        out=dstT[:, half * 8:(half + 1) * 8, col0:col0 + 128], in_=tp[:], func=AF.Copy),
                        r=[Rtp], w=[RdstT])

        with P.scope():
            spT = P.sb([16, T_ALL], F32)
            RspT = Res()
            passes = []
            if T_CTX:
                passes.append((0, T_CTX, True))
            passes.append((T_CTX, T_OWN, False))
            TP = max(T_CTX, T_OWN)
            with P.scope():
                hT = P.sb([128, 16, TP], BF16)
                RhT = [Res() for _ in range(TP // 128)]
                gpre = P.sb([128, D], F32)
                Rg = Res()
                dg = P.dsem()
                P.dma("sp", gpre[:], g_pre.broadcast_to([128, D]), dg, w=[Rg])
                junk = P.sb([128, D], BF16)
                Rjunk = Res()
                small = make_small_ring(3)
                xring = Ring([(P.sb([128, D], F32), Res(), P.dsem()) for _ in range(4)])
                hbring = Ring([(P.sb([128, D], BF16), Res()) for _ in range(2)])
                tpring = Ring([(P.ps([128, 8, 128], BF16), Res()) for _ in range(2)])
                mmring = Ring([(P.ps([128, 512], F32), Res()) for _ in range(4)])
                wring = Ring([(P.sb([128, 16, 256], BF16), Res(), P.dsem(sw=True)) for _ in range(3)])
                stgring = Ring([(P.sb([128, 512], BF16), Res(), P.dsem()) for _ in range(4)])
                vstring = Ring([(P.sb([128, 4, 128], BF16), Res(), P.dsem()) for _ in range(3)])
                etring = Ring([(P.sb([16, 512], F32), Res()) for _ in range(2)])
                for (vst, Rv, _d) in vstring.items:
                    P.op("pool", lambda e, vst=vst: e.memset(vst[:], 1.0), w=[Rv])
                evc = [0]

                for (tok0, ntok, is_ctx) in passes:
                    pend = None
                    for tb in range(ntok // 128):
                        xs, Rxs, dxs = xring.next()
                        P.dma("sp", xs[:], xx[tok0 + tb * 128: tok0 + (tb + 1) * 128, :], dxs, w=[Rxs])
                        rstd, Rs = norm_stats(xs[:], Rxs, junk, Rjunk, small)
                        hb, Rhb = hbring.next()
                        P.op("dve", lambda e, hb=hb, xs=xs, rstd=rstd: e.scalar_tensor_tensor(
                            out=hb[:], in0=xs[:], scalar=rstd[:, 0:1], in1=gpre[:], op0=ALU.mult, op1=ALU.mult),
                            r=[Rxs, Rs, Rg], w=[Rhb])
                        if pend is not None:
                            transpose_block(*pend)
                        pend = (hb, Rhb, hT, RhT[tb], tb * 128, tpring, tb)
                    transpose_block(*pend)

                    full_tiles = [(t0, 512) for t0 in range(0, ntok, 512)]
                    halo_tiles = [(ntok - 128, 128)]

                    def qcol(t0):
                        return (t0 - (ntok - 128)) if is_ctx else (HALO + t0)

                    def fm(Wt, RW, col_lo, tiles, scale, dest_fn):
                        for (t0, tw) in tiles:
                            ps, Rps = mmring.next()
                            for c in range(16):
                                P.op("pe", lambda e, ps=ps, c=c, t0=t0, tw=tw: e.matmul(
                                    ps[:, 0:tw], lhsT=Wt[:, c, col_lo:col_lo + 128], rhs=hT[:, c, t0:t0 + tw],
                                    start=(c == 0), stop=(c == 15)),
                                    r=[RW] + RhT[t0 // 128:(t0 + tw) // 128], w=[Rps])
                            stg, Rstg, dstg = stgring.next()
                            evc[0] += 1
                            if evc[0] % 2 == 0:
                                P.op("act", lambda e, ps=ps, stg=stg, tw=tw: e.activation(
                                    out=stg[:, 0:tw], in_=ps[:, 0:tw], func=AF.Copy, scale=scale),
                                    r=[Rps], w=[Rstg])
                            else:
                                P.op("dve", lambda e, ps=ps, stg=stg, tw=tw: e.tensor_scalar(
                                    out=stg[:, 0:tw], in0=ps[:, 0:tw], scalar1=scale, scalar2=None, op0=ALU.mult),
                                    r=[Rps], w=[Rstg])
                            for hh in range(2):
                                P.dma("sp", dest_fn(hh, t0, tw), stg[hh * 64:(hh + 1) * 64, 0:tw], dstg, r=[Rstg])

                    def tm(Wt, RW, col_lo, nheads, dst, dcol0):
                        ncols = nheads * 64
                        for tb in range(ntok // 128):
                            ps, Rps = mmring.next()
                            for c in range(16):
                                P.op("pe", lambda e, ps=ps, c=c, tb=tb: e.matmul(
                                    ps[:, 0:ncols], lhsT=hT[:, c, tb * 128:(tb + 1) * 128],
                                    rhs=Wt[:, c, col_lo:col_lo + ncols], start=(c == 0), stop=(c == 15)),
                                    r=[RW, RhT[tb]], w=[Rps])
                            vst, Rvst, dvst = vstring.next()
                            evc[0] += 1
                            src = ps[:, 0:ncols].rearrange("p (h d) -> p h d", d=64)
                            if evc[0] % 2 == 0:
                                P.op("act", lambda e, vst=vst, src=src: e.activation(
                                    out=vst[:, 0:nheads, 0:64], in_=src, func=AF.Copy), r=[Rps], w=[Rvst])
                            else:
                                P.op("dve", lambda e, vst=vst, src=src: e.tensor_copy(
                                    out=vst[:, 0:nheads, 0:64], in_=src), r=[Rps], w=[Rvst])
                            r0 = tok0 + tb * 128
                            P.dma("sp", dst[r0:r0 + 128, dcol0:dcol0 + nheads * 128].rearrange("p (h d) -> p h d", d=128),
                                  vst[:, 0:nheads, :], dvst, r=[Rvst])

                    for ch in range(18):
                        if is_ctx and False:
                            continue
                        Wt, RW, dW = wring.next()
                        ncol = 256 if ch < 17 else 16
                        P.dma("pool", Wt[:, :, 0:ncol], w_in_v[:, :, ch * 256: ch * 256 + ncol], dW, w=[RW])
                        if ch < 4 or 5 <= ch <= 8:
                            dstT = qaT if ch < 4 else qfT
                            h0 = (ch if ch < 4 else ch - 5) * 4
                            tiles = halo_tiles if is_ctx else full_tiles
                            for sub in range(2):
                                fm(Wt, RW, sub * 128, tiles, 0.125,
                                   lambda hh, t0, tw, h0=h0, sub=sub, dstT=dstT:
                                   dstT[h0 + 2 * sub + hh, 0:64, qcol(t0):qcol(t0) + tw])
                        elif ch == 4:
                            fm(Wt, RW, 0, full_tiles, 1.0,
                               lambda hh, t0, tw: kaT[hh, 0:64, tok0 + t0: tok0 + t0 + tw])
                            tm(Wt, RW, 128, 2, va, 0)
                        elif 9 <= ch <= 12:
                            h0 = (ch - 9) * 4
                            for sub in range(2):
                                fm(Wt, RW, sub * 128, full_tiles, 1.0,
                                   lambda hh, t0, tw, h0=h0, sub=sub:
                                   kfT[h0 + 2 * sub + hh, 0:64, tok0 + t0: tok0 + t0 + tw])
                        elif 13 <= ch <= 16:
                            tm(Wt, RW, 0, 4, vf, (ch - 13) * 4 * 128)
                        else:
                            for (t0, tw) in full_tiles:
                                ps, Rps = mmring.next()
                                for c in range(16):
                                    P.op("pe", lambda e, ps=ps, c=c, t0=t0, tw=tw, Wt=Wt: e.matmul(
                                        ps[0:16, 0:tw], lhsT=Wt[:, c, 0:16], rhs=hT[:, c, t0:t0 + tw],
                                        start=(c == 0), stop=(c == 15)),
                                        r=[RW] + RhT[t0 // 128:(t0 + tw) // 128], w=[Rps])
                                et, Ret = etring.next()
                                P.op("act", lambda e, ps=ps, et=et, tw=tw: e.activation(
                                    out=et[:, 0:tw], in_=ps[0:16, 0:tw], func=AF.Exp, scale=-1.0, bias=negb[:, 0:1]),
                                    r=[Rps, Rc], w=[Ret])
                                P.op("act", lambda e, et=et, t0=t0, tw=tw, tok0=tok0: e.activation(
                                    out=spT[:, tok0 + t0: tok0 + t0 + tw], in_=et[:, 0:tw], func=AF.Ln, bias=1.0),
                                    r=[Ret], w=[RspT])

            with P.scope():
                zeros = P.sb([16, T_ALL], F32)
                cs = P.sb([16, T_ALL], F32)
                r1 = P.sb([16, T_ALL], F32)
                hi = P.sb([16, T_ALL], BF16)
                mid = P.sb([16, T_ALL], BF16)
                lo = P.sb([16, T_ALL], BF16)
                nq3 = P.sb([16, 3, NQ], BF16)
                ones = P.sb([16, T_ALL], BF16)
                row70 = P.sb([16, NQ], BF16)
                ctxb = P.sb([16, T_ALL], BF16)
                Rz = Res()
                dz = P.dsem(sw=True)
                dz2 = P.dsem()
                P.op("pool", lambda e: e.memset(zeros[:], 0.0), w=[Rz])
                P.op("pool", lambda e: e.memset(ones[:], 1.0), w=[Rz])
                P.op("pool", lambda e: e.memset(row70[:], 1.0), w=[Rz])
                if HALO:
                    P.op("pool", lambda e: e.memset(row70[:, 0:HALO], 0.0), w=[Rz])
                P.dma("pool", ctxb[:], ctxrow_d.broadcast_to([16, T_ALL]), dz, w=[Rz])
                V_ = "dve"
                P.op(V_, lambda e: e.tensor_tensor_scan(out=cs[:], data0=spT[:], data1=zeros[:], initial=0.0,
                                                        op0=ALU.add, op1=ALU.add), r=[RspT, Rz], w=[Rz])
                P.op(V_, lambda e: e.tensor_copy(out=hi[:], in_=cs[:]), r=[Rz], w=[Rz])
                P.op(V_, lambda e: e.tensor_tensor(out=r1[:], in0=cs[:], in1=hi[:], op=ALU.subtract), r=[Rz], w=[Rz])
                P.op(V_, lambda e: e.tensor_copy(out=mid[:], in_=r1[:]), r=[Rz], w=[Rz])
                P.op(V_, lambda e: e.tensor_tensor(out=cs[:], in0=r1[:], in1=mid[:], op=ALU.subtract), r=[Rz], w=[Rz])
                P.op(V_, lambda e: e.tensor_copy(out=lo[:], in_=cs[:]), r=[Rz], w=[Rz])
                qs = T_CTX - HALO
                for j, src in enumerate((hi, mid, lo)):
                    P.op(V_, lambda e, j=j, src=src: e.tensor_scalar(
                        out=nq3[:, j, :], in0=src[:, qs:T_ALL], scalar1=-1.0, scalar2=None, op0=ALU.mult),
                        r=[Rz], w=[Rz])
                P.dma("sp", qfT[:, 64:67, :], nq3[:], dz2, r=[Rz])
                for j, src in enumerate((hi, mid, lo)):
                    P.dma("sp", kfT[:, 67 + j, :], src[:], dz2, r=[Rz])
                for j in range(3):
                    P.dma("sp", qfT[:, 67 + j, :], ones[:, 0:NQ], dz2, r=[Rz])
                    P.dma("sp", kfT[:, 64 + j, :], ones[:], dz2, r=[Rz])
                P.dma("sp", qfT[:, 70, :], row70[:], dz2, r=[Rz])
                P.dma("sp", qaT[:, 64, :], row70[:], dz2, r=[Rz])
                P.dma("sp", kfT[:, 70, :], ctxb[:], dz2, r=[Rz])
                P.dma("sp", kaT[:, 64, :], ctxb[0:2, :], dz2, r=[Rz])
                zb = P.sb([16, T_ALL], BF16)
                P.op("pool", lambda e: e.memset(zb[:], 0.0), w=[Rz])
                P.dma("sp", qfT[:, 71, :], zb[:, 0:NQ], dz2, r=[Rz])
                P.dma("sp", kfT[:, 71, :], zb[:], dz2, r=[Rz])
                P.dma("sp", qaT[:, 65, :], zb[:, 0:NQ], dz2, r=[Rz])
                P.dma("sp", kaT[:, 65, :], zb[0:2, :], dz2, r=[Rz])

        tiles = []
        if HALO:
            tiles.append((0, 128, True))
        for i in range(T_OWN // 512):
            tiles.append((HALO + i * 512, 512, False))

        with P.scope():
            h2T_halo = P.sb([128, 16, 128], BF16)
            Rh2h = Res()
            have_halo = [False]

            def do_tile(ti, q0, W, is_halo):
                ts = T_CTX - HALO + q0
                kb0 = ts // 128
                nqb = W // 128
                nkb = kb0 + nqb
                with P.scope():
                    OA = P.sb([128, 16, 512], F32)
                    ROA = Res()

                    with P.scope():
                        sring = Ring([(P.ps([128, 512], F32), Res()) for _ in range(3)])
                        oring = Ring([(P.ps([128, 512], F32), Res()) for _ in range(2)])
                        oring_swa = Ring([(P.ps([128, 512], F32), Res()) for _ in range(1)])
                        sring_swa = Ring([(P.ps([128, 512], F32), Res()) for _ in range(2)])
                        ptring_swa = Ring([(P.sb([128, 512], BF16), Res()) for _ in range(2)])
                        ptring = Ring([(P.sb([128, 512], BF16), Res()) for _ in range(4)])
                        ktring = Ring([(P.sb([72, T_ALL], BF16), Res(), P.dsem()) for _ in range(2)])
                        qtring = Ring([(P.sb([72, 512], BF16), Res(), P.dsem()) for _ in range(3)])
                        vgring = Ring([(P.sb([128, NKB, 512], BF16), (Res(), Res()), (P.dsem(), P.dsem()))
                                       for _ in range(2)])
                        rdring = Ring([(P.sb([64, 512], F32), Res()) for _ in range(2)])
                        alhi = P.sb([128, 16, 256], BF16)
                        allo = P.sb([128, 16, 256], BF16)
                        kbase = max(kb0 - 1, 0)
                        nka = nkb - kbase
                        KaT = P.sb([66, 2, nka * 128], BF16)
                        VA = P.sb([128, nka, 256], BF16)
                        Rtab = Res()
                        dtab = P.dsem()
                        dtabs = P.dsem(sw=True)
                        P.dma("sp", KaT[:], kaT.rearrange("k r t -> r k t")[:, :, kbase * 128: nkb * 128], dtab, w=[Rtab])
                        P.dma("sp", VA[:], va[kbase * 128: nkb * 128, :].rearrange("(kb p) c -> p kb c", p=128),
                              dtab, w=[Rtab])

                        def finalize(h, O, RO, ch, pb, is_swa):
                            rd, Rrd = rdring.next()
                            if is_swa:
                                P.op("dve", lambda e: e.tensor_scalar(
                                    out=rd[:, 0:W], in0=O[64:128, 0:W], scalar1=esink[64:128, h:h + 1],
                                    scalar2=None, op0=ALU.add), r=[RO, Rc], w=[Rrd])
                                P.op("dve", lambda e: e.reciprocal(out=rd[:, 0:W], in_=rd[:, 0:W]),
                                     r=[Rrd], w=[Rrd])
                            else:
                                P.op("dve", lambda e: e.reciprocal(out=rd[:, 0:W], in_=O[64:128, 0:W]),
                                     r=[RO], w=[Rrd])
                            P.op("dve", lambda e: e.tensor_tensor(
                                out=OA[pb:pb + 64, ch, 0:W], in0=O[0:64, 0:W], in1=rd[:, 0:W], op=ALU.mult),
                                r=[RO, Rrd], w=[ROA])

                        def swa_head(h):
                            kv = h // 8
                            qt, Rqt, dqt = qtring.next()
                            P.dma("sp", qt[0:66, 0:W], qaT[h, :, q0:q0 + W], dqt, w=[Rqt])
                            O, RO = oring_swa.next()
                            pvs = []
                            for n0 in range(0, nqb, 2):
                                S, RS = sring_swa.next()
                                PT, RPT = ptring_swa.next()
                                ns = [n for n in (n0, n0 + 1) if n < nqb]
                                lo_c = None
                                for n in ns:
                                    cur = kb0 + n
                                    prev = cur - 1
                                    base = (n % 2) * 256
                                    a0 = 0 if prev >= 0 else 128
                                    if lo_c is None:
                                        lo_c = base + a0
                                    P.op("pe", lambda e, base=base, a0=a0, S=S: e.matmul(
                                        S[:, base + a0:base + 256], lhsT=identb[:], rhs=alhi[:, h, a0:256],
                                        start=True, stop=False), r=[Rc, Rtab], w=[RS])
                                    P.op("pe", lambda e, base=base, a0=a0, S=S: e.matmul(
                                        S[:, base + a0:base + 256], lhsT=identb[:], rhs=allo[:, h, a0:256],
                                        start=False, stop=False), r=[Rc, Rtab], w=[RS])
                                    if prev >= 0:
                                        P.op("pe", lambda e, base=base, prev=prev, n=n, S=S: e.matmul(
                                            S[:, base:base + 128],
                                            lhsT=KaT[0:65, kv, (prev - kbase) * 128:(prev - kbase + 1) * 128],
                                            rhs=qt[0:65, n * 128:(n + 1) * 128], start=False, stop=False),
                                            r=[Rtab, Rqt], w=[RS])
                                    P.op("pe", lambda e, base=base, cur=cur, n=n, S=S: e.matmul(
                                        S[:, base + 128:base + 256],
                                        lhsT=KaT[0:65, kv, (cur - kbase) * 128:(cur - kbase + 1) * 128],
                                        rhs=qt[0:65, n * 128:(n + 1) * 128], start=False, stop=True),
                                        r=[Rtab, Rqt], w=[RS])
                                hi_c = (ns[-1] % 2) * 256 + 256
                                P.op("act", lambda e, lo_c=lo_c, hi_c=hi_c, S=S, PT=PT: e.activation(
                                    out=PT[:, lo_c:hi_c], in_=S[:, lo_c:hi_c], func=AF.Exp), r=[RS], w=[RPT])
                                pvs.append((ns, PT, RPT))

                            def part_b():
                                for (ns, PT, RPT) in pvs:
                                    for n in ns:
                                        cur = kb0 + n
                                        prev = cur - 1
                                        base = (n % 2) * 256
                                        if prev >= 0:
                                            P.op("pe", lambda e, base=base, prev=prev, n=n, PT=PT: e.matmul(
                                                O[:, n * 128:(n + 1) * 128],
                                                lhsT=VA[:, prev - kbase, kv * 128:(kv + 1) * 128],
                                                rhs=PT[:, base:base + 128], start=True, stop=False),
                                                r=[Rtab, RPT], w=[RO])
                                        P.op("pe", lambda e, base=base, cur=cur, n=n, PT=PT, prev=prev: e.matmul(
                                            O[:, n * 128:(n + 1) * 128],
                                            lhsT=VA[:, cur - kbase, kv * 128:(kv + 1) * 128],
                                            rhs=PT[:, base + 128:base + 256], start=(prev < 0), stop=True),
                                            r=[Rtab, RPT], w=[RO])
                                finalize(h, O, RO, h // 2, (h % 2) * 64, True)
                            return part_b
                        units = [(h, j) for h in range(NH) for j in range(nkb)]
                        state = {}
                        headres = {}
                        deferred = []
                        swa_b = {}
                        LOOK = 2

                        vhalf = (nkb + 1) // 2

                        def head_setup(h):
                            kt, Rkt, dkt = ktring.next()
                            P.dma("sp", kt[:, 0:nkb * 128], kfT[h, :, 0:nkb * 128], dkt, w=[Rkt])
                            qt, Rqt, dqt = qtring.next()
                            P.dma("sp", qt[:, 0:W], qfT[h, :, q0:q0 + W], dqt, w=[Rqt])
                            if h % 4 == 0:
                                vg, Rvg, dvg = vgring.next()
                                g = h // 4
                                for k, (a, b) in enumerate(((0, vhalf), (vhalf, nkb))):
                                    if b > a:
                                        P.dma("sp", vg[:, a:b, :],
                                              vf[a * 128:b * 128, g * 512:(g + 1) * 512].rearrange(
                                                  "(kb p) c -> p kb c", p=128), dvg[k], w=[Rvg[k]])
                                headres["vg"] = (vg, Rvg)
                                if h == 0:
                                    P.dma("pool", alhi[:], alhi_d.rearrange("p (h t) -> p h t", t=256), dtabs,
                                          r=[Rvg[0]], w=[Rtab])
                                    P.dma("pool", allo[:], allo_d.rearrange("p (h t) -> p h t", t=256), dtabs,
                                          r=[Rvg[0]], w=[Rtab])
                            O, RO = oring.next()
                            headres[h] = (kt, Rkt, qt, Rqt, O, RO) + headres["vg"]

                        def emit_S(h, j):
                            if j == 0:
                                head_setup(h)
                            kt, Rkt, qt, Rqt, O, RO, vg, Rvg = headres[h]
                            o = j - kb0
                            c0 = max(o, 0) * 128
                            N = W - c0
                            S, RS = sring.next()
                            if o >= 0:
                                P.op("pe", lambda e: e.matmul(S[:, c0:W], lhsT=identb[:], rhs=masktab[:, 0:N],
                                                              start=True, stop=False), r=[Rc], w=[RS])
                                P.op("pe", lambda e: e.matmul(S[:, c0:W], lhsT=kt[0:71, j * 128:(j + 1) * 128],
                                                              rhs=qt[0:71, c0:W], start=False, stop=True),
                                     r=[Rkt, Rqt], w=[RS])
                            else:
                                P.op("pe", lambda e: e.matmul(S[:, 0:W], lhsT=kt[0:71, j * 128:(j + 1) * 128],
                                                              rhs=qt[0:71, 0:W], start=True, stop=True),
                                     r=[Rkt, Rqt], w=[RS])
                            PT, RPT = ptring.next()
                            P.op("act", lambda e: e.activation(out=PT[:, c0:W], in_=S[:, c0:W], func=AF.Exp),
                                 r=[RS], w=[RPT])
                            state[(h, j)] = (PT, RPT, c0)

                        def emit_PV(h, j):
                            kt, Rkt, qt, Rqt, O, RO, vg, Rvg = headres[h]
                            PT, RPT, c0 = state.pop((h, j))
                            hl = h % 4
                            P.op("pe", lambda e: e.matmul(O[:, c0:W], lhsT=vg[:, j, hl * 128:(hl + 1) * 128],
                                                          rhs=PT[:, c0:W], start=(j == 0), stop=(j == nkb - 1)),
                                 r=[Rvg[0 if j < vhalf else 1], RPT], w=[RO])
                            if j == nkb - 1:
                                finalize(h, O, RO, 8 + h // 2, (h % 2) * 64, False)

                        def run_deferred(force=False):
                            for d in list(deferred):
                                d[0] -= 1
                                if d[0] <= 0 or force:
                                    d[1]()
                                    deferred.remove(d)

                        for i in range(len(units) + LOOK):
                            if i < len(units):
                                emit_S(*units[i])
                                if units[i][1] == nkb // 2:
                                    swa_b[units[i][0]] = swa_head(units[i][0])
                                if units[i][1] == min(nkb // 2 + 3, nkb - 1):
                                    swa_b.pop(units[i][0])()
                            if i - LOOK >= 0:
                                emit_PV(*units[i - LOOK])
                            run_deferred()
                        run_deferred(force=True)


                    if debug:
                        dd = P.dsem()
                        P.dma("sp", dbg_oa[ti].rearrange("p (c t) -> p c t", t=512), OA[:], dd, r=[ROA])

                    x1 = P.sb([128, 4, D], F32)
                    Rx1 = [Res() for _ in range(4)]
                    h2T = P.sb([128, 16, 512], BF16)
                    Rh2T = Res()
                    wu0 = None
                    if not is_halo:
                        wu0 = (P.sb([128, 16, 512], BF16), Res(), P.dsem(sw=True))
                    with P.scope():
                      gpost = P.sb([128, D], F32)
                      gpre2 = P.sb([128, D], F32)
                      Rg = Res()
                      dg = P.dsem()
                      P.dma("sp", gpost[:], g_post.broadcast_to([128, D]), dg, w=[Rg])
                      P.dma("sp", gpre2[:], g_pre2.broadcast_to([128, D]), dg, w=[Rg])
                      xring = Ring([(P.sb([128, D], F32), Res(), P.dsem()) for _ in range(2)])
                      with P.scope():
                        sqs = [(P.sb([128, 8, 512], BF16), Res()) for _ in range(2)]
                        onT = P.sb([128, 16, 512], BF16)
                        RonT = Res()
                        rsbs = [(P.sb([128, 512], F32), Res()) for _ in range(2)]
                        ssbs = [(P.ps([128, 512], F32), Res()) for _ in range(2)]
                        for g in range(2):
                            sq, Rsq = sqs[g]
                            P.op("act", lambda e, g=g, sq=sq: e.activation(
                                out=sq[:, :, 0:W], in_=OA[:, g * 8:(g + 1) * 8, 0:W], func=AF.Square),
                                r=[ROA], w=[Rsq])
                        for g in range(2):
                            sq, Rsq = sqs[g]
                            ssb, Rssb = ssbs[g]
                            for k in range(8):
                                P.op("pe", lambda e, k=k, sq=sq, ssb=ssb: e.matmul(
                                    ssb[:, 0:W], lhsT=onesb[:], rhs=sq[:, k, 0:W], start=(k == 0), stop=(k == 7)),
                                    r=[Rsq, Rc], w=[Rssb])
                        for g in range(2):
                            ssb, Rssb = ssbs[g]
                            rsb, Rrsb = rsbs[g]
                            P.op("act", lambda e, ssb=ssb, rsb=rsb: e.activation(
                                out=rsb[:, 0:W], in_=ssb[:, 0:W], func=AF.Sqrt, scale=1.0 / 1024, bias=EPS),
                                r=[Rssb], w=[Rrsb])
                            P.op("dve", lambda e, rsb=rsb: e.reciprocal(out=rsb[:, 0:W], in_=rsb[:, 0:W]),
                                 r=[Rrsb], w=[Rrsb])
                            for k in range(8):
                                c = g * 8 + k
                                P.op("dve", lambda e, c=c, rsb=rsb: e.scalar_tensor_tensor(
                                    out=onT[:, c, 0:W], in0=OA[:, c, 0:W], scalar=ggT[:, c:c + 1], in1=rsb[:, 0:W],
                                    op0=ALU.mult, op1=ALU.mult), r=[ROA, Rrsb, Rc], w=[RonT])
                        woring = Ring([(P.sb([128, 16, 512], BF16), Res(), P.dsem(sw=True)) for _ in range(2)])
                        mmring = Ring([(P.ps([128, 512], F32), Res()) for _ in range(3)])
                        evc = 0
                        for nt in range(4):
                            Wo, RWo, dWo = woring.next()
                            P.dma("pool", Wo[:], w_out_v[:, :, nt * 512:(nt + 1) * 512], dWo, w=[RWo])
                            for tb in range(nqb):
                                ps, Rps = mmring.next()
                                for c in range(16):
                                    P.op("pe", lambda e, ps=ps, c=c, tb=tb, Wo=Wo: e.matmul(
                                        ps[:, :], lhsT=onT[:, c, tb * 128:(tb + 1) * 128], rhs=Wo[:, c, :],
                                        start=(c == 0), stop=(c == 15)), r=[RonT, RWo], w=[Rps])
                                evc += 1
                                if evc % 2 == 0:
                                    P.op("act", lambda e, ps=ps, tb=tb, nt=nt: e.activation(
                                        out=x1[:, tb, nt * 512:(nt + 1) * 512], in_=ps[:, :], func=AF.Copy),
                                        r=[Rps], w=[Rx1[tb]])
                                else:
                                    P.op("dve", lambda e, ps=ps, tb=tb, nt=nt: e.tensor_copy(
                                        out=x1[:, tb, nt * 512:(nt + 1) * 512], in_=ps[:, :]), r=[Rps], w=[Rx1[tb]])
                      if True:
                        if wu0 is not None:
                            P.dma("pool", wu0[0][:, :, 0:256], w_up_v[:, :, 0:256], wu0[2], w=[wu0[1]])
                            P.dma("pool", wu0[0][:, :, 256:512], w_up_v[:, :, DFF:DFF + 256], wu0[2], w=[wu0[1]])
                        junk = P.sb([128, D], BF16)
                        Rjunk = Res()
                        junk2 = P.sb([128, D], BF16)
                        Rjunk2 = Res()
                        small = make_small_ring(8)
                        hbring = Ring([(P.sb([128, D], BF16), Res()) for _ in range(2)])
                        tpring = Ring([(P.ps([128, 8, 128], BF16), Res()) for _ in range(2)])
                        dst_h2T, Rdst = (h2T_halo, Rh2h) if is_halo else (h2T, Rh2T)
                        hbs = {}

                        def stA(tb):
                            xs, Rxs, dxs = xring.next()
                            P.dma("sp", xs[:], xx[ts + tb * 128: ts + (tb + 1) * 128, :], dxs, w=[Rxs])
                            rstd, Rs = norm_stats(x1[:, tb, :], Rx1[tb], junk, Rjunk, small)
                            P.op("dve", lambda e: e.scalar_tensor_tensor(
                                out=x1[:, tb, :], in0=x1[:, tb, :], scalar=rstd[:, 0:1], in1=gpost[:],
                                op0=ALU.mult, op1=ALU.mult), r=[Rx1[tb], Rs, Rg], w=[Rx1[tb]])
                            P.op("pool", lambda e: e.tensor_tensor(
                                out=x1[:, tb, :], in0=x1[:, tb, :], in1=xs[:], op=ALU.add),
                                r=[Rx1[tb], Rxs], w=[Rx1[tb]])
                            if debug:
                                dd = P.dsem()
                                P.dma("sp", dbg_x1[q0 + tb * 128: q0 + (tb + 1) * 128, :], x1[:, tb, :], dd, r=[Rx1[tb]])

                        def stB(tb):
                            rstd2, Rs2 = norm_stats(x1[:, tb, :], Rx1[tb], junk2, Rjunk2, small)
                            hb, Rhb = hbring.next()
                            hbs[tb] = (hb, Rhb)
                            P.op("dve", lambda e: e.scalar_tensor_tensor(
                                out=hb[:], in0=x1[:, tb, :], scalar=rstd2[:, 0:1], in1=gpre2[:],
                                op0=ALU.mult, op1=ALU.mult), r=[Rx1[tb], Rs2, Rg], w=[Rhb])

                        def stC(tb):
                            hb, Rhb = hbs.pop(tb)
                            transpose_block(hb, Rhb, dst_h2T, Rdst, tb * 128, tpring, 0, evac="dve")

                        for step in range(nqb + 3):
                            for fn, lag in ((stA, 0), (stB, 2), (stC, 3)):
                                if 0 <= step - lag < nqb:
                                    fn(step - lag)
                    if is_halo:
                        have_halo[0] = True
                        return

                    with P.scope():
                        aT = P.sb([128, NFC, 512], BF16)
                        RaT = Res()
                        use_halo = have_halo[0]
                        have_halo[0] = False
                        with P.scope():
                            wuring = Ring([wu0] + [(P.sb([128, 16, 512], BF16), Res(), P.dsem(sw=True))
                                                   for _ in range(2)])
                            wuring.next()
                            wu_slots = {0: (wu0[0], wu0[1])}
                            upring = Ring([(P.ps([128, 512], F32), Res()) for _ in range(4)])
                            phring = Ring([(P.ps([128, 2], F32), Res()) for _ in range(2)])
                            uering = Ring([(P.sb([128, 514], F32), Res()) for _ in range(3)])
                            ybring = Ring([(P.sb([128, 512], F32), Res()) for _ in range(4)])
                            glring = Ring([(P.sb([128, 512], F32), Res()) for _ in range(2)])
                            for fc in range(NFC):
                                if fc % 2 == 0:
                                    for g in ([1, 2] if fc == 0 else [fc // 2 + 2]):
                                        if g < NFC // 2:
                                            Wn, RWn, dWn = wuring.next()
                                            wu_slots[g] = (Wn, RWn)
                                            P.dma("pool", Wn[:, :, 0:256], w_up_v[:, :, g * 256:(g + 1) * 256],
                                                  dWn, w=[RWn])
                                            P.dma("pool", Wn[:, :, 256:512],
                                                  w_up_v[:, :, DFF + g * 256: DFF + (g + 1) * 256], dWn, w=[RWn])
                                    Wu, RWu = wu_slots.pop(fc // 2)
                                wo = (fc % 2) * 128
                                ys = []
                                for part in range(2):
                                    idx = part * NFC + fc
                                    ue, Rue = uering.next()
                                    if use_halo:
                                        ph, Rph = phring.next()
                                        for c in range(16):
                                            P.op("pe", lambda e, ph=ph, c=c, Wu=Wu, part=part, wo=wo: e.matmul(
                                                ph[:, :], lhsT=Wu[:, c, part * 256 + wo:part * 256 + wo + 128],
                                                rhs=h2T_halo[:, c, 126:128], start=(c == 0), stop=(c == 15)),
                                                r=[RWu, Rh2h], w=[Rph])
                                        P.op("dve", lambda e, ph=ph, ue=ue: e.tensor_scalar(
                                            out=ue[:, 0:2], in0=ph[:, 0:2], scalar1=flag[:, 0:1], scalar2=None,
                                            op0=ALU.mult), r=[Rph, Rc], w=[Rue])
                                    else:
                                        P.op("act", lambda e, ue=ue, idx=idx: e.activation(
                                            out=ue[:, 0:2], in_=carry[:, idx, :], func=AF.Copy), r=[Rcarry], w=[Rue])
                                    ps, Rps = upring.next()
                                    for c in range(16):
                                        P.op("pe", lambda e, ps=ps, c=c, Wu=Wu, part=part, wo=wo: e.matmul(
                                            ps[:, :], lhsT=Wu[:, c, part * 256 + wo:part * 256 + wo + 128], rhs=h2T[:, c, :],
                                            start=(c == 0), stop=(c == 15)), r=[RWu, Rh2T], w=[Rps])
                                    P.op("act", lambda e, ps=ps, ue=ue: e.activation(
                                        out=ue[:, 2:514], in_=ps[:, :], func=AF.Copy), r=[Rps], w=[Rue])
                                    P.op("act", lambda e, ue=ue, idx=idx: e.activation(
                                        out=carry[:, idx, :], in_=ue[:, 512:514], func=AF.Copy), r=[Rue], w=[Rcarry])
                                    yb, Ryb = ybring.next()
                                    P.op("act", lambda e, ps=ps, yb=yb, idx=idx: e.activation(
                                        out=yb[:], in_=ps[:, :], func=AF.Identity, scale=cwT[:, idx, 2:3],
                                        bias=cbT[:, idx:idx + 1]), r=[Rps, Rc], w=[Ryb])
                                    P.op("dve", lambda e, ue=ue, yb=yb, idx=idx: e.scalar_tensor_tensor(
                                        out=yb[:], in0=ue[:, 1:513], scalar=cwT[:, idx, 1:2], in1=yb[:],
                                        op0=ALU.mult, op1=ALU.add), r=[Rue, Rc, Ryb], w=[Ryb])
                                    P.op("dve", lambda e, ue=ue, yb=yb, idx=idx: e.scalar_tensor_tensor(
                                        out=yb[:], in0=ue[:, 0:512], scalar=cwT[:, idx, 0:1], in1=yb[:],
                                        op0=ALU.mult, op1=ALU.add), r=[Rue, Rc, Ryb], w=[Ryb])
                                    ys.append((yb, Ryb))
                                gl, Rgl = glring.next()
                                (yg, Ryg), (yv, Ryv) = ys
                                P.op("act", lambda e, gl=gl, yg=yg: e.activation(
                                    out=gl[:], in_=yg[:], func=AF.Gelu_apprx_tanh), r=[Ryg], w=[Rgl])
                                P.op("dve", lambda e, gl=gl, yv=yv, fc=fc: e.tensor_tensor(
                                    out=aT[:, fc, :], in0=gl[:], in1=yv[:], op=ALU.mult), r=[Rgl, Ryv], w=[RaT])

                        with P.scope():
                            yt = P.sb([128, 4, D], F32)
                            Ryt = [Res() for _ in range(4)]
                            wdring = Ring([(P.sb([128, 4, 512], BF16), Res(), P.dsem(sw=True)) for _ in range(4)])
                            acc = [(P.ps([128, 512], F32), Res()) for _ in range(4)]
                            gpost2 = P.sb([128, D], F32)
                            Rg = Res()
                            dg = P.dsem()
                            P.dma("sp", gpost2[:], g_post2.broadcast_to([128, D]), dg, w=[Rg])
                            small = make_small_ring(4)
                            dout = [P.dsem() for _ in range(4)]
                            evc = 0
                            for nt in range(4):
                                for piece in range(11):
                                    Wd, RWd, dWd = wdring.next()
                                    P.dma("pool", Wd[:], w_down_v[:, piece * 4:(piece + 1) * 4,
                                                                   nt * 512:(nt + 1) * 512], dWd, w=[RWd])
                                    for tb in range(4):
                                        a_ps, Ra = acc[tb]
                                        for k in range(4):
                                            fc = piece * 4 + k
                                            P.op("pe", lambda e, a_ps=a_ps, fc=fc, tb=tb, Wd=Wd, k=k: e.matmul(
                                                a_ps[:, :], lhsT=aT[:, fc, tb * 128:(tb + 1) * 128], rhs=Wd[:, k, :],
                                                start=(fc == 0), stop=(fc == NFC - 1)), r=[RaT, RWd], w=[Ra])
                                for tb in range(4):
                                    a_ps, Ra = acc[tb]
                                    evc += 1
                                    if evc % 2 == 0:
                                        P.op("act", lambda e, a_ps=a_ps, tb=tb, nt=nt: e.activation(
                                            out=yt[:, tb, nt * 512:(nt + 1) * 512], in_=a_ps[:, :], func=AF.Copy),
                                            r=[Ra], w=[Ryt[tb]])
                                    else:
                                        P.op("dve", lambda e, a_ps=a_ps, tb=tb, nt=nt: e.tensor_copy(
                                            out=yt[:, tb, nt * 512:(nt + 1) * 512], in_=a_ps[:, :]), r=[Ra], w=[Ryt[tb]])
                            for tb in range(4):
                                rstd, Rs = norm_stats(yt[:, tb, :].rearrange("p (a b) -> p a b", b=512), Ryt[tb],
                                                      aT[:, 0:4, :], RaT, small)
                                P.op("dve", lambda e, tb=tb, rstd=rstd: e.scalar_tensor_tensor(
                                    out=yt[:, tb, :], in0=yt[:, tb, :], scalar=rstd[:, 0:1], in1=gpost2[:],
                                    op0=ALU.mult, op1=ALU.mult), r=[Ryt[tb], Rs, Rg], w=[Ryt[tb]])
                                P.op("pool" if tb % 2 == 0 else "dve", lambda e, tb=tb: e.tensor_tensor(
                                    out=yt[:, tb, :], in0=yt[:, tb, :], in1=x1[:, tb, :], op=ALU.add),
                                    r=[Ryt[tb], Rx1[tb]], w=[Ryt[tb]])
                                orow = q0 - HALO + tb * 128
                                P.dma("sp", out[orow:orow + 128, :], yt[:, tb, :], dout[tb], r=[Ryt[tb]])
            for ti, (q0, W, is_halo) in enumerate(tiles):
                do_tile(ti, q0, W, is_halo)
        P.emit()
    return nc


_CACHE = {}


def _consts():
    ident = np.eye(128, dtype=np.float32)
    s = np.arange(128)[:, None]
    t = np.arange(128)[None, :]
    masktab = np.zeros((128, 512), np.float32)
    masktab[:, 0:128] = np.where(t >= s, 0.0, NEG)
    slopes = (2.0 ** (-8.0 * np.arange(1, NH + 1) / NH)).astype(np.float32)
    al = np.zeros((128, NH, 256), np.float32)
    for h in range(NH):
        dist_prev = (t + 128 - s).astype(np.float32)
        al[:, h, 0:128] = np.where(s > t, -slopes[h] * dist_prev, NEG)
        dist_cur = (t - s).astype(np.float32)
        al[:, h, 128:256] = np.where(t >= s, -slopes[h] * dist_cur, NEG)
    hi = al.astype(ml_dtypes.bfloat16).astype(np.float32)
    lo = (al - hi).astype(ml_dtypes.bfloat16).astype(np.float32)
    return ident, masktab, hi.reshape(128, NH * 256), lo.reshape(128, NH * 256)


def make_in_maps(inputs, T_CTX, T_OWN, n_cores):
    x = np.asarray(inputs["x"], np.float32)
    B, S, _ = x.shape
    ident, masktab, alhi, allo = _consts()
    f32 = lambda a: np.ascontiguousarray(np.asarray(a, np.float32))
    g_grp = np.concatenate([np.asarray(inputs["grp_swa_g"])[0], np.asarray(inputs["grp_fox_g"])[0]])
    conv_w = np.asarray(inputs["conv_w"], np.float32)[0]
    conv_b = np.asarray(inputs["conv_b"], np.float32)[0]
    common = {
        "w_in": f32(inputs["w_in"][0]), "w_out": f32(inputs["w_out"][0]),
        "w_up": f32(inputs["w_up"][0]), "w_down": f32(inputs["w_down"][0]),
        "g_pre": f32(inputs["pre_mix_g"][0]).reshape(1, D), "g_post": f32(inputs["post_mix_g"][0]).reshape(1, D),
        "g_pre2": f32(inputs["pre_ffn_g"][0]).reshape(1, D), "g_post2": f32(inputs["post_ffn_g"][0]).reshape(1, D),
        "g_grpT": f32(g_grp.reshape(16, 128).T),
        "b_forget": f32(inputs["b_forget"][0]).reshape(16, 1),
        "sinks": f32(inputs["sinks"][0]).reshape(1, 16),
        "cwT": f32(conv_w.reshape(3, 2 * NFC, 128).transpose(2, 1, 0).reshape(128, 2 * NFC * 3)),
        "cbT": f32(conv_b.reshape(2 * NFC, 128).T),
        "ident": ident, "masktab": masktab, "alhi": alhi, "allo": allo,
    }
    nhalf = S // T_OWN
    maps = []
    for c in range(n_cores):
        b, half = c // nhalf, c % nhalf
        m = dict(common)
        own = x[b, half * T_OWN:(half + 1) * T_OWN]
        T_ALL = T_CTX + T_OWN
        ctxrow = np.zeros((1, T_ALL), np.float32)
        if T_CTX:
            if half == 0:
                ctx = x[b, 0:T_CTX]
                ctxrow[0, 0:T_CTX] = NEG
            else:
                ctx = x[b, half * T_OWN - T_CTX: half * T_OWN]
            m["xx"] = np.ascontiguousarray(np.concatenate([ctx, own], axis=0))
        else:
            m["xx"] = np.ascontiguousarray(own)
        m["ctxrow"] = ctxrow
        m["flag"] = np.full((128, 1), 0.0 if half == 0 else 1.0, np.float32)
        maps.append(m)
    return maps


T_CTX_CFG = 2048
T_OWN_CFG = 2048


def kernel(**inputs):
    x = np.asarray(inputs["x"])
    B, S, _ = x.shape
    n_cores = B * (S // T_OWN_CFG)
    key = (T_CTX_CFG, T_OWN_CFG)
    if key not in _CACHE:
        _CACHE[key] = build_program(T_CTX_CFG, T_OWN_CFG)
    nc = _CACHE[key]
    maps = make_in_maps(inputs, T_CTX_CFG, T_OWN_CFG, n_cores)
    res = run_bass_kernel_spmd(nc, maps, core_ids=list(range(n_cores)))
    outs = [np.asarray(r["out"], np.float32) for r in res.results]
    nhalf = S // T_OWN_CFG
    full = np.stack([np.concatenate(outs[b * nhalf:(b + 1) * nhalf], axis=0) for b in range(B)], axis=0)
    return full.astype(np.float32)
```
